# Optimizing a Trainium2 kernel written in Bass

```python
import math
import jax, jax.numpy as jnp
from jax import lax
import numpy as np

D_MODEL = 1024
BATCH = 8
SEQ = 8192
DEPTH = 4
DEC_BATCH = 32
DEC_SEQ = 16
PAST_LEN = 4096

CHUNK = 64
Q_BLOCK = 128
CONV_W = 3
D_CONV = D_MODEL // 2
N_HEADS = 8
QK_NOPE = 64
QK_ROPE = 32
V_HEAD = 64
KV_RANK = 128
Q_RANK = 256
D_MIX = D_CONV + N_HEADS * V_HEAD
D_IN = 3 * D_CONV + Q_RANK + KV_RANK + QK_ROPE
D_FF = 2816
ROPE_BASE = 10000.0
EPS = 1e-6
ATTN_SCALE = 1.0 / math.sqrt(QK_NOPE + QK_ROPE)
IN_SPLITS = (D_CONV, 2 * D_CONV, 3 * D_CONV, 3 * D_CONV + Q_RANK, 3 * D_CONV + Q_RANK + KV_RANK)

kernel_name = "hybrid_shortconv_mla_convffn_stream_step"


def rmsnorm(x, g):
    xf = x.astype(jnp.float32)
    y = xf * lax.rsqrt(jnp.mean(xf * xf, axis=-1, keepdims=True) + EPS)
    return (y * g.astype(jnp.float32)).astype(x.dtype)


def causal_dwconv(u, prev, w):
    T = u.shape[1]
    up = jnp.concatenate([prev.astype(u.dtype), u], axis=1)
    y = up[:, 0:T] * w[0]
    for k in range(1, CONV_W):
        y = y + up[:, k:k + T] * w[k]
    return y, up[:, T:]


def rope_tables(pos):
    inv = 1.0 / (ROPE_BASE ** (jnp.arange(0, QK_ROPE, 2, dtype=jnp.float32) / QK_ROPE))
    ang = pos.astype(jnp.float32)[:, None] * inv[None, :]
    return jnp.cos(ang), jnp.sin(ang)


def apply_rope(x, cos, sin):
    xf = x.astype(jnp.float32)
    x1, x2 = xf[..., :QK_ROPE // 2], xf[..., QK_ROPE // 2:]
    return jnp.concatenate([x1 * cos - x2 * sin, x2 * cos + x1 * sin], axis=-1).astype(x.dtype)


def mla_attend(q_lat, q_pe, c_kv, k_pe, q_pos, k_pos):
    s = (jnp.einsum('bthr,bsr->bhts', q_lat, c_kv).astype(jnp.float32)
         + jnp.einsum('bthp,bsp->bhts', q_pe, k_pe).astype(jnp.float32)) * ATTN_SCALE
    mask = (k_pos[None, :] // CHUNK) <= (q_pos[:, None] // CHUNK)
    s = jnp.where(mask[None, None], s, -jnp.inf)
    p = jax.nn.softmax(s, axis=-1).astype(c_kv.dtype)
    return jnp.einsum('bhts,bsr->bthr', p, c_kv)


def mla_mix(q_lat, q_pe, c_kv, k_pe, q_pos, k_pos):
    B, T, H, R = q_lat.shape
    if T > Q_BLOCK and T % Q_BLOCK == 0:
        nb = T // Q_BLOCK
        ql = q_lat.reshape(B, nb, Q_BLOCK, H, R).swapaxes(0, 1)
        qp = q_pe.reshape(B, nb, Q_BLOCK, H, QK_ROPE).swapaxes(0, 1)
        qpos = q_pos.reshape(nb, Q_BLOCK)
        o = lax.map(lambda a: mla_attend(a[0], a[1], c_kv, k_pe, a[2], k_pos), (ql, qp, qpos))
        return o.swapaxes(0, 1).reshape(B, T, H, R)
    return mla_attend(q_lat, q_pe, c_kv, k_pe, q_pos, k_pos)


def trunk_layer(x, q_pos, conv_prev, ffn_prev, ckv_past, kpe_past,
                w_in, w_conv, g_qa, w_uq, g_kva, w_uk, w_uv, w_o, g_mix_pre, g_mix_post,
                w_up, w_ffn_conv, b_ffn_conv, w_down, g_ffn_pre, g_ffn_post):
    B, T, _ = x.shape
    h = rmsnorm(x, g_mix_pre)
    z = h @ w_in
    xv, gb, gc, qa, kva, kpe = jnp.split(z, IN_SPLITS, axis=-1)
    conv_out, conv_new = causal_dwconv(gc * xv, conv_prev, w_conv)
    y_conv = gb * conv_out
    q = (rmsnorm(qa, g_qa) @ w_uq).reshape(B, T, N_HEADS, QK_NOPE + QK_ROPE)
    q_nope, q_pe = q[..., :QK_NOPE], q[..., QK_NOPE:]
    cos, sin = rope_tables(q_pos)
    q_pe = apply_rope(q_pe, cos[None, :, None], sin[None, :, None])
    ckv = rmsnorm(kva, g_kva)
    kpe = apply_rope(kpe, cos[None], sin[None])
    c_all = jnp.concatenate([ckv_past.astype(ckv.dtype), ckv], axis=1)
    k_all = jnp.concatenate([kpe_past.astype(kpe.dtype), kpe], axis=1)
    k_pos = jnp.arange(c_all.shape[1])
    q_lat = jnp.einsum('bthn,rhn->bthr', q_nope, w_uk)
    o_lat = mla_mix(q_lat, q_pe, c_all, k_all, q_pos, k_pos)
    y_mla = jnp.einsum('bthr,rhv->bthv', o_lat, w_uv).reshape(B, T, N_HEADS * V_HEAD)
    mix = jnp.concatenate([y_conv, y_mla], axis=-1) @ w_o
    x = x + rmsnorm(mix, g_mix_post)
    h = rmsnorm(x, g_ffn_pre)
    u, ffn_new = causal_dwconv(h @ w_up, ffn_prev, w_ffn_conv)
    u = u + b_ffn_conv
    a, b = u[..., :D_FF], u[..., D_FF:]
    x = x + rmsnorm((jax.nn.silu(a) * b) @ w_down, g_ffn_post)
    return x, conv_new, ffn_new, ckv, kpe


def setup_inputs(seed: int = 0) -> dict:
    key = jax.random.key(seed)
    ks = jax.random.split(key, 32)
    f32 = jnp.float32

    def w(k, shape, fan_in):
        return jax.random.normal(k, shape, f32) * (fan_in ** -0.5)

    def gain(k, n):
        return 1.0 + 0.05 * jax.random.normal(k, (DEPTH, n), f32)

    return {
        "x_prompt": jax.random.normal(ks[0], (BATCH, SEQ, D_MODEL), f32),
        "x_sample": jax.random.normal(ks[1], (DEC_BATCH, DEC_SEQ, D_MODEL), f32),
        "cache_ckv": jax.random.normal(ks[2], (DEPTH, DEC_BATCH, PAST_LEN, KV_RANK), f32),
        "cache_kpe": jax.random.normal(ks[3], (DEPTH, DEC_BATCH, PAST_LEN, QK_ROPE), f32),
        "state_conv": jax.random.normal(ks[4], (DEPTH, DEC_BATCH, CONV_W - 1, D_CONV), f32),
        "state_ffn": jax.random.normal(ks[5], (DEPTH, DEC_BATCH, CONV_W - 1, 2 * D_FF), f32),
        "w_in": w(ks[6], (DEPTH, D_MODEL, D_IN), D_MODEL),
        "w_conv": w(ks[7], (DEPTH, CONV_W, D_CONV), CONV_W),
        "g_qa": gain(ks[8], Q_RANK),
        "w_uq": w(ks[9], (DEPTH, Q_RANK, N_HEADS * (QK_NOPE + QK_ROPE)), Q_RANK),
        "g_kva": gain(ks[10], KV_RANK),
        "w_uk": w(ks[11], (DEPTH, KV_RANK, N_HEADS, QK_NOPE), KV_RANK),
        "w_uv": w(ks[12], (DEPTH, KV_RANK, N_HEADS, V_HEAD), KV_RANK),
        "w_o": w(ks[13], (DEPTH, D_MIX, D_MODEL), D_MIX),
        "g_mix_pre": gain(ks[14], D_MODEL),
        "g_mix_post": gain(ks[15], D_MODEL),
        "w_up": w(ks[16], (DEPTH, D_MODEL, 2 * D_FF), D_MODEL),
        "w_ffn_conv": w(ks[17], (DEPTH, CONV_W, 2 * D_FF), CONV_W),
        "b_ffn_conv": 0.01 * jax.random.normal(ks[18], (DEPTH, 2 * D_FF), f32),
        "w_down": w(ks[19], (DEPTH, D_FF, D_MODEL), D_FF),
        "g_ffn_pre": gain(ks[20], D_MODEL),
        "g_ffn_post": gain(ks[21], D_MODEL),
    }


def reference(x_prompt, x_sample, cache_ckv, cache_kpe, state_conv, state_ffn,
              w_in, w_conv, g_qa, w_uq, g_kva, w_uk, w_uv, w_o, g_mix_pre, g_mix_post,
              w_up, w_ffn_conv, b_ffn_conv, w_down, g_ffn_pre, g_ffn_post):
    Bp, Tp, _ = x_prompt.shape
    Bs, Ts, _ = x_sample.shape
    past_len = cache_ckv.shape[2]
    p_pos = jnp.arange(Tp)
    s_pos = past_len + jnp.arange(Ts)
    dt = x_prompt.dtype
    p_conv0 = jnp.zeros((Bp, CONV_W - 1, D_CONV), dt)
    p_ffn0 = jnp.zeros((Bp, CONV_W - 1, 2 * D_FF), dt)
    p_ckv0 = jnp.zeros((Bp, 0, KV_RANK), dt)
    p_kpe0 = jnp.zeros((Bp, 0, QK_ROPE), dt)

    xp, xs = x_prompt, x_sample
    pc, pk, pcv, pf = [], [], [], []
    sc, sk, scv, sf = [], [], [], []
    for l in range(DEPTH):
        wl = (w_in[l], w_conv[l], g_qa[l], w_uq[l], g_kva[l], w_uk[l], w_uv[l], w_o[l],
              g_mix_pre[l], g_mix_post[l], w_up[l], w_ffn_conv[l], b_ffn_conv[l], w_down[l],
              g_ffn_pre[l], g_ffn_post[l])
        xp, conv_p, ffn_p, ckv_p, kpe_p = trunk_layer(xp, p_pos, p_conv0, p_ffn0, p_ckv0, p_kpe0, *wl)
        xs, conv_s, ffn_s, ckv_s, kpe_s = trunk_layer(xs, s_pos, state_conv[l], state_ffn[l],
                                                      cache_ckv[l], cache_kpe[l], *wl)
        pc.append(ckv_p); pk.append(kpe_p); pcv.append(conv_p); pf.append(ffn_p)
        sc.append(ckv_s); sk.append(kpe_s); scv.append(conv_s); sf.append(ffn_s)

    p_ckv = jnp.stack(pc); p_kpe = jnp.stack(pk); p_conv = jnp.stack(pcv); p_ffn = jnp.stack(pf)
    s_ckv = jnp.stack(sc); s_kpe = jnp.stack(sk); s_conv = jnp.stack(scv); s_ffn = jnp.stack(sf)
    return (xp, xs, p_ckv, p_kpe, p_conv, p_ffn, s_ckv, s_kpe, s_conv, s_ffn)
```

```python
import math
from contextlib import ExitStack
import numpy as np
import concourse.bass as bass
import concourse.mybir as mybir
from concourse.bass_utils import run_bass_kernel_spmd

F32 = mybir.dt.float32
BF16 = mybir.dt.bfloat16
AF = mybir.ActivationFunctionType
ALU = mybir.AluOpType

D = 1024
DC = 512
NH = 8
DFF = 2816
NPAIR = 22
EPS = 1e-6
SCALE = 1.0 / math.sqrt(96.0)
VL = 223
V_GMPRE, V_GMPOST, V_GFPRE, V_GFPOST, V_GQA, V_GKVA, V_WCONV, V_WFFN, V_BFFN = 0, 8, 16, 24, 32, 34, 35, 47, 179
SEM_LIMIT = 30000
NSLOT = 10
N_CORES = 8
DEBUG = False


class Res:
    __slots__ = ("lw", "rd")

    def __init__(self):
        self.lw = None
        self.rd = {}


class Q:
    def __init__(self, kb, kind):
        self.kb = kb
        self.kind = kind
        self.prog = []
        self.seen = {}
        self.own = set()
        self.sem = None
        self.val = 0
        self.slots = [[None, 0] for _ in range(NSLOT)]
        self.nd = 0

    def wait(self, ev):
        if ev is None:
            return
        s, v = ev
        if self.kind == "pe" and s in self.own:
            return
        if self.seen.get(s, 0) >= v:
            return
        self.seen[s] = v
        self.prog.append(lambda e, s=s, v=v: e.wait_ge(s, v))

    def bump(self, fn):
        if self.sem is None or self.val >= SEM_LIMIT:
            self.sem = self.kb.new_sem()
            self.own.add(self.sem)
            self.val = 0
        self.val += 1
        s = self.sem
        self.prog.append(lambda e, fn=fn, s=s: fn(e).then_inc(s, 1))
        return (s, self.val)


class KB:
    def __init__(self, nc, es):
        self.nc = nc
        self.es = es
        self.nsem = 0
        self.pe = Q(self, "pe")
        self.act = Q(self, "cmp")
        self.dve = Q(self, "cmp")
        self.pool = Q(self, "cmp")
        self.sp = Q(self, "dma")
        self.gq = self.pool

    def new_sem(self):
        self.nsem += 1
        return self.es.enter_context(self.nc.semaphore(f"s{self.nsem}"))

    def deps(self, q, reads, writes):
        for r in reads:
            q.wait(r.lw)
        for w in writes:
            q.wait(w.lw)
            for s, v in w.rd.items():
                q.wait((s, v))

    def done(self, ev, reads, writes):
        s, v = ev
        for r in reads:
            if r.rd.get(s, 0) < v:
                r.rd[s] = v
        for w in writes:
            w.lw = ev
            w.rd = {}

    def op(self, q, fn, reads=(), writes=()):
        self.deps(q, reads, writes)
        ev = q.bump(fn)
        self.done(ev, reads, writes)

    def mm(self, fns, reads=(), writes=()):
        q = self.pe
        self.deps(q, reads, writes)
        for f in fns[:-1]:
            q.prog.append(lambda e, f=f: f(e))
        ev = q.bump(fns[-1])
        self.done(ev, reads, writes)

    def dma(self, q, out, in_, reads=(), writes=()):
        self.deps(q, reads, writes)
        k = q.nd % NSLOT
        q.nd += 1
        slot = q.slots[k]
        if slot[0] is None or slot[1] >= SEM_LIMIT:
            slot[0] = self.new_sem()
            slot[1] = 0
        else:
            q.wait((slot[0], slot[1]))
        slot[1] += 16
        s = slot[0]
        q.prog.append(lambda e, out=out, in_=in_, s=s: e.dma_start(out=out, in_=in_).then_inc(s, 16))
        self.done((s, slot[1]), reads, writes)

    def finish(self):
        for q in (self.sp, self.pool):
            for s, v in q.slots:
                if s is not None:
                    q.wait((s, v))


def build(L, SEQ, PAST, NS, TS):
    NT = SEQ // 512
    NSX = NS * TS
    NPB = PAST // 128
    assert PAST % 1024 == 0 and SEQ % 512 == 0
    nc = bass.Bass("TRN2", target_bir_lowering=False)
    es = ExitStack()
    kb = KB(nc, es)
    PE, ACT, DVE, POOL, SP = kb.pe, kb.act, kb.dve, kb.pool, kb.sp

    def din(name, shape):
        return nc.dram_tensor(name, list(shape), F32, kind="ExternalInput").ap()

    def dout(name, shape):
        return nc.dram_tensor(name, list(shape), F32, kind="ExternalOutput").ap()

    def dscr(name, shape):
        return nc.dram_tensor(name, list(shape), BF16, kind="Internal").ap()

    def sb(name, shape, dt=F32):
        return es.enter_context(nc.sbuf_tensor("sb_" + name, list(shape), dt))

    xT = din("xT", [D, SEQ])
    xsT = din("xsT", [D, NSX])
    cT_in = din("cT", [L, NS, 128, PAST])
    kT_in = din("kT", [L, NS, 32, PAST])
    cc_in = din("cc", [L, NS, PAST, 128])
    cst_in = din("cst", [128, L * 4 * NS * 2])
    fst_in = din("fst", [128, L * NPAIR * 2 * NS * 2])
    w_in = din("w_in", [L, D, 2048])
    w_uq = din("w_uq", [L, 256, 1024])
    w_uk = din("w_uk", [L, 128, 512])
    w_ukT = din("w_ukT", [L, 64, 1024])
    w_uv = din("w_uv", [L, 128, 512])
    w_o = din("w_o", [L, D, D])
    w_up = din("w_up", [L, D, 2 * DFF])
    w_dn = din("w_dn", [L, DFF, D])
    vecs_in = din("vecs", [128, L * VL])
    cosP = din("cosP", [32, SEQ])
    sinP = din("sinP", [32, SEQ])
    cosS = din("cosS", [32, NSX])
    sinS = din("sinS", [32, NSX])
    ident_in = din("ident", [128, 128])

    yT = dout("yT", [D, SEQ])
    ysT = dout("ysT", [D, NSX])
    pckvT = dout("pckvT", [L, 128, SEQ])
    pkpeT = dout("pkpeT", [L, 32, SEQ])
    pconv = dout("pconv", [128, L * 4 * 2])
    pffn = dout("pffn", [128, L * NPAIR * 2 * 2])
    sckvT = dout("sckvT", [L, 128, NSX])
    skpeT = dout("skpeT", [L, 32, NSX])
    sconv = dout("sconv", [128, L * 4 * NS * 2])
    sffn = dout("sffn", [128, L * NPAIR * 2 * NS * 2])
    dbg = dout("dbg", [128, 8, NSX]) if DEBUG else None

    win_b = dscr("win_b", [L, 4, 128, 8, 512])
    wo_b = dscr("wo_b", [L, 2, 128, 8, 512])
    wup_b = dscr("wup_b", [L, 11, 128, 8, 512])
    wdn_b = dscr("wdn_b", [L, 8, 128, NPAIR, 128])
    wuq_b = dscr("wuq_b", [L, 128, 2, 1024])
    wuk_b = dscr("wuk_b", [L, 128, 512])
    wukT_b = dscr("wukT_b", [L, 64, 1024])
    wuv_b = dscr("wuv_b", [L, 128, 512])
    Ksc = dscr("Ksc", [L, 2, NT, 96, 4, 512])
    Vsc = dscr("Vsc", [L, 2, NT, 128, 4, 4, 128])
    win_r = [[Res() for _ in range(4)] for _ in range(L)]
    wo_r = [[Res() for _ in range(2)] for _ in range(L)]
    wup_r = [[Res() for _ in range(11)] for _ in range(L)]
    wdn_r = [[Res() for _ in range(8)] for _ in range(L)]
    wsm_r = [[Res() for _ in range(4)] for _ in range(L)]
    ksc_r = [[[Res() for _ in range(NT)] for _ in range(2)] for _ in range(L)]
    vsc_r = [[[Res() for _ in range(NT)] for _ in range(2)] for _ in range(L)]

    x = sb("x", [128, 8, 512]); xr = [Res() for _ in range(8)]
    hb = sb("hb", [128, 8, 512], BF16); hr = [Res() for _ in range(8)]
    mixb = sb("mixb", [128, 8, 512], BF16); mr = [Res() for _ in range(8)]
    arF = sb("arF", [128, 8, 512]); yr = [Res() for _ in range(8)]
    arB = sb("arB", [128, NPAIR, 512], BF16); ar = [Res() for _ in range(NPAIR)]
    sq = sb("sq", [128, 2, 512], BF16); sqr = [Res(), Res()]
    srt = sb("srt", [128, 512]); srt_r = Res()
    rstd = sb("rstd", [128, 512]); rstd_r = Res()
    ubc = sb("ubc", [128, 2, 516]); ubc_r = [Res(), Res()]
    qn = sb("qn", [128, 2, 512], BF16); qn_r = [Res(), Res()]
    ckvb = sb("ckvb", [128, 512], BF16); ckvb_r = Res()
    rt1 = sb("rt1", [128, 2, 512]); rt1_r = [Res(), Res()]
    rt2 = sb("rt2", [128, 2, 512]); rt2_r = [Res(), Res()]
    kpef = sb("kpef", [128, 512]); kpef_r = Res()
    Qb = sb("Qb", [128, 8, 512], BF16); q_r = [Res() for _ in range(8)]
    Kcur = sb("Kcur", [128, 8, 512], BF16); k_r = [Res() for _ in range(8)]
    Vcur = sb("Vcur", [128, 2, 4, 4, 128], BF16); v_r = [Res() for _ in range(4)]
    rec = sb("rec", [128, 2, 512]); rec_r = [Res(), Res()]
    wa = sb("wa", [128, 3, 8, 512], BF16); wa_r = [Res(), Res(), Res()]
    wb = sb("wb", [128, 3, NPAIR, 128], BF16); wb_r = [Res(), Res(), Res()]
    wuq_sb = sb("wuq_sb", [128, 2, 1024], BF16); wuq_r = Res()
    wuk_sb = sb("wuk_sb", [128, 512], BF16); wuk_r = Res()
    wukT_sb = sb("wukT_sb", [128, 1024], BF16); wukT_r = Res()
    wuv_sb = sb("wuv_sb", [128, 512], BF16); wuv_r = Res()
    ub = sb("ub", [128, 2, 2, 516]); ub_r = [Res(), Res()]
    facc = sb("facc", [128, 2, 2, 512]); facc_r = [[Res(), Res()], [Res(), Res()]]
    cstP = sb("cstP", [128, L, 4, 1, 2]); cstP_r = [[Res() for _ in range(4)] for _ in range(L)]
    fstP = sb("fstP", [128, L, NPAIR, 2, 1, 2]); fstP_r = [[Res() for _ in range(NPAIR)] for _ in range(L)]
    cstS = sb("cstS", [128, L, 4, NS, 2]); cstS_r = [[Res() for _ in range(4)] for _ in range(L)]
    fstS = sb("fstS", [128, L, NPAIR, 2, NS, 2]); fstS_r = [[Res() for _ in range(NPAIR)] for _ in range(L)]
    ones_b = sb("ones_b", [128, 128], BF16); ones_r = Res()
    ident = sb("ident", [128, 128]); ident_r = Res()
    vecs = sb("vecs", [128, L * VL]); vecs_r = Res()
    cosF = sb("cosF", [128, 512]); sinF = sb("sinF", [128, 512]); cs_r = Res()
    qlat = sb("qlat", [128, NS * 8 * TS], BF16); qlat_r = Res()
    qpe = sb("qpe", [128, NS * 8 * TS], BF16); qpe_r = Res()
    kpeb = sb("kpeb", [128, NSX], BF16); kpeb_r = Res()
    onb = sb("onb", [128, 2, 8 * TS], BF16); onb_r = [Res(), Res()]
    cntok = sb("cntok", [128, NS, 128], BF16); cntok_r = Res()
    recs = sb("recs", [128, 8 * TS]); recs_r = Res()
    pts = sb("pts", [128, 2, 8 * TS], BF16); pts_r = [Res(), Res()]

    banks = [es.enter_context(nc.psum_tensor(f"bank{i}", [128, 512], F32)) for i in range(8)]
    bank_r = [Res() for _ in range(8)]
    STATB = 7
    rot = {"bank": 0, "wa": 0, "wb": 0, "sq": 0}

    def next_bank():
        i = rot["bank"]
        rot["bank"] = (i + 1) % 7
        return banks[i], bank_r[i]

    def vcol(l, off):
        c = l * VL + off
        return vecs[:, c:c + 1]

    kb.op(POOL, lambda e: e.memset(ones_b[:, :], 1.0), writes=[ones_r])
    kb.op(POOL, lambda e: e.memset(Vcur[:, :, :, :, :].rearrange("p a b c d -> p (a b c) d")[:, :, 64:128], 1.0), writes=v_r)
    kb.op(POOL, lambda e: e.memset(cstP[:, :, :, :, :].rearrange("p l c s t -> p (l c s t)"), 0.0), writes=[r for rr in cstP_r for r in rr])
    kb.op(POOL, lambda e: e.memset(fstP[:, :, :, :, :, :].rearrange("p l c a s t -> p (l c a s t)"), 0.0), writes=[r for rr in fstP_r for r in rr])
    kb.dma(SP, vecs[:, :], vecs_in[:, :], writes=[vecs_r])
    kb.dma(SP, ident[:, :], ident_in[:, :], writes=[ident_r])
    kb.dma(SP, cstS[:, :, :, :, :], cst_in.rearrange("p (l c s t) -> p l c s t", l=L, c=4, s=NS),
           writes=[r for rr in cstS_r for r in rr])
    kb.dma(SP, fstS[:, :, :, :, :, :], fst_in.rearrange("p (l c a s t) -> p l c a s t", l=L, c=NPAIR, a=2, s=NS),
           writes=[r for rr in fstS_r for r in rr])
    GQ = kb.gq

    def emit_casts(l):
        for g in range(4):
            kb.dma(GQ, win_b[l, g], w_in[l].rearrange("(kc p) n -> p kc n", p=128)[:, :, g * 512:(g + 1) * 512],
                   writes=[win_r[l][g]])
        kb.dma(GQ, wuq_b[l], w_uq[l].rearrange("(kc p) n -> p kc n", p=128), writes=[wsm_r[l][0]])
        kb.dma(GQ, wuk_b[l], w_uk[l], writes=[wsm_r[l][1]])
        kb.dma(GQ, wukT_b[l], w_ukT[l], writes=[wsm_r[l][2]])
        kb.dma(GQ, wuv_b[l], w_uv[l], writes=[wsm_r[l][3]])
        for g in range(2):
            kb.dma(GQ, wo_b[l, g], w_o[l].rearrange("(kc p) n -> p kc n", p=128)[:, :, g * 512:(g + 1) * 512],
                   writes=[wo_r[l][g]])
        for g in range(11):
            kb.dma(GQ, wup_b[l, g], w_up[l].rearrange("(kc p) n -> p kc n", p=128)[:, :, g * 512:(g + 1) * 512],
                   writes=[wup_r[l][g]])
        for m in range(8):
            kb.dma(GQ, wdn_b[l, m], w_dn[l].rearrange("(kc p) n -> p kc n", p=128)[:, :, m * 128:(m + 1) * 128],
                   writes=[wdn_r[l][m]])

    emit_casts(0)

    class Ctx:
        pass

    eps_t = sb("eps_t", [128, 1]); eps_r = Res()
    eps_ap = eps_t[:, 0:1]
    kb.op(POOL, lambda e: e.memset(eps_t[:, :], EPS), writes=[eps_r])

    def sq_next():
        i = rot["sq"]
        rot["sq"] = 1 - i
        return i

    def rms_pre(cx, l, goff):
        N = cx.N
        bank, br = banks[STATB], bank_r[STATB]
        for c in range(8):
            i = sq_next()
            if c % 4 in (0, 1):
                kb.op(ACT, lambda e, c=c, i=i: e.activation(out=sq[:, i, 0:N], in_=x[:, c, 0:N], func=AF.Square),
                      reads=[xr[c]], writes=[sqr[i]])
            else:
                kb.op(DVE if c % 4 == 2 else POOL, lambda e, c=c, i=i: e.tensor_tensor(out=sq[:, i, 0:N], in0=x[:, c, 0:N], in1=x[:, c, 0:N],
                                                                                      op=ALU.mult), reads=[xr[c]], writes=[sqr[i]])
            kb.mm([lambda e, c=c, i=i: e.matmul(bank[:, 0:N], ones_b[:, :], sq[:, i, 0:N], start=(c == 0), stop=(c == 7))],
                  reads=[sqr[i], ones_r], writes=[br])
        for c in (5, 6, 7):
            kb.op(ACT, lambda e, c=c: e.activation(out=arF[:, c, 0:N], in_=x[:, c, 0:N], func=AF.Copy, scale=vcol(l, goff + c)),
                  reads=[xr[c], vecs_r], writes=[yr[c]])
        kb.op(ACT, lambda e: e.activation(out=srt[:, 0:N], in_=bank[:, 0:N], func=AF.Ln, scale=1.0 / D, bias=eps_ap),
              reads=[br, eps_r], writes=[srt_r])
        kb.op(ACT, lambda e: e.activation(out=rstd[:, 0:N], in_=srt[:, 0:N], func=AF.Exp, scale=-0.5), reads=[srt_r], writes=[rstd_r])
        for c in range(8):
            if c < 5:
                kb.op(DVE, lambda e, c=c: e.scalar_tensor_tensor(out=hb[:, c, 0:N], in0=x[:, c, 0:N], scalar=vcol(l, goff + c),
                                                                 in1=rstd[:, 0:N], op0=ALU.mult, op1=ALU.mult),
                      reads=[xr[c], rstd_r, vecs_r], writes=[hr[c]])
            else:
                kb.op(POOL, lambda e, c=c: e.tensor_tensor(out=hb[:, c, 0:N], in0=arF[:, c, 0:N], in1=rstd[:, 0:N], op=ALU.mult),
                      reads=[yr[c], rstd_r], writes=[hr[c]])

    pend_stats = []

    def post_consumer(cx, m, bank, br):
        N = cx.N
        while pend_stats:
            pend_stats.pop(0)()
        i = sq_next()
        kb.op(ACT, lambda e: e.activation(out=arF[:, m, 0:N], in_=bank[:, 0:N], func=AF.Copy), reads=[br], writes=[yr[m]])
        kb.op(ACT, lambda e: e.activation(out=sq[:, i, 0:N], in_=bank[:, 0:N], func=AF.Square), reads=[br], writes=[sqr[i]])
        sbk, sbr = banks[STATB], bank_r[STATB]
        pend_stats.append(lambda: kb.mm([lambda e: e.matmul(sbk[:, 0:N], ones_b[:, :], sq[:, i, 0:N], start=(m == 0), stop=(m == 7))],
                                        reads=[sqr[i], ones_r], writes=[sbr]))

    def post_finish(cx, l, goff):
        N = cx.N
        sbk, sbr = banks[STATB], bank_r[STATB]
        while pend_stats:
            pend_stats.pop(0)()
        kb.op(ACT, lambda e: e.activation(out=srt[:, 0:N], in_=sbk[:, 0:N], func=AF.Ln, scale=1.0 / D, bias=eps_ap),
              reads=[sbr, eps_r], writes=[srt_r])
        kb.op(ACT, lambda e: e.activation(out=rstd[:, 0:N], in_=srt[:, 0:N], func=AF.Exp, scale=-0.5), reads=[srt_r], writes=[rstd_r])
        for c in range(8):
            q = DVE if c % 2 == 0 else POOL
            kb.op(q, lambda e, c=c: e.tensor_tensor(out=arF[:, c, 0:N], in0=arF[:, c, 0:N], in1=rstd[:, 0:N], op=ALU.mult),
                  reads=[yr[c], rstd_r], writes=[yr[c]])
            kb.op(DVE, lambda e, c=c: e.scalar_tensor_tensor(out=x[:, c, 0:N], in0=arF[:, c, 0:N], scalar=vcol(l, goff + c),
                                                           in1=x[:, c, 0:N], op0=ALU.mult, op1=ALU.add),
                  reads=[yr[c], xr[c], vecs_r], writes=[xr[c]])

    def wa_load(src_ap, src_res):
        s = rot["wa"]
        rot["wa"] = (s + 1) % 3
        kb.dma(SP, wa[:, s], src_ap, reads=[src_res], writes=[wa_r[s]])
        return s

    def mm_k8(bank, br, N, s, col0, M, rhs, rhs_res):
        kb.mm([lambda e, kc=kc: e.matmul(bank[0:M, 0:N], wa[:, s, kc, col0:col0 + M], rhs[:, kc, 0:N],
                                         start=(kc == 0), stop=(kc == 7)) for kc in range(8)],
              reads=[wa_r[s]] + list(rhs_res), writes=[br])

    def v3(ap2, cx, width):
        return ap2.rearrange("p (s t) -> p s t", s=cx.S)

    def mixer(cx, l):
        N, S, T = cx.N, cx.S, cx.T
        W2 = T + 2
        rms_pre(cx, l, V_GMPRE)
        s = wa_load(win_b[l, 0], win_r[l][0])
        for c in range(4):
            bank, br = next_bank()
            mm_k8(bank, br, N, s, c * 128, 128, hb, hr)
            kb.op(ACT, lambda e, c=c, bank=bank: e.activation(out=arF[:, c, 0:N], in_=bank[:, 0:N], func=AF.Copy),
                  reads=[br], writes=[yr[c]])
        s = wa_load(win_b[l, 1], win_r[l][1])
        for c in range(4):
            bank, br = next_bank()
            mm_k8(bank, br, N, s, c * 128, 128, hb, hr)
            r = c % 2
            uv = v3(ubc[:, r, 0:S * W2], cx, W2)
            kb.op(DVE, lambda e, c=c, bank=bank, uv=uv: e.tensor_tensor(out=uv[:, :, 2:W2], in0=v3(bank[:, 0:N], cx, T),
                                                                        in1=v3(arF[:, c, 0:N], cx, T), op=ALU.mult),
                  reads=[br, yr[c]], writes=[ubc_r[r]])
            kb.op(POOL, lambda e, c=c, uv=uv: e.tensor_copy(out=uv[:, :, 0:2], in_=cx.cst[:, l, c, :, :]),
                  reads=[cx.cst_r[l][c]], writes=[ubc_r[r]])
            kb.op(POOL, lambda e, c=c, uv=uv: e.tensor_copy(out=cx.cst[:, l, c, :, :], in_=uv[:, :, T:W2]),
                  reads=[ubc_r[r]], writes=[cx.cst_r[l][c]])
            acc = v3(arF[:, c, 0:N], cx, T)
            kb.op(DVE, lambda e, c=c, uv=uv, acc=acc: e.tensor_scalar(out=acc, in0=uv[:, :, 0:T], scalar1=vcol(l, V_WCONV + c),
                                                                      scalar2=0.0, op0=ALU.mult, op1=ALU.add),
                  reads=[ubc_r[r], vecs_r], writes=[yr[c]])
            kb.op(DVE, lambda e, c=c, uv=uv, acc=acc: e.scalar_tensor_tensor(out=acc, in0=uv[:, :, 1:T + 1],
                                                                              scalar=vcol(l, V_WCONV + 4 + c), in1=acc,
                                                                              op0=ALU.mult, op1=ALU.add),
                  reads=[ubc_r[r], vecs_r, yr[c]], writes=[yr[c]])
            kb.op(DVE, lambda e, c=c, uv=uv, acc=acc: e.scalar_tensor_tensor(out=acc, in0=uv[:, :, 2:W2],
                                                                             scalar=vcol(l, V_WCONV + 8 + c), in1=acc,
                                                                             op0=ALU.mult, op1=ALU.add),
                  reads=[ubc_r[r], vecs_r, yr[c]], writes=[yr[c]])
        s = wa_load(win_b[l, 2], win_r[l][2])
        for c in range(4):
            bank, br = next_bank()
            mm_k8(bank, br, N, s, c * 128, 128, hb, hr)
            kb.op(DVE, lambda e, c=c, bank=bank: e.tensor_tensor(out=mixb[:, c, 0:N], in0=bank[:, 0:N], in1=arF[:, c, 0:N],
                                                                 op=ALU.mult), reads=[br, yr[c]], writes=[mr[c]])
        s = wa_load(win_b[l, 3], win_r[l][3])
        sbk, sbr = banks[STATB], bank_r[STATB]
        for c in range(2):
            bank, br = next_bank()
            mm_k8(bank, br, N, s, c * 128, 128, hb, hr)
            i = sq_next()
            kb.op(ACT, lambda e, c=c, bank=bank: e.activation(out=arF[:, 4 + c, 0:N], in_=bank[:, 0:N], func=AF.Copy),
                  reads=[br], writes=[yr[4 + c]])
            kb.op(ACT, lambda e, i=i, bank=bank: e.activation(out=sq[:, i, 0:N], in_=bank[:, 0:N], func=AF.Square),
                  reads=[br], writes=[sqr[i]])
            kb.mm([lambda e, c=c, i=i: e.matmul(sbk[:, 0:N], ones_b[:, :], sq[:, i, 0:N], start=(c == 0), stop=(c == 1))],
                  reads=[sqr[i], ones_r], writes=[sbr])
        kb.op(ACT, lambda e: e.activation(out=srt[:, 0:N], in_=sbk[:, 0:N], func=AF.Ln, scale=1.0 / 256, bias=eps_ap),
              reads=[sbr, eps_r], writes=[srt_r])
        kb.op(ACT, lambda e: e.activation(out=rstd[:, 0:N], in_=srt[:, 0:N], func=AF.Exp, scale=-0.5), reads=[srt_r], writes=[rstd_r])
        for c in range(2):
            kb.op(DVE, lambda e, c=c: e.scalar_tensor_tensor(out=qn[:, c, 0:N], in0=arF[:, 4 + c, 0:N], scalar=vcol(l, V_GQA + c),
                                                             in1=rstd[:, 0:N], op0=ALU.mult, op1=ALU.mult),
                  reads=[yr[4 + c], rstd_r, vecs_r], writes=[qn_r[c]])
        bank, br = next_bank()
        mm_k8(bank, br, N, s, 256, 128, hb, hr)
        i = sq_next()
        kb.op(ACT, lambda e, bank=bank: e.activation(out=arF[:, 6, 0:N], in_=bank[:, 0:N], func=AF.Copy), reads=[br], writes=[yr[6]])
        kb.op(ACT, lambda e, bank=bank, i=i: e.activation(out=sq[:, i, 0:N], in_=bank[:, 0:N], func=AF.Square),
              reads=[br], writes=[sqr[i]])
        kb.mm([lambda e, i=i: e.matmul(sbk[:, 0:N], ones_b[:, :], sq[:, i, 0:N], start=True, stop=True)],
              reads=[sqr[i], ones_r], writes=[sbr])
        kb.op(ACT, lambda e: e.activation(out=srt[:, 0:N], in_=sbk[:, 0:N], func=AF.Ln, scale=1.0 / 128, bias=eps_ap),
              reads=[sbr, eps_r], writes=[srt_r])
        kb.op(ACT, lambda e: e.activation(out=rstd[:, 0:N], in_=srt[:, 0:N], func=AF.Exp, scale=-0.5), reads=[srt_r], writes=[rstd_r])
        kb.op(DVE, lambda e: e.scalar_tensor_tensor(out=arF[:, 7, 0:N], in0=arF[:, 6, 0:N], scalar=vcol(l, V_GKVA),
                                                    in1=rstd[:, 0:N], op0=ALU.mult, op1=ALU.mult),
              reads=[yr[6], rstd_r, vecs_r], writes=[yr[7]])
        kb.op(POOL, lambda e: e.tensor_copy(out=ckvb[:, 0:N], in_=arF[:, 7, 0:N]), reads=[yr[7]], writes=[ckvb_r])
        kb.dma(SP, cx.ckv_out(l), arF[:, 7, 0:N], reads=[yr[7]])
        bA, bAr = next_bank()
        mm_k8(bA, bAr, N, s, 320, 96, hb, hr)
        bB, bBr = next_bank()
        mm_k8(bB, bBr, N, s, 352, 96, hb, hr)
        kb.op(DVE, lambda e: e.tensor_tensor(out=rt1[64:96, 0, 0:N], in0=bA[64:96, 0:N], in1=cosF[64:96, 0:N], op=ALU.mult),
              reads=[bAr, cs_r], writes=[rt1_r[0]])
        kb.op(DVE, lambda e: e.tensor_tensor(out=rt2[64:96, 0, 0:N], in0=bB[64:96, 0:N], in1=sinF[64:96, 0:N], op=ALU.mult),
              reads=[bBr, cs_r], writes=[rt2_r[0]])
        kb.op(POOL, lambda e: e.tensor_tensor(out=kpef[64:96, 0:N], in0=rt1[64:96, 0, 0:N], in1=rt2[64:96, 0, 0:N], op=ALU.add),
              reads=[rt1_r[0], rt2_r[0]], writes=[kpef_r])
        kb.dma(SP, cx.kpe_out(l), kpef[64:96, 0:N], reads=[kpef_r])
        if cx.prompt:
            for h in range(8):
                q = ACT if h % 2 == 0 else POOL
                if q is ACT:
                    kb.op(q, lambda e, h=h: e.activation(out=Kcur[64:96, h, 0:N], in_=kpef[64:96, 0:N], func=AF.Copy),
                          reads=[kpef_r], writes=[k_r[h]])
                else:
                    kb.op(q, lambda e, h=h: e.tensor_copy(out=Kcur[64:96, h, 0:N], in_=kpef[64:96, 0:N]),
                          reads=[kpef_r], writes=[k_r[h]])
        else:
            kb.op(POOL, lambda e: e.tensor_copy(out=kpeb[64:96, 0:N], in_=kpef[64:96, 0:N]), reads=[kpef_r], writes=[kpeb_r])
        kb.dma(SP, wuq_sb[:, :, :], wuq_b[l], reads=[wsm_r[l][0]], writes=[wuq_r])
        for h in range(8):
            b1, b1r = next_bank()
            kb.mm([lambda e, kc=kc, h=h, b1=b1: e.matmul(b1[0:96, 0:N], wuq_sb[:, kc, h * 128:h * 128 + 96], qn[:, kc, 0:N],
                                                          start=(kc == 0), stop=(kc == 1)) for kc in range(2)],
                  reads=[wuq_r, qn_r[0], qn_r[1]], writes=[b1r])
            b2, b2r = next_bank()
            kb.mm([lambda e, kc=kc, h=h, b2=b2: e.matmul(b2[0:96, 0:N], wuq_sb[:, kc, h * 128 + 32:h * 128 + 128], qn[:, kc, 0:N],
                                                          start=(kc == 0), stop=(kc == 1)) for kc in range(2)],
                  reads=[wuq_r, qn_r[0], qn_r[1]], writes=[b2r])
            r = h % 2
            kb.op(ACT, lambda e, h=h, b1=b1: e.activation(out=Qb[0:64, h, 0:N], in_=b1[0:64, 0:N], func=AF.Copy),
                  reads=[b1r], writes=[q_r[h]])
            kb.op(DVE, lambda e, b1=b1, r=r: e.tensor_tensor(out=rt1[64:96, r, 0:N], in0=b1[64:96, 0:N], in1=cosF[64:96, 0:N],
                                                             op=ALU.mult), reads=[b1r, cs_r], writes=[rt1_r[r]])
            kb.op(DVE, lambda e, b2=b2, r=r: e.tensor_tensor(out=rt2[64:96, r, 0:N], in0=b2[64:96, 0:N], in1=sinF[64:96, 0:N],
                                                             op=ALU.mult), reads=[b2r, cs_r], writes=[rt2_r[r]])
            if cx.prompt:
                kb.op(POOL, lambda e, h=h, r=r: e.tensor_tensor(out=Qb[64:96, h, 0:N], in0=rt1[64:96, r, 0:N],
                                                                in1=rt2[64:96, r, 0:N], op=ALU.add),
                      reads=[rt1_r[r], rt2_r[r]], writes=[q_r[h]])
            else:
                qv = qpe[64:96, :].rearrange("p (s h t) -> p s h t", s=NS, h=8)[:, :, h, :]
                kb.op(POOL, lambda e, qv=qv, r=r: e.tensor_tensor(out=qv, in0=v3(rt1[64:96, r, 0:N], cx, T),
                                                                  in1=v3(rt2[64:96, r, 0:N], cx, T), op=ALU.add),
                      reads=[rt1_r[r], rt2_r[r]], writes=[qpe_r])
        if cx.prompt:
            attn_prompt(cx, l)
        else:
            attn_sample(cx, l)
        if DEBUG and (not cx.prompt) and l == 0:
            kb.dma(GQ, dbg[:, :, :], mixb[:, :, 0:N], reads=mr)
        for g in range(2):
            s = wa_load(wo_b[l, g], wo_r[l][g])
            for mi in range(4):
                bank, br = next_bank()
                mm_k8(bank, br, N, s, mi * 128, 128, mixb, mr)
                post_consumer(cx, g * 4 + mi, bank, br)
        post_finish(cx, l, V_GMPOST)

    def attn_prompt(cx, l):
        j = cx.j
        kb.dma(SP, wuk_sb[:, :], wuk_b[l], reads=[wsm_r[l][1]], writes=[wuk_r])
        kb.dma(SP, wuv_sb[:, :], wuv_b[l], reads=[wsm_r[l][3]], writes=[wuv_r])
        for h in range(8):
            bank, br = next_bank()
            kb.mm([lambda e, h=h, bank=bank: e.matmul(bank[0:64, 0:512], wuk_sb[:, h * 64:(h + 1) * 64], ckvb[:, 0:512],
                                                      start=True, stop=True)], reads=[wuk_r, ckvb_r], writes=[br])
            if h % 2 == 0:
                kb.op(ACT, lambda e, h=h, bank=bank: e.activation(out=Kcur[0:64, h, :], in_=bank[0:64, 0:512], func=AF.Copy),
                      reads=[br], writes=[k_r[h]])
            else:
                kb.op(DVE, lambda e, h=h, bank=bank: e.tensor_copy(out=Kcur[0:64, h, :], in_=bank[0:64, 0:512]),
                      reads=[br], writes=[k_r[h]])
        for kbi in range(4):
            bank, br = next_bank()
            kb.mm([lambda e, kbi=kbi, bank=bank: e.matmul(bank[:, 0:512], ckvb[:, kbi * 128:(kbi + 1) * 128], wuv_sb[:, :],
                                                          start=True, stop=True)], reads=[wuv_r, ckvb_r], writes=[br])
            src = bank[:, 0:512].rearrange("p (a h v) -> p a h v", a=2, h=4)
            kb.op(DVE, lambda e, kbi=kbi, src=src: e.tensor_copy(out=Vcur[:, 0, kbi, :, 0:64], in_=src[:, 0]),
                  reads=[br], writes=[v_r[kbi]])
            kb.op(ACT, lambda e, kbi=kbi, src=src: e.activation(out=Vcur[:, 1, kbi, :, 0:64], in_=src[:, 1], func=AF.Copy),
                  reads=[br], writes=[v_r[kbi]])
        if j < NT - 1:
            for hp in range(2):
                kb.dma(SP, Ksc[l, hp, j], Kcur[0:96, hp * 4:(hp + 1) * 4, :], reads=k_r[hp * 4:(hp + 1) * 4],
                       writes=[ksc_r[l][hp][j]])
                kb.dma(SP, Vsc[l, hp, j], Vcur[:, hp], reads=v_r, writes=[vsc_r[l][hp][j]])
        slot_i = [0]
        for hp in range(2):
            steps = []
            chunk_dma = []
            for jj in range(j):
                si = slot_i[0] % 2
                slot_i[0] += 1
                ks, vs = 2 * si, 2 * si + 1
                ksl = arB[0:96, 4 * ks:4 * ks + 4, :]
                vsl = arB[:, 4 * vs:4 * vs + 4, :].rearrange("p a (h v) -> p a h v", h=4)
                kres = ar[4 * ks:4 * ks + 4]
                vres = ar[4 * vs:4 * vs + 4]
                chunk_dma.append((ksl, vsl, kres, vres))
                for kbi in range(4):
                    for hh in range(4):
                        steps.append((arB[0:96, 4 * ks + hh, kbi * 128:(kbi + 1) * 128], kres,
                                      vsl[:, kbi, hh, :], vres, 0, False, hh))
            for kbi in range(4):
                for hh in range(4):
                    h = hp * 4 + hh
                    steps.append((Kcur[0:96, h, kbi * 128:(kbi + 1) * 128], [k_r[h]],
                                  Vcur[:, hp, kbi, hh, :], [v_r[kbi]], kbi * 128, True, hh))
            nsteps = len(steps)
            first = [True] * 4
            lastidx = [max(i for i in range(nsteps) if steps[i][6] == hh) for hh in range(4)]
            pend = []

            def do_pv(idx, pi, c0):
                kap, kres, vap, vres, _, _, hh = steps[idx]
                st = first[hh]
                first[hh] = False
                kb.mm([lambda e: e.matmul(banks[hh][:, c0:512], vap, arB[:, 16 + pi, c0:512], start=st, stop=(idx == lastidx[hh]))],
                      reads=[ar[16 + pi]] + list(vres), writes=[bank_r[hh]])

            def emit_dma(jj, hp=hp, chunk_dma=chunk_dma):
                ksl, vsl, kres, vres = chunk_dma[jj]
                kb.dma(SP, ksl, Ksc[l, hp, jj], reads=[ksc_r[l][hp][jj]], writes=kres)
                kb.dma(SP, vsl, Vsc[l, hp, jj], reads=[vsc_r[l][hp][jj]], writes=vres)

            for idx in range(nsteps):
                if idx == 0:
                    for jj in range(min(2, j)):
                        emit_dma(jj)
                elif idx % 16 == 3 and 2 <= idx // 16 + 1 < j:
                    emit_dma(idx // 16 + 1)
                kap, kres, vap, vres, c0, diag, hh = steps[idx]
                h = hp * 4 + hh
                sbi = 4 + idx % 3
                pi = idx % 4
                kb.mm([lambda e, kap=kap, h=h, sbi=sbi, c0=c0: e.matmul(banks[sbi][:, c0:512], kap, Qb[0:96, h, c0:512],
                                                                         start=True, stop=True)],
                      reads=list(kres) + [q_r[h]], writes=[bank_r[sbi]])
                kb.op(ACT, lambda e, sbi=sbi, pi=pi, c0=c0: e.activation(out=arB[:, 16 + pi, c0:512], in_=banks[sbi][:, c0:512],
                                                                       func=AF.Exp, scale=SCALE),
                      reads=[bank_r[sbi]], writes=[ar[16 + pi]])
                if diag:
                    kb.op(POOL, lambda e, pi=pi, c0=c0: e.memset(arB[64:128, 16 + pi, c0:c0 + 64], 0.0), writes=[ar[16 + pi]])
                pend.append((idx, pi, c0))
                if len(pend) > 2:
                    do_pv(*pend.pop(0))
            while pend:
                do_pv(*pend.pop(0))
            for hh in range(4):
                h = hp * 4 + hh
                r = hh % 2
                kb.op(ACT, lambda e, hh=hh, r=r: e.activation(out=rec[0:64, r, :], in_=banks[hh][64:128, 0:512], func=AF.Ln),
                      reads=[bank_r[hh]], writes=[rec_r[r]])
                kb.op(ACT, lambda e, r=r: e.activation(out=rec[0:64, r, :], in_=rec[0:64, r, :], func=AF.Exp, scale=-1.0),
                      reads=[rec_r[r]], writes=[rec_r[r]])
                p0 = (h % 2) * 64
                kb.op(DVE, lambda e, hh=hh, r=r, h=h, p0=p0: e.tensor_tensor(out=mixb[p0:p0 + 64, 4 + h // 2, :],
                                                                              in0=banks[hh][0:64, 0:512], in1=rec[0:64, r, :],
                                                                              op=ALU.mult),
                      reads=[bank_r[hh], rec_r[r]], writes=[mr[4 + h // 2]])

    def attn_sample(cx, l):
        N = cx.N
        kb.dma(SP, wukT_sb[0:64, :], wukT_b[l], reads=[wsm_r[l][2]], writes=[wukT_r])
        kb.dma(SP, wuv_sb[:, :], wuv_b[l], reads=[wsm_r[l][3]], writes=[wuv_r])
        for h in range(8):
            bank, br = next_bank()
            kb.mm([lambda e, h=h, bank=bank: e.matmul(bank[:, 0:N], wukT_sb[0:64, h * 128:(h + 1) * 128], Qb[0:64, h, 0:N],
                                                      start=True, stop=True)], reads=[wukT_r, q_r[h]], writes=[br])
            qv = qlat[:, :].rearrange("p (s h t) -> p s h t", s=NS, h=8)[:, :, h, :]
            kb.op(ACT if h % 2 == 0 else DVE,
                  (lambda e, qv=qv, bank=bank: e.activation(out=qv, in_=v3(bank[:, 0:N], cx, TS), func=AF.Copy)) if h % 2 == 0 else
                  (lambda e, qv=qv, bank=bank: e.tensor_copy(out=qv, in_=v3(bank[:, 0:N], cx, TS))),
                  reads=[br], writes=[qlat_r])
        bank, br = banks[3], bank_r[3]
        for s_ in range(NS):
            kb.mm([lambda e, s_=s_, bank=bank: e.transpose(bank[0:TS, s_ * 128:(s_ + 1) * 128], arF[:, 7, s_ * TS:(s_ + 1) * TS],
                                                            ident[:, :])], reads=[yr[7], ident_r], writes=[br])
        kb.op(DVE, lambda e, bank=bank: e.tensor_copy(out=cntok[0:TS, :, :], in_=bank[0:TS, 0:NS * 128].rearrange("p (s r) -> p s r", s=NS)),
              reads=[br], writes=[cntok_r])
        W = 8 * TS
        ybank, ybr = banks[0], bank_r[0]
        chunk_i = [0]
        for s_ in range(NS):
            obank, obr = banks[1 + s_ % 2], bank_r[1 + s_ % 2]
            nblk = NPB + 1
            blk = 0
            pend = []
            qlv = qlat[:, s_ * W:(s_ + 1) * W]
            qpv = qpe[64:96, s_ * W:(s_ + 1) * W]

            def do_pv(cap, cres, kk, pi, b, obank=obank, obr=obr, nblk=nblk):
                kb.mm([lambda e: e.matmul(obank[:, 0:W], cap, pts[0:kk, pi, :], start=(b == 0), stop=(b == nblk - 1)),
                       lambda e: e.matmul(obank[:, W:2 * W], ones_b[0:kk, :], pts[0:kk, pi, :], start=False, stop=(b == nblk - 1),
                                          skip_group_check=True)],
                      reads=[pts_r[pi], ones_r] + list(cres), writes=[obr])

            def do_sc(ctap, ktap, cres, kk, b, qlv=qlv, qpv=qpv):
                sbi = 4 + b % 3
                pi = b % 2
                kb.mm([lambda e: e.matmul(banks[sbi][0:kk, 0:W], ctap, qlv, start=True, stop=False),
                       lambda e: e.matmul(banks[sbi][0:kk, 0:W], ktap, qpv, start=False, stop=True)],
                      reads=list(cres) + [qlat_r, qpe_r], writes=[bank_r[sbi]])
                kb.op(ACT, lambda e: e.activation(out=pts[0:kk, pi, :], in_=banks[sbi][0:kk, 0:W], func=AF.Exp, scale=SCALE),
                      reads=[bank_r[sbi]], writes=[pts_r[pi]])
                return pi

            for ch in range(PAST // 1024):
                ci = chunk_i[0] % 2
                chunk_i[0] += 1
                sa, sb_ = 2 * ci, 2 * ci + 1
                resA = ar[4 * sa:4 * sa + 4]
                resB = ar[4 * sb_:4 * sb_ + 4]
                cTs = arB[:, 4 * sa:4 * sa + 2, :]
                ccs = arB[:, 4 * sa + 2:4 * sa + 4, :]
                kTs = arB[64:96, 4 * sb_:4 * sb_ + 2, :]
                kb.dma(GQ, cTs, cT_in[l, s_, :, ch * 1024:(ch + 1) * 1024].rearrange("p (a n) -> p a n", a=2), writes=resA)
                kb.dma(GQ, ccs.rearrange("p a (b r) -> p (a b) r", r=128),
                       cc_in[l, s_, ch * 1024:(ch + 1) * 1024, :].rearrange("(b p) r -> p b r", p=128), writes=resA)
                kb.dma(GQ, kTs, kT_in[l, s_, :, ch * 1024:(ch + 1) * 1024].rearrange("p (a n) -> p a n", a=2), writes=resB)
                for bi in range(8):
                    ctap = arB[:, 4 * sa + bi // 4, (bi % 4) * 128:(bi % 4) * 128 + 128]
                    ktap = arB[64:96, 4 * sb_ + bi // 4, (bi % 4) * 128:(bi % 4) * 128 + 128]
                    cap = arB[:, 4 * sa + 2 + bi // 4, (bi % 4) * 128:(bi % 4) * 128 + 128]
                    pi = do_sc(ctap, ktap, list(resA) + list(resB), 128, blk)
                    pend.append((cap, list(resA), 128, pi, blk))
                    blk += 1
                    if len(pend) > 1:
                        do_pv(*pend.pop(0))
            pi = do_sc(ckvb[:, s_ * TS:(s_ + 1) * TS], kpeb[64:96, s_ * TS:(s_ + 1) * TS], [ckvb_r, kpeb_r], TS, blk)
            pend.append((cntok[0:TS, s_, :], [cntok_r], TS, pi, blk))
            while pend:
                do_pv(*pend.pop(0))
            oi = s_ % 2
            kb.op(DVE, lambda e, obank=obank: e.reciprocal(out=recs[:, :], in_=obank[:, W:2 * W]), reads=[obr], writes=[recs_r])
            kb.op(DVE, lambda e, obank=obank, oi=oi: e.tensor_tensor(out=onb[:, oi, :], in0=obank[:, 0:W], in1=recs[:, :], op=ALU.mult),
                  reads=[obr, recs_r], writes=[onb_r[oi]])
            for h in range(8):
                p0 = (h % 2) * 64
                c0 = (h // 2) * N + s_ * TS
                kb.mm([lambda e, h=h, p0=p0, c0=c0, oi=oi: e.matmul(ybank[p0:p0 + 64, c0:c0 + TS], wuv_sb[:, h * 64:(h + 1) * 64],
                                                                    onb[:, oi, h * TS:(h + 1) * TS], start=True, stop=True)],
                      reads=[wuv_r, onb_r[oi]], writes=[ybr])
        kb.op(DVE, lambda e: e.tensor_copy(out=mixb[:, 4:8, 0:N], in_=ybank[:, 0:4 * N].rearrange("p (c n) -> p c n", c=4)),
              reads=[ybr], writes=mr[4:8])

    def ffn(cx, l):
        N, S, T = cx.N, cx.S, cx.T
        W2 = T + 2
        rms_pre(cx, l, V_GFPRE)
        ffn_tail = []
        for g in range(11):
            s = wa_load(wup_b[l, g], wup_r[l][g])
            for pi in range(2):
                pair = g * 2 + pi
                r = pair % 2
                bks = []
                for xx in range(2):
                    bank, br = next_bank()
                    mm_k8(bank, br, N, s, (pi * 2 + xx) * 128, 128, hb, hr)
                    bks.append((bank, br))
                ubv = ub[:, r, :, 0:S * W2].rearrange("p a (s t) -> p a s t", s=S)
                for xx in range(2):
                    bank, br = bks[xx]
                    kb.op(ACT, lambda e, xx=xx, bank=bank, ubv=ubv: e.activation(out=ubv[:, xx, :, 2:W2], in_=v3(bank[:, 0:N], cx, T),
                                                                                  func=AF.Copy), reads=[br], writes=[ub_r[r]])
                kb.op(POOL, lambda e, ubv=ubv, pair=pair: e.tensor_copy(out=ubv[:, :, :, 0:2], in_=cx.fst[:, l, pair, :, :, :]),
                      reads=[cx.fst_r[l][pair]], writes=[ub_r[r]])
                kb.op(POOL, lambda e, ubv=ubv, pair=pair: e.tensor_copy(out=cx.fst[:, l, pair, :, :, :], in_=ubv[:, :, :, T:W2]),
                      reads=[ub_r[r]], writes=[cx.fst_r[l][pair]])
                for xx in range(2):
                    q = DVE
                    chn = pair + xx * NPAIR
                    acc = v3(facc[:, r, xx, 0:N], cx, T)
                    fr = facc_r[r][xx]
                    kb.op(ACT, lambda e, xx=xx, acc=acc, ubv=ubv, chn=chn: e.activation(
                        out=acc, in_=ubv[:, xx, :, 0:T], func=AF.Identity, scale=vcol(l, V_WFFN + chn), bias=vcol(l, V_BFFN + chn)),
                        reads=[ub_r[r], vecs_r], writes=[fr])
                    kb.op(q, lambda e, xx=xx, acc=acc, ubv=ubv, chn=chn: e.scalar_tensor_tensor(
                        out=acc, in0=ubv[:, xx, :, 1:T + 1], scalar=vcol(l, V_WFFN + 44 + chn), in1=acc, op0=ALU.mult, op1=ALU.add),
                        reads=[ub_r[r], vecs_r, fr], writes=[fr])
                    kb.op(q, lambda e, xx=xx, acc=acc, ubv=ubv, chn=chn: e.scalar_tensor_tensor(
                        out=acc, in0=ubv[:, xx, :, 2:W2], scalar=vcol(l, V_WFFN + 88 + chn), in1=acc, op0=ALU.mult, op1=ALU.add),
                        reads=[ub_r[r], vecs_r, fr], writes=[fr])
                def tail(r=r, pair=pair):
                    kb.op(ACT, lambda e: e.activation(out=facc[:, r, 0, 0:N], in_=facc[:, r, 0, 0:N], func=AF.Silu),
                          reads=[facc_r[r][0]], writes=[facc_r[r][0]])
                    kb.op(POOL, lambda e: e.tensor_tensor(out=arB[:, pair, 0:N], in0=facc[:, r, 0, 0:N],
                                                          in1=facc[:, r, 1, 0:N], op=ALU.mult),
                          reads=[facc_r[r][0], facc_r[r][1]], writes=[ar[pair]])
                while ffn_tail:
                    ffn_tail.pop(0)()
                ffn_tail.append(tail)
        while ffn_tail:
            ffn_tail.pop(0)()
        for m in range(8):
            s = rot["wb"]
            rot["wb"] = (s + 1) % 3
            kb.dma(SP, wb[:, s], wdn_b[l, m], reads=[wdn_r[l][m]], writes=[wb_r[s]])
            bank, br = next_bank()
            kb.mm([lambda e, kc=kc, s=s, bank=bank: e.matmul(bank[:, 0:N], wb[:, s, kc, :], arB[:, kc, 0:N],
                                                             start=(kc == 0), stop=(kc == NPAIR - 1)) for kc in range(NPAIR)],
                  reads=[wb_r[s]] + ar, writes=[br])
            post_consumer(cx, m, bank, br)
        post_finish(cx, l, V_GFPOST)

    def run_tile(cx):
        N = cx.N
        kb.dma(SP, x[:, :, 0:N], cx.x_in, writes=xr)
        kb.dma(SP, cosF[64:96, 0:N], cx.cos_in, writes=[cs_r])
        kb.dma(SP, sinF[64:96, 0:N], cx.sin_in, writes=[cs_r])
        for l in range(L):
            if not cx.prompt and l + 1 < L:
                emit_casts(l + 1)
            mixer(cx, l)
            ffn(cx, l)
        kb.dma(SP, cx.y_out, x[:, :, 0:N], reads=xr)

    cx = Ctx()
    cx.prompt = False
    cx.S, cx.T, cx.N, cx.j = NS, TS, NSX, 0
    cx.cst, cx.cst_r, cx.fst, cx.fst_r = cstS, cstS_r, fstS, fstS_r
    cx.x_in = xsT.rearrange("(c p) n -> p c n", p=128)
    cx.y_out = ysT.rearrange("(c p) n -> p c n", p=128)
    cx.cos_in, cx.sin_in = cosS[:, :], sinS[:, :]
    cx.ckv_out = lambda l: sckvT[l]
    cx.kpe_out = lambda l: skpeT[l]
    run_tile(cx)
    for j in range(NT):
        cx = Ctx()
        cx.prompt = True
        cx.S, cx.T, cx.N, cx.j = 1, 512, 512, j
        cx.cst, cx.cst_r, cx.fst, cx.fst_r = cstP, cstP_r, fstP, fstP_r
        cs = slice(j * 512, (j + 1) * 512)
        cx.x_in = xT.rearrange("(c p) n -> p c n", p=128)[:, :, cs]
        cx.y_out = yT.rearrange("(c p) n -> p c n", p=128)[:, :, cs]
        cx.cos_in, cx.sin_in = cosP[:, cs], sinP[:, cs]
        cx.ckv_out = lambda l, cs=cs: pckvT[l, :, cs]
        cx.kpe_out = lambda l, cs=cs: pkpeT[l, :, cs]
        run_tile(cx)
    allc = [r for rr in cstP_r for r in rr]
    kb.dma(SP, pconv.rearrange("p (l c s t) -> p l c s t", l=L, c=4, s=1), cstP[:, :, :, :, :], reads=allc)
    kb.dma(SP, pffn.rearrange("p (l c a s t) -> p l c a s t", l=L, c=NPAIR, a=2, s=1), fstP[:, :, :, :, :, :],
           reads=[r for rr in fstP_r for r in rr])
    kb.dma(SP, sconv.rearrange("p (l c s t) -> p l c s t", l=L, c=4, s=NS), cstS[:, :, :, :, :],
           reads=[r for rr in cstS_r for r in rr])
    kb.dma(SP, sffn.rearrange("p (l c a s t) -> p l c a s t", l=L, c=NPAIR, a=2, s=NS), fstS[:, :, :, :, :, :],
           reads=[r for rr in fstS_r for r in rr])
    kb.finish()

    with nc.Block() as block:
        @block.sync
        def _(e):
            for th in SP.prog:
                th(e)

        @block.tensor
        def _(e):
            for th in PE.prog:
                th(e)

        @block.scalar
        def _(e):
            for th in ACT.prog:
                th(e)

        @block.vector
        def _(e):
            for th in DVE.prog:
                th(e)

        @block.gpsimd
        def _(e):
            for th in POOL.prog:
                th(e)
    es.close()
    return nc


def _fm(v, nch):
    Lh = v.shape[0]
    return np.ascontiguousarray(v.reshape(Lh, nch, 128).transpose(2, 0, 1))


def _rope_tables(pos):
    inv = (1.0 / (np.float32(10000.0) ** (np.arange(0, 32, 2, dtype=np.float32) / np.float32(32)))).astype(np.float32)
    ang = pos.astype(np.float32)[:, None] * inv[None, :]
    c = np.cos(ang).astype(np.float32).T
    s = np.sin(ang).astype(np.float32).T
    return np.ascontiguousarray(np.concatenate([c, c], 0)), np.ascontiguousarray(np.concatenate([-s, s], 0))


def kernel(x_prompt, x_sample, cache_ckv, cache_kpe, state_conv, state_ffn,
           w_in, w_conv, g_qa, w_uq, g_kva, w_uk, w_uv, w_o, g_mix_pre, g_mix_post,
           w_up, w_ffn_conv, b_ffn_conv, w_down, g_ffn_pre, g_ffn_post):
    f = lambda a: np.asarray(a, dtype=np.float32)
    x_prompt, x_sample, cache_ckv, cache_kpe, state_conv, state_ffn = map(f, (x_prompt, x_sample, cache_ckv, cache_kpe, state_conv, state_ffn))
    w_in, w_conv, g_qa, w_uq, g_kva, w_uk, w_uv, w_o = map(f, (w_in, w_conv, g_qa, w_uq, g_kva, w_uk, w_uv, w_o))
    g_mix_pre, g_mix_post, w_up, w_ffn_conv, b_ffn_conv, w_down, g_ffn_pre, g_ffn_post = map(
        f, (g_mix_pre, g_mix_post, w_up, w_ffn_conv, b_ffn_conv, w_down, g_ffn_pre, g_ffn_post))
    BP, SEQ, _ = x_prompt.shape
    BS, TS, _ = x_sample.shape
    L = w_in.shape[0]
    PAST = cache_ckv.shape[2]
    assert BP == N_CORES and BS % N_CORES == 0
    NS = BS // N_CORES
    NSX = NS * TS

    xv, gb, gc, qa, kva, kpe = np.split(w_in, [512, 1024, 1536, 1792, 1920], axis=2)
    kpes = np.concatenate([kpe[:, :, 16:32], kpe[:, :, 0:16]], axis=2)
    pad = np.zeros((L, D, 64), np.float32)
    w_in_x = np.ascontiguousarray(np.concatenate([xv, gc, gb, qa, kva, kpe, kpes, pad], axis=2))
    wq = w_uq.reshape(L, 256, 8, 96)
    w_uq_x = np.ascontiguousarray(np.concatenate([wq, wq[..., 80:96], wq[..., 64:80]], axis=3).reshape(L, 256, 1024))
    w_uk2 = np.ascontiguousarray(w_uk.reshape(L, 128, 512))
    w_ukT = np.ascontiguousarray(w_uk.transpose(0, 3, 2, 1).reshape(L, 64, 1024))
    w_uv2 = np.ascontiguousarray(w_uv.reshape(L, 128, 512))
    upa = w_up[:, :, :DFF].reshape(L, D, NPAIR, 1, 128)
    upb = w_up[:, :, DFF:].reshape(L, D, NPAIR, 1, 128)
    w_up_x = np.ascontiguousarray(np.concatenate([upa, upb], axis=3).reshape(L, D, 2 * DFF))
    vecs = np.zeros((128, L, VL), np.float32)
    vecs[:, :, V_GMPRE:V_GMPRE + 8] = _fm(g_mix_pre, 8)
    vecs[:, :, V_GMPOST:V_GMPOST + 8] = _fm(g_mix_post, 8)
    vecs[:, :, V_GFPRE:V_GFPRE + 8] = _fm(g_ffn_pre, 8)
    vecs[:, :, V_GFPOST:V_GFPOST + 8] = _fm(g_ffn_post, 8)
    vecs[:, :, V_GQA:V_GQA + 2] = _fm(g_qa, 2)
    vecs[:, :, V_GKVA:V_GKVA + 1] = _fm(g_kva, 1)
    for k in range(3):
        vecs[:, :, V_WCONV + 4 * k:V_WCONV + 4 * k + 4] = _fm(w_conv[:, k, :], 4)
        vecs[:, :, V_WFFN + 44 * k:V_WFFN + 44 * k + 44] = _fm(w_ffn_conv[:, k, :], 44)
    vecs[:, :, V_BFFN:V_BFFN + 44] = _fm(b_ffn_conv, 44)
    vecs = np.ascontiguousarray(vecs.reshape(128, L * VL))
    cosP, sinP = _rope_tables(np.arange(SEQ))
    cS, sS = _rope_tables(PAST + np.arange(TS))
    cosS = np.ascontiguousarray(np.tile(cS, (1, NS)))
    sinS = np.ascontiguousarray(np.tile(sS, (1, NS)))
    ident = np.eye(128, dtype=np.float32)

    nc = build(L, SEQ, PAST, NS, TS)
    in_maps = []
    for c in range(N_CORES):
        sl = slice(c * NS, (c + 1) * NS)
        cc = cache_ckv[:, sl]
        stc = state_conv[:, sl]
        stf = state_ffn[:, sl]
        cst = stc.reshape(L, NS, 2, 4, 128).transpose(4, 0, 3, 1, 2)
        fst = stf.reshape(L, NS, 2, 2, NPAIR, 128).transpose(5, 0, 4, 3, 1, 2)
        in_maps.append({
            "xT": np.ascontiguousarray(x_prompt[c].T),
            "xsT": np.ascontiguousarray(x_sample[sl].reshape(NSX, D).T),
            "cT": np.ascontiguousarray(cc.transpose(0, 1, 3, 2)),
            "kT": np.ascontiguousarray(cache_kpe[:, sl].transpose(0, 1, 3, 2)),
            "cc": np.ascontiguousarray(cc),
            "cst": np.ascontiguousarray(cst).reshape(128, -1),
            "fst": np.ascontiguousarray(fst).reshape(128, -1),
            "w_in": w_in_x, "w_uq": w_uq_x, "w_uk": w_uk2, "w_ukT": w_ukT, "w_uv": w_uv2, "w_o": w_o,
            "w_up": w_up_x, "w_dn": w_down, "vecs": vecs,
            "cosP": cosP, "sinP": sinP, "cosS": cosS, "sinS": sinS, "ident": ident,
        })
    res = run_bass_kernel_spmd(nc, in_maps, core_ids=list(range(N_CORES)))
    R = res.results
    if DEBUG:
        kernel.dbg = [R[c]["dbg"] for c in range(N_CORES)]
    y_prompt = np.stack([R[c]["yT"].T for c in range(N_CORES)])
    y_sample = np.concatenate([R[c]["ysT"].T.reshape(NS, TS, D) for c in range(N_CORES)])
    p_ckv = np.stack([R[c]["pckvT"].transpose(0, 2, 1) for c in range(N_CORES)], axis=1)
    p_kpe = np.stack([R[c]["pkpeT"].transpose(0, 2, 1) for c in range(N_CORES)], axis=1)
    p_conv = np.stack([R[c]["pconv"].reshape(128, L, 4, 2).transpose(1, 3, 2, 0).reshape(L, 2, 512) for c in range(N_CORES)], axis=1)
    p_ffn = np.stack([R[c]["pffn"].reshape(128, L, NPAIR, 2, 2).transpose(1, 4, 3, 2, 0).reshape(L, 2, 2 * DFF)
                      for c in range(N_CORES)], axis=1)
    s_ckv = np.concatenate([R[c]["sckvT"].transpose(0, 2, 1).reshape(L, NS, TS, 128) for c in range(N_CORES)], axis=1)
    s_kpe = np.concatenate([R[c]["skpeT"].transpose(0, 2, 1).reshape(L, NS, TS, 32) for c in range(N_CORES)], axis=1)
    s_conv = np.concatenate([R[c]["sconv"].reshape(128, L, 4, NS, 2).transpose(1, 3, 4, 2, 0).reshape(L, NS, 2, 512)
                             for c in range(N_CORES)], axis=1)
    s_ffn = np.concatenate([R[c]["sffn"].reshape(128, L, NPAIR, 2, NS, 2).transpose(1, 4, 5, 3, 2, 0).reshape(L, NS, 2, 2 * DFF)
                            for c in range(N_CORES)], axis=1)
    outs = (y_prompt, y_sample, p_ckv, p_kpe, p_conv, p_ffn, s_ckv, s_kpe, s_conv, s_ffn)
    return tuple(np.ascontiguousarray(o, dtype=np.float32) for o in outs)
```

```python
import math
from contextlib import ExitStack
import numpy as np
import concourse.bass as bass
import concourse.mybir as mybir
from concourse.bass_utils import run_bass_kernel_spmd

F32 = mybir.dt.float32
BF16 = mybir.dt.bfloat16
AF = mybir.ActivationFunctionType
ALU = mybir.AluOpType

D = 1024
DC = 512
NH = 8
DFF = 2816
NPAIR = 22
EPS = 1e-6
SCALE = 1.0 / math.sqrt(96.0)
VL = 223
V_GMPRE, V_GMPOST, V_GFPRE, V_GFPOST, V_GQA, V_GKVA, V_WCONV, V_WFFN, V_BFFN = 0, 8, 16, 24, 32, 34, 35, 47, 179
SEM_LIMIT = 30000
NSLOT = 10
N_CORES = 8
DEBUG = False


class Res:
    __slots__ = ("lw", "rd")

    def __init__(self):
        self.lw = None
        self.rd = {}


class Q:
    def __init__(self, kb, kind):
        self.kb = kb
        self.kind = kind
        self.prog = []
        self.seen = {}
        self.own = set()
        self.sem = None
        self.val = 0
        self.slots = [[None, 0] for _ in range(NSLOT)]
        self.nd = 0

    def wait(self, ev):
        if ev is None:
            return
        s, v = ev
        if self.kind == "pe" and s in self.own:
            return
        if self.seen.get(s, 0) >= v:
            return
        self.seen[s] = v
        self.prog.append(lambda e, s=s, v=v: e.wait_ge(s, v))

    def bump(self, fn):
        if self.sem is None or self.val >= SEM_LIMIT:
            self.sem = self.kb.new_sem()
            self.own.add(self.sem)
            self.val = 0
        self.val += 1
        s = self.sem
        self.prog.append(lambda e, fn=fn, s=s: fn(e).then_inc(s, 1))
        return (s, self.val)


class KB:
    def __init__(self, nc, es):
        self.nc = nc
        self.es = es
        self.nsem = 0
        self.pe = Q(self, "pe")
        self.act = Q(self, "cmp")
        self.dve = Q(self, "cmp")
        self.pool = Q(self, "cmp")
        self.sp = Q(self, "dma")
        self.gq = self.pool

    def new_sem(self):
        self.nsem += 1
        return self.es.enter_context(self.nc.semaphore(f"s{self.nsem}"))

    def deps(self, q, reads, writes):
        for r in reads:
            q.wait(r.lw)
        for w in writes:
            q.wait(w.lw)
            for s, v in w.rd.items():
                q.wait((s, v))

    def done(self, ev, reads, writes):
        s, v = ev
        for r in reads:
            if r.rd.get(s, 0) < v:
                r.rd[s] = v
        for w in writes:
            w.lw = ev
            w.rd = {}

    def op(self, q, fn, reads=(), writes=()):
        self.deps(q, reads, writes)
        ev = q.bump(fn)
        self.done(ev, reads, writes)

    def mm(self, fns, reads=(), writes=()):
        q = self.pe
        self.deps(q, reads, writes)
        for f in fns[:-1]:
            q.prog.append(lambda e, f=f: f(e))
        ev = q.bump(fns[-1])
        self.done(ev, reads, writes)

    def dma(self, q, out, in_, reads=(), writes=()):
        self.deps(q, reads, writes)
        k = q.nd % NSLOT
        q.nd += 1
        slot = q.slots[k]
        if slot[0] is None or slot[1] >= SEM_LIMIT:
            slot[0] = self.new_sem()
            slot[1] = 0
        else:
            q.wait((slot[0], slot[1]))
        slot[1] += 16
        s = slot[0]
        q.prog.append(lambda e, out=out, in_=in_, s=s: e.dma_start(out=out, in_=in_).then_inc(s, 16))
        self.done((s, slot[1]), reads, writes)

    def finish(self):
        for q in (self.sp, self.pool):
            for s, v in q.slots:
                if s is not None:
                    q.wait((s, v))


def build(L, SEQ, PAST, NS, TS):
    NT = SEQ // 512
    NSX = NS * TS
    NPB = PAST // 128
    assert PAST % 1024 == 0 and SEQ % 512 == 0
    nc = bass.Bass("TRN2", target_bir_lowering=False)
    es = ExitStack()
    kb = KB(nc, es)
    PE, ACT, DVE, POOL, SP = kb.pe, kb.act, kb.dve, kb.pool, kb.sp

    def din(name, shape):
        return nc.dram_tensor(name, list(shape), F32, kind="ExternalInput").ap()

    def dout(name, shape):
        return nc.dram_tensor(name, list(shape), F32, kind="ExternalOutput").ap()

    def dscr(name, shape):
        return nc.dram_tensor(name, list(shape), BF16, kind="Internal").ap()

    def sb(name, shape, dt=F32):
        return es.enter_context(nc.sbuf_tensor("sb_" + name, list(shape), dt))

    xT = din("xT", [D, SEQ])
    xsT = din("xsT", [D, NSX])
    cT_in = din("cT", [L, NS, 128, PAST])
    kT_in = din("kT", [L, NS, 32, PAST])
    cc_in = din("cc", [L, NS, PAST, 128])
    cst_in = din("cst", [128, L * 4 * NS * 2])
    fst_in = din("fst", [128, L * NPAIR * 2 * NS * 2])
    w_in = din("w_in", [L, D, 2048])
    w_uq = din("w_uq", [L, 256, 1024])
    w_uk = din("w_uk", [L, 128, 512])
    w_ukT = din("w_ukT", [L, 64, 1024])
    w_uv = din("w_uv", [L, 128, 512])
    w_o = din("w_o", [L, D, D])
    w_up = din("w_up", [L, D, 2 * DFF])
    w_dn = din("w_dn", [L, DFF, D])
    vecs_in = din("vecs", [128, L * VL])
    cosP = din("cosP", [32, SEQ])
    sinP = din("sinP", [32, SEQ])
    cosS = din("cosS", [32, NSX])
    sinS = din("sinS", [32, NSX])
    ident_in = din("ident", [128, 128])

    yT = dout("yT", [D, SEQ])
    ysT = dout("ysT", [D, NSX])
    pckvT = dout("pckvT", [L, 128, SEQ])
    pkpeT = dout("pkpeT", [L, 32, SEQ])
    pconv = dout("pconv", [128, L * 4 * 2])
    pffn = dout("pffn", [128, L * NPAIR * 2 * 2])
    sckvT = dout("sckvT", [L, 128, NSX])
    skpeT = dout("skpeT", [L, 32, NSX])
    sconv = dout("sconv", [128, L * 4 * NS * 2])
    sffn = dout("sffn", [128, L * NPAIR * 2 * NS * 2])
    dbg = dout("dbg", [128, 8, NSX]) if DEBUG else None

    win_b = dscr("win_b", [L, 4, 128, 8, 512])
    wo_b = dscr("wo_b", [L, 2, 128, 8, 512])
    wup_b = dscr("wup_b", [L, 11, 128, 8, 512])
    wdn_b = dscr("wdn_b", [L, 8, 128, NPAIR, 128])
    wuq_b = dscr("wuq_b", [L, 128, 2, 1024])
    wuk_b = dscr("wuk_b", [L, 128, 512])
    wukT_b = dscr("wukT_b", [L, 64, 1024])
    wuv_b = dscr("wuv_b", [L, 128, 512])
    Ksc = dscr("Ksc", [L, 2, NT, 96, 4, 512])
    Vsc = dscr("Vsc", [L, 2, NT, 128, 4, 4, 128])
    win_r = [[Res() for _ in range(4)] for _ in range(L)]
    wo_r = [[Res() for _ in range(2)] for _ in range(L)]
    wup_r = [[Res() for _ in range(11)] for _ in range(L)]
    wdn_r = [[Res() for _ in range(8)] for _ in range(L)]
    wsm_r = [[Res() for _ in range(4)] for _ in range(L)]
    ksc_r = [[[Res() for _ in range(NT)] for _ in range(2)] for _ in range(L)]
    vsc_r = [[[Res() for _ in range(NT)] for _ in range(2)] for _ in range(L)]

    x = sb("x", [128, 8, 512]); xr = [Res() for _ in range(8)]
    hb = sb("hb", [128, 8, 512], BF16); hr = [Res() for _ in range(8)]
    mixb = sb("mixb", [128, 8, 512], BF16); mr = [Res() for _ in range(8)]
    arF = sb("arF", [128, 8, 512]); yr = [Res() for _ in range(8)]
    arB = sb("arB", [128, NPAIR, 512], BF16); ar = [Res() for _ in range(NPAIR)]
    sq = sb("sq", [128, 2, 512], BF16); sqr = [Res(), Res()]
    srt = sb("srt", [128, 512]); srt_r = Res()
    rstd = sb("rstd", [128, 512]); rstd_r = Res()
    ubc = sb("ubc", [128, 2, 516]); ubc_r = [Res(), Res()]
    qn = sb("qn", [128, 2, 512], BF16); qn_r = [Res(), Res()]
    ckvb = sb("ckvb", [128, 512], BF16); ckvb_r = Res()
    rt1 = sb("rt1", [128, 2, 512]); rt1_r = [Res(), Res()]
    rt2 = sb("rt2", [128, 2, 512]); rt2_r = [Res(), Res()]
    kpef = sb("kpef", [128, 512]); kpef_r = Res()
    Qb = sb("Qb", [128, 8, 512], BF16); q_r = [Res() for _ in range(8)]
    Kcur = sb("Kcur", [128, 8, 512], BF16); k_r = [Res() for _ in range(8)]
    Vcur = sb("Vcur", [128, 2, 4, 4, 128], BF16); v_r = [Res() for _ in range(4)]
    rec = sb("rec", [128, 2, 512]); rec_r = [Res(), Res()]
    wa = sb("wa", [128, 3, 8, 512], BF16); wa_r = [Res(), Res(), Res()]
    wb = sb("wb", [128, 3, NPAIR, 128], BF16); wb_r = [Res(), Res(), Res()]
    wuq_sb = sb("wuq_sb", [128, 2, 1024], BF16); wuq_r = Res()
    wuk_sb = sb("wuk_sb", [128, 512], BF16); wuk_r = Res()
    wukT_sb = sb("wukT_sb", [128, 1024], BF16); wukT_r = Res()
    wuv_sb = sb("wuv_sb", [128, 512], BF16); wuv_r = Res()
    ub = sb("ub", [128, 2, 2, 516]); ub_r = [Res(), Res()]
    facc = sb("facc", [128, 2, 2, 512]); facc_r = [[Res(), Res()], [Res(), Res()]]
    cstP = sb("cstP", [128, L, 4, 1, 2]); cstP_r = [[Res() for _ in range(4)] for _ in range(L)]
    fstP = sb("fstP", [128, L, NPAIR, 2, 1, 2]); fstP_r = [[Res() for _ in range(NPAIR)] for _ in range(L)]
    cstS = sb("cstS", [128, L, 4, NS, 2]); cstS_r = [[Res() for _ in range(4)] for _ in range(L)]
    fstS = sb("fstS", [128, L, NPAIR, 2, NS, 2]); fstS_r = [[Res() for _ in range(NPAIR)] for _ in range(L)]
    ones_b = sb("ones_b", [128, 128], BF16); ones_r = Res()
    ident = sb("ident", [128, 128]); ident_r = Res()
    vecs = sb("vecs", [128, L * VL]); vecs_r = Res()
    cosF = sb("cosF", [128, 512]); sinF = sb("sinF", [128, 512]); cs_r = Res()
    qlat = sb("qlat", [128, NS * 8 * TS], BF16); qlat_r = Res()
    qpe = sb("qpe", [128, NS * 8 * TS], BF16); qpe_r = Res()
    kpeb = sb("kpeb", [128, NSX], BF16); kpeb_r = Res()
    onb = sb("onb", [128, 2, 8 * TS], BF16); onb_r = [Res(), Res()]
    cntok = sb("cntok", [128, NS, 128], BF16); cntok_r = Res()
    recs = sb("recs", [128, 8 * TS]); recs_r = Res()
    pts = sb("pts", [128, 2, 8 * TS], BF16); pts_r = [Res(), Res()]

    banks = [es.enter_context(nc.psum_tensor(f"bank{i}", [128, 512], F32)) for i in range(8)]
    bank_r = [Res() for _ in range(8)]
    STATB = 7
    rot = {"bank": 0, "wa": 0, "wb": 0, "sq": 0}

    def next_bank():
        i = rot["bank"]
        rot["bank"] = (i + 1) % 7
        return banks[i], bank_r[i]

    def vcol(l, off):
        c = l * VL + off
        return vecs[:, c:c + 1]

    kb.op(POOL, lambda e: e.memset(ones_b[:, :], 1.0), writes=[ones_r])
    kb.op(POOL, lambda e: e.memset(Vcur[:, :, :, :, :].rearrange("p a b c d -> p (a b c) d")[:, :, 64:128], 1.0), writes=v_r)
    kb.op(POOL, lambda e: e.memset(cstP[:, :, :, :, :].rearrange("p l c s t -> p (l c s t)"), 0.0), writes=[r for rr in cstP_r for r in rr])
    kb.op(POOL, lambda e: e.memset(fstP[:, :, :, :, :, :].rearrange("p l c a s t -> p (l c a s t)"), 0.0), writes=[r for rr in fstP_r for r in rr])
    kb.dma(SP, vecs[:, :], vecs_in[:, :], writes=[vecs_r])
    kb.dma(SP, ident[:, :], ident_in[:, :], writes=[ident_r])
    kb.dma(SP, cstS[:, :, :, :, :], cst_in.rearrange("p (l c s t) -> p l c s t", l=L, c=4, s=NS),
           writes=[r for rr in cstS_r for r in rr])
    kb.dma(SP, fstS[:, :, :, :, :, :], fst_in.rearrange("p (l c a s t) -> p l c a s t", l=L, c=NPAIR, a=2, s=NS),
           writes=[r for rr in fstS_r for r in rr])
    GQ = kb.gq

    def emit_casts(l):
        for g in range(4):
            kb.dma(GQ, win_b[l, g], w_in[l].rearrange("(kc p) n -> p kc n", p=128)[:, :, g * 512:(g + 1) * 512],
                   writes=[win_r[l][g]])
        kb.dma(GQ, wuq_b[l], w_uq[l].rearrange("(kc p) n -> p kc n", p=128), writes=[wsm_r[l][0]])
        kb.dma(GQ, wuk_b[l], w_uk[l], writes=[wsm_r[l][1]])
        kb.dma(GQ, wukT_b[l], w_ukT[l], writes=[wsm_r[l][2]])
        kb.dma(GQ, wuv_b[l], w_uv[l], writes=[wsm_r[l][3]])
        for g in range(2):
            kb.dma(GQ, wo_b[l, g], w_o[l].rearrange("(kc p) n -> p kc n", p=128)[:, :, g * 512:(g + 1) * 512],
                   writes=[wo_r[l][g]])
        for g in range(11):
            kb.dma(GQ, wup_b[l, g], w_up[l].rearrange("(kc p) n -> p kc n", p=128)[:, :, g * 512:(g + 1) * 512],
                   writes=[wup_r[l][g]])
        for m in range(8):
            kb.dma(GQ, wdn_b[l, m], w_dn[l].rearrange("(kc p) n -> p kc n", p=128)[:, :, m * 128:(m + 1) * 128],
                   writes=[wdn_r[l][m]])

    emit_casts(0)

    class Ctx:
        pass

    eps_t = sb("eps_t", [128, 1]); eps_r = Res()
    eps_ap = eps_t[:, 0:1]
    kb.op(POOL, lambda e: e.memset(eps_t[:, :], EPS), writes=[eps_r])

    def sq_next():
        i = rot["sq"]
        rot["sq"] = 1 - i
        return i

    def rms_pre(cx, l, goff):
        N = cx.N
        bank, br = banks[STATB], bank_r[STATB]
        for c in range(8):
            i = sq_next()
            if c % 4 in (0, 1):
                kb.op(ACT, lambda e, c=c, i=i: e.activation(out=sq[:, i, 0:N], in_=x[:, c, 0:N], func=AF.Square),
                      reads=[xr[c]], writes=[sqr[i]])
            else:
                kb.op(DVE if c % 4 == 2 else POOL, lambda e, c=c, i=i: e.tensor_tensor(out=sq[:, i, 0:N], in0=x[:, c, 0:N], in1=x[:, c, 0:N],
                                                                                      op=ALU.mult), reads=[xr[c]], writes=[sqr[i]])
            kb.mm([lambda e, c=c, i=i: e.matmul(bank[:, 0:N], ones_b[:, :], sq[:, i, 0:N], start=(c == 0), stop=(c == 7))],
                  reads=[sqr[i], ones_r], writes=[br])
        for c in (5, 6, 7):
            kb.op(ACT, lambda e, c=c: e.activation(out=arF[:, c, 0:N], in_=x[:, c, 0:N], func=AF.Copy, scale=vcol(l, goff + c)),
                  reads=[xr[c], vecs_r], writes=[yr[c]])
        kb.op(ACT, lambda e: e.activation(out=srt[:, 0:N], in_=bank[:, 0:N], func=AF.Ln, scale=1.0 / D, bias=eps_ap),
              reads=[br, eps_r], writes=[srt_r])
        kb.op(ACT, lambda e: e.activation(out=rstd[:, 0:N], in_=srt[:, 0:N], func=AF.Exp, scale=-0.5), reads=[srt_r], writes=[rstd_r])
        for c in range(8):
            if c < 5:
                kb.op(DVE, lambda e, c=c: e.scalar_tensor_tensor(out=hb[:, c, 0:N], in0=x[:, c, 0:N], scalar=vcol(l, goff + c),
                                                                 in1=rstd[:, 0:N], op0=ALU.mult, op1=ALU.mult),
                      reads=[xr[c], rstd_r, vecs_r], writes=[hr[c]])
            else:
                kb.op(POOL, lambda e, c=c: e.tensor_tensor(out=hb[:, c, 0:N], in0=arF[:, c, 0:N], in1=rstd[:, 0:N], op=ALU.mult),
                      reads=[yr[c], rstd_r], writes=[hr[c]])

    pend_stats = []

    def post_consumer(cx, m, bank, br):
        N = cx.N
        while pend_stats:
            pend_stats.pop(0)()
        i = sq_next()
        kb.op(ACT, lambda e: e.activation(out=arF[:, m, 0:N], in_=bank[:, 0:N], func=AF.Copy), reads=[br], writes=[yr[m]])
        kb.op(ACT, lambda e: e.activation(out=sq[:, i, 0:N], in_=bank[:, 0:N], func=AF.Square), reads=[br], writes=[sqr[i]])
        sbk, sbr = banks[STATB], bank_r[STATB]
        pend_stats.append(lambda: kb.mm([lambda e: e.matmul(sbk[:, 0:N], ones_b[:, :], sq[:, i, 0:N], start=(m == 0), stop=(m == 7))],
                                        reads=[sqr[i], ones_r], writes=[sbr]))

    def post_finish(cx, l, goff):
        N = cx.N
        sbk, sbr = banks[STATB], bank_r[STATB]
        while pend_stats:
            pend_stats.pop(0)()
        kb.op(ACT, lambda e: e.activation(out=srt[:, 0:N], in_=sbk[:, 0:N], func=AF.Ln, scale=1.0 / D, bias=eps_ap),
              reads=[sbr, eps_r], writes=[srt_r])
        kb.op(ACT, lambda e: e.activation(out=rstd[:, 0:N], in_=srt[:, 0:N], func=AF.Exp, scale=-0.5), reads=[srt_r], writes=[rstd_r])
        for c in range(8):
            q = DVE if c % 2 == 0 else POOL
            kb.op(q, lambda e, c=c: e.tensor_tensor(out=arF[:, c, 0:N], in0=arF[:, c, 0:N], in1=rstd[:, 0:N], op=ALU.mult),
                  reads=[yr[c], rstd_r], writes=[yr[c]])
            kb.op(DVE, lambda e, c=c: e.scalar_tensor_tensor(out=x[:, c, 0:N], in0=arF[:, c, 0:N], scalar=vcol(l, goff + c),
                                                           in1=x[:, c, 0:N], op0=ALU.mult, op1=ALU.add),
                  reads=[yr[c], xr[c], vecs_r], writes=[xr[c]])

    def wa_load(src_ap, src_res):
        s = rot["wa"]
        rot["wa"] = (s + 1) % 3
        kb.dma(SP, wa[:, s], src_ap, reads=[src_res], writes=[wa_r[s]])
        return s

    def mm_k8(bank, br, N, s, col0, M, rhs, rhs_res, split=False):
        fns = [lambda e, kc=kc: e.matmul(bank[0:M, 0:N], wa[:, s, kc, col0:col0 + M], rhs[:, kc, 0:N],
                                         start=(kc == 0), stop=(kc == 7)) for kc in range(8)]
        if split:
            for kc in range(8):
                kb.mm([fns[kc]], reads=[wa_r[s], rhs_res[kc]], writes=[br])
        else:
            kb.mm(fns, reads=[wa_r[s]] + list(rhs_res), writes=[br])

    def v3(ap2, cx, width):
        return ap2.rearrange("p (s t) -> p s t", s=cx.S)

    def mixer(cx, l):
        N, S, T = cx.N, cx.S, cx.T
        W2 = T + 2
        rms_pre(cx, l, V_GMPRE)
        s = wa_load(win_b[l, 0], win_r[l][0])
        for c in range(4):
            bank, br = next_bank()
            mm_k8(bank, br, N, s, c * 128, 128, hb, hr, split=(c == 0))
            kb.op(ACT, lambda e, c=c, bank=bank: e.activation(out=arF[:, c, 0:N], in_=bank[:, 0:N], func=AF.Copy),
                  reads=[br], writes=[yr[c]])
        s = wa_load(win_b[l, 1], win_r[l][1])
        for c in range(4):
            bank, br = next_bank()
            mm_k8(bank, br, N, s, c * 128, 128, hb, hr)
            r = c % 2
            uv = v3(ubc[:, r, 0:S * W2], cx, W2)
            kb.op(DVE, lambda e, c=c, bank=bank, uv=uv: e.tensor_tensor(out=uv[:, :, 2:W2], in0=v3(bank[:, 0:N], cx, T),
                                                                        in1=v3(arF[:, c, 0:N], cx, T), op=ALU.mult),
                  reads=[br, yr[c]], writes=[ubc_r[r]])
            kb.op(POOL, lambda e, c=c, uv=uv: e.tensor_copy(out=uv[:, :, 0:2], in_=cx.cst[:, l, c, :, :]),
                  reads=[cx.cst_r[l][c]], writes=[ubc_r[r]])
            kb.op(POOL, lambda e, c=c, uv=uv: e.tensor_copy(out=cx.cst[:, l, c, :, :], in_=uv[:, :, T:W2]),
                  reads=[ubc_r[r]], writes=[cx.cst_r[l][c]])
            acc = v3(arF[:, c, 0:N], cx, T)
            kb.op(DVE, lambda e, c=c, uv=uv, acc=acc: e.tensor_scalar(out=acc, in0=uv[:, :, 0:T], scalar1=vcol(l, V_WCONV + c),
                                                                      scalar2=0.0, op0=ALU.mult, op1=ALU.add),
                  reads=[ubc_r[r], vecs_r], writes=[yr[c]])
            kb.op(DVE, lambda e, c=c, uv=uv, acc=acc: e.scalar_tensor_tensor(out=acc, in0=uv[:, :, 1:T + 1],
                                                                              scalar=vcol(l, V_WCONV + 4 + c), in1=acc,
                                                                              op0=ALU.mult, op1=ALU.add),
                  reads=[ubc_r[r], vecs_r, yr[c]], writes=[yr[c]])
            kb.op(DVE, lambda e, c=c, uv=uv, acc=acc: e.scalar_tensor_tensor(out=acc, in0=uv[:, :, 2:W2],
                                                                             scalar=vcol(l, V_WCONV + 8 + c), in1=acc,
                                                                             op0=ALU.mult, op1=ALU.add),
                  reads=[ubc_r[r], vecs_r, yr[c]], writes=[yr[c]])
        s = wa_load(win_b[l, 2], win_r[l][2])
        for c in range(4):
            bank, br = next_bank()
            mm_k8(bank, br, N, s, c * 128, 128, hb, hr)
            kb.op(DVE, lambda e, c=c, bank=bank: e.tensor_tensor(out=mixb[:, c, 0:N], in0=bank[:, 0:N], in1=arF[:, c, 0:N],
                                                                 op=ALU.mult), reads=[br, yr[c]], writes=[mr[c]])
        s = wa_load(win_b[l, 3], win_r[l][3])
        sbk, sbr = banks[STATB], bank_r[STATB]
        for c in range(2):
            bank, br = next_bank()
            mm_k8(bank, br, N, s, c * 128, 128, hb, hr)
            i = sq_next()
            kb.op(ACT, lambda e, c=c, bank=bank: e.activation(out=arF[:, 4 + c, 0:N], in_=bank[:, 0:N], func=AF.Copy),
                  reads=[br], writes=[yr[4 + c]])
            kb.op(ACT, lambda e, i=i, bank=bank: e.activation(out=sq[:, i, 0:N], in_=bank[:, 0:N], func=AF.Square),
                  reads=[br], writes=[sqr[i]])
            kb.mm([lambda e, c=c, i=i: e.matmul(sbk[:, 0:N], ones_b[:, :], sq[:, i, 0:N], start=(c == 0), stop=(c == 1))],
                  reads=[sqr[i], ones_r], writes=[sbr])
        kb.op(ACT, lambda e: e.activation(out=srt[:, 0:N], in_=sbk[:, 0:N], func=AF.Ln, scale=1.0 / 256, bias=eps_ap),
              reads=[sbr, eps_r], writes=[srt_r])
        kb.op(ACT, lambda e: e.activation(out=rstd[:, 0:N], in_=srt[:, 0:N], func=AF.Exp, scale=-0.5), reads=[srt_r], writes=[rstd_r])
        for c in range(2):
            kb.op(DVE, lambda e, c=c: e.scalar_tensor_tensor(out=qn[:, c, 0:N], in0=arF[:, 4 + c, 0:N], scalar=vcol(l, V_GQA + c),
                                                             in1=rstd[:, 0:N], op0=ALU.mult, op1=ALU.mult),
                  reads=[yr[4 + c], rstd_r, vecs_r], writes=[qn_r[c]])
        bank, br = next_bank()
        mm_k8(bank, br, N, s, 256, 128, hb, hr)
        i = sq_next()
        kb.op(ACT, lambda e, bank=bank: e.activation(out=arF[:, 6, 0:N], in_=bank[:, 0:N], func=AF.Copy), reads=[br], writes=[yr[6]])
        kb.op(ACT, lambda e, bank=bank, i=i: e.activation(out=sq[:, i, 0:N], in_=bank[:, 0:N], func=AF.Square),
              reads=[br], writes=[sqr[i]])
        kb.mm([lambda e, i=i: e.matmul(sbk[:, 0:N], ones_b[:, :], sq[:, i, 0:N], start=True, stop=True)],
              reads=[sqr[i], ones_r], writes=[sbr])
        kb.op(ACT, lambda e: e.activation(out=srt[:, 0:N], in_=sbk[:, 0:N], func=AF.Ln, scale=1.0 / 128, bias=eps_ap),
              reads=[sbr, eps_r], writes=[srt_r])
        kb.op(ACT, lambda e: e.activation(out=rstd[:, 0:N], in_=srt[:, 0:N], func=AF.Exp, scale=-0.5), reads=[srt_r], writes=[rstd_r])
        kb.op(DVE, lambda e: e.scalar_tensor_tensor(out=arF[:, 7, 0:N], in0=arF[:, 6, 0:N], scalar=vcol(l, V_GKVA),
                                                    in1=rstd[:, 0:N], op0=ALU.mult, op1=ALU.mult),
              reads=[yr[6], rstd_r, vecs_r], writes=[yr[7]])
        kb.op(POOL, lambda e: e.tensor_copy(out=ckvb[:, 0:N], in_=arF[:, 7, 0:N]), reads=[yr[7]], writes=[ckvb_r])
        kb.dma(SP, cx.ckv_out(l), arF[:, 7, 0:N], reads=[yr[7]])
        bA, bAr = next_bank()
        mm_k8(bA, bAr, N, s, 320, 96, hb, hr)
        bB, bBr = next_bank()
        mm_k8(bB, bBr, N, s, 352, 96, hb, hr)
        kb.op(DVE, lambda e: e.tensor_tensor(out=rt1[64:96, 0, 0:N], in0=bA[64:96, 0:N], in1=cosF[64:96, 0:N], op=ALU.mult),
              reads=[bAr, cs_r], writes=[rt1_r[0]])
        kb.op(DVE, lambda e: e.tensor_tensor(out=rt2[64:96, 0, 0:N], in0=bB[64:96, 0:N], in1=sinF[64:96, 0:N], op=ALU.mult),
              reads=[bBr, cs_r], writes=[rt2_r[0]])
        kb.op(POOL, lambda e: e.tensor_tensor(out=kpef[64:96, 0:N], in0=rt1[64:96, 0, 0:N], in1=rt2[64:96, 0, 0:N], op=ALU.add),
              reads=[rt1_r[0], rt2_r[0]], writes=[kpef_r])
        kb.dma(SP, cx.kpe_out(l), kpef[64:96, 0:N], reads=[kpef_r])
        if cx.prompt:
            for h in range(8):
                q = ACT if h % 2 == 0 else POOL
                if q is ACT:
                    kb.op(q, lambda e, h=h: e.activation(out=Kcur[64:96, h, 0:N], in_=kpef[64:96, 0:N], func=AF.Copy),
                          reads=[kpef_r], writes=[k_r[h]])
                else:
                    kb.op(q, lambda e, h=h: e.tensor_copy(out=Kcur[64:96, h, 0:N], in_=kpef[64:96, 0:N]),
                          reads=[kpef_r], writes=[k_r[h]])
        else:
            kb.op(POOL, lambda e: e.tensor_copy(out=kpeb[64:96, 0:N], in_=kpef[64:96, 0:N]), reads=[kpef_r], writes=[kpeb_r])
        kb.dma(SP, wuq_sb[:, :, :], wuq_b[l], reads=[wsm_r[l][0]], writes=[wuq_r])
        for h in range(8):
            b1, b1r = next_bank()
            kb.mm([lambda e, kc=kc, h=h, b1=b1: e.matmul(b1[0:96, 0:N], wuq_sb[:, kc, h * 128:h * 128 + 96], qn[:, kc, 0:N],
                                                          start=(kc == 0), stop=(kc == 1)) for kc in range(2)],
                  reads=[wuq_r, qn_r[0], qn_r[1]], writes=[b1r])
            b2, b2r = next_bank()
            kb.mm([lambda e, kc=kc, h=h, b2=b2: e.matmul(b2[0:96, 0:N], wuq_sb[:, kc, h * 128 + 32:h * 128 + 128], qn[:, kc, 0:N],
                                                          start=(kc == 0), stop=(kc == 1)) for kc in range(2)],
                  reads=[wuq_r, qn_r[0], qn_r[1]], writes=[b2r])
            r = h % 2
            kb.op(ACT, lambda e, h=h, b1=b1: e.activation(out=Qb[0:64, h, 0:N], in_=b1[0:64, 0:N], func=AF.Copy),
                  reads=[b1r], writes=[q_r[h]])
            kb.op(DVE, lambda e, b1=b1, r=r: e.tensor_tensor(out=rt1[64:96, r, 0:N], in0=b1[64:96, 0:N], in1=cosF[64:96, 0:N],
                                                             op=ALU.mult), reads=[b1r, cs_r], writes=[rt1_r[r]])
            kb.op(DVE, lambda e, b2=b2, r=r: e.tensor_tensor(out=rt2[64:96, r, 0:N], in0=b2[64:96, 0:N], in1=sinF[64:96, 0:N],
                                                             op=ALU.mult), reads=[b2r, cs_r], writes=[rt2_r[r]])
            if cx.prompt:
                kb.op(POOL, lambda e, h=h, r=r: e.tensor_tensor(out=Qb[64:96, h, 0:N], in0=rt1[64:96, r, 0:N],
                                                                in1=rt2[64:96, r, 0:N], op=ALU.add),
                      reads=[rt1_r[r], rt2_r[r]], writes=[q_r[h]])
            else:
                qv = qpe[64:96, :].rearrange("p (s h t) -> p s h t", s=NS, h=8)[:, :, h, :]
                kb.op(POOL, lambda e, qv=qv, r=r: e.tensor_tensor(out=qv, in0=v3(rt1[64:96, r, 0:N], cx, T),
                                                                  in1=v3(rt2[64:96, r, 0:N], cx, T), op=ALU.add),
                      reads=[rt1_r[r], rt2_r[r]], writes=[qpe_r])
        if cx.prompt:
            attn_prompt(cx, l)
        else:
            attn_sample(cx, l)
        if DEBUG and (not cx.prompt) and l == 0:
            kb.dma(GQ, dbg[:, :, :], mixb[:, :, 0:N], reads=mr)
        for g in range(2):
            s = wa_load(wo_b[l, g], wo_r[l][g])
            for mi in range(4):
                bank, br = next_bank()
                mm_k8(bank, br, N, s, mi * 128, 128, mixb, mr, split=(g == 0 and mi == 0))
                post_consumer(cx, g * 4 + mi, bank, br)
        post_finish(cx, l, V_GMPOST)

    def attn_prompt(cx, l):
        j = cx.j
        kb.dma(SP, wuk_sb[:, :], wuk_b[l], reads=[wsm_r[l][1]], writes=[wuk_r])
        kb.dma(SP, wuv_sb[:, :], wuv_b[l], reads=[wsm_r[l][3]], writes=[wuv_r])
        for h in range(8):
            bank, br = next_bank()
            kb.mm([lambda e, h=h, bank=bank: e.matmul(bank[0:64, 0:512], wuk_sb[:, h * 64:(h + 1) * 64], ckvb[:, 0:512],
                                                      start=True, stop=True)], reads=[wuk_r, ckvb_r], writes=[br])
            if h % 2 == 0:
                kb.op(ACT, lambda e, h=h, bank=bank: e.activation(out=Kcur[0:64, h, :], in_=bank[0:64, 0:512], func=AF.Copy),
                      reads=[br], writes=[k_r[h]])
            else:
                kb.op(DVE, lambda e, h=h, bank=bank: e.tensor_copy(out=Kcur[0:64, h, :], in_=bank[0:64, 0:512]),
                      reads=[br], writes=[k_r[h]])
        for kbi in range(4):
            bank, br = next_bank()
            kb.mm([lambda e, kbi=kbi, bank=bank: e.matmul(bank[:, 0:512], ckvb[:, kbi * 128:(kbi + 1) * 128], wuv_sb[:, :],
                                                          start=True, stop=True)], reads=[wuv_r, ckvb_r], writes=[br])
            src = bank[:, 0:512].rearrange("p (a h v) -> p a h v", a=2, h=4)
            kb.op(DVE, lambda e, kbi=kbi, src=src: e.tensor_copy(out=Vcur[:, 0, kbi, :, 0:64], in_=src[:, 0]),
                  reads=[br], writes=[v_r[kbi]])
            kb.op(ACT, lambda e, kbi=kbi, src=src: e.activation(out=Vcur[:, 1, kbi, :, 0:64], in_=src[:, 1], func=AF.Copy),
                  reads=[br], writes=[v_r[kbi]])
        if j < NT - 1:
            for hp in range(2):
                kb.dma(SP, Ksc[l, hp, j], Kcur[0:96, hp * 4:(hp + 1) * 4, :], reads=k_r[hp * 4:(hp + 1) * 4],
                       writes=[ksc_r[l][hp][j]])
                kb.dma(SP, Vsc[l, hp, j], Vcur[:, hp], reads=v_r, writes=[vsc_r[l][hp][j]])
        slot_i = [0]
        for hp in range(2):
            steps = []
            chunk_dma = []
            for jj in range(j):
                si = slot_i[0] % 2
                slot_i[0] += 1
                ks, vs = 2 * si, 2 * si + 1
                ksl = arB[0:96, 4 * ks:4 * ks + 4, :]
                vsl = arB[:, 4 * vs:4 * vs + 4, :].rearrange("p a (h v) -> p a h v", h=4)
                kres = ar[4 * ks:4 * ks + 4]
                vres = ar[4 * vs:4 * vs + 4]
                chunk_dma.append((ksl, vsl, kres, vres))
                for kbi in range(4):
                    for hh in range(4):
                        steps.append((arB[0:96, 4 * ks + hh, kbi * 128:(kbi + 1) * 128], kres,
                                      vsl[:, kbi, hh, :], vres, 0, False, hh))
            for kbi in range(4):
                for hh in range(4):
                    h = hp * 4 + hh
                    steps.append((Kcur[0:96, h, kbi * 128:(kbi + 1) * 128], [k_r[h]],
                                  Vcur[:, hp, kbi, hh, :], [v_r[kbi]], kbi * 128, True, hh))
            nsteps = len(steps)
            first = [True] * 4
            lastidx = [max(i for i in range(nsteps) if steps[i][6] == hh) for hh in range(4)]
            pend = []

            def do_pv(idx, pi, c0):
                kap, kres, vap, vres, _, _, hh = steps[idx]
                st = first[hh]
                first[hh] = False
                kb.mm([lambda e: e.matmul(banks[hh][:, c0:512], vap, arB[:, 16 + pi, c0:512], start=st, stop=(idx == lastidx[hh]))],
                      reads=[ar[16 + pi]] + list(vres), writes=[bank_r[hh]])

            def emit_dma(jj, hp=hp, chunk_dma=chunk_dma):
                ksl, vsl, kres, vres = chunk_dma[jj]
                kb.dma(SP, ksl, Ksc[l, hp, jj], reads=[ksc_r[l][hp][jj]], writes=kres)
                kb.dma(SP, vsl, Vsc[l, hp, jj], reads=[vsc_r[l][hp][jj]], writes=vres)

            for idx in range(nsteps):
                if idx == 0:
                    for jj in range(min(2, j)):
                        emit_dma(jj)
                elif idx % 16 == 3 and 2 <= idx // 16 + 1 < j:
                    emit_dma(idx // 16 + 1)
                kap, kres, vap, vres, c0, diag, hh = steps[idx]
                h = hp * 4 + hh
                sbi = 4 + idx % 3
                pi = idx % 4
                kb.mm([lambda e, kap=kap, h=h, sbi=sbi, c0=c0: e.matmul(banks[sbi][:, c0:512], kap, Qb[0:96, h, c0:512],
                                                                         start=True, stop=True)],
                      reads=list(kres) + [q_r[h]], writes=[bank_r[sbi]])
                kb.op(ACT, lambda e, sbi=sbi, pi=pi, c0=c0: e.activation(out=arB[:, 16 + pi, c0:512], in_=banks[sbi][:, c0:512],
                                                                       func=AF.Exp, scale=SCALE),
                      reads=[bank_r[sbi]], writes=[ar[16 + pi]])
                if diag:
                    kb.op(POOL, lambda e, pi=pi, c0=c0: e.memset(arB[64:128, 16 + pi, c0:c0 + 64], 0.0), writes=[ar[16 + pi]])
                pend.append((idx, pi, c0))
                if len(pend) > 2:
                    do_pv(*pend.pop(0))
            while pend:
                do_pv(*pend.pop(0))
            for hh in range(4):
                h = hp * 4 + hh
                r = hh % 2
                kb.op(ACT, lambda e, hh=hh, r=r: e.activation(out=rec[0:64, r, :], in_=banks[hh][64:128, 0:512], func=AF.Ln),
                      reads=[bank_r[hh]], writes=[rec_r[r]])
                kb.op(ACT, lambda e, r=r: e.activation(out=rec[0:64, r, :], in_=rec[0:64, r, :], func=AF.Exp, scale=-1.0),
                      reads=[rec_r[r]], writes=[rec_r[r]])
                p0 = (h % 2) * 64
                kb.op(DVE, lambda e, hh=hh, r=r, h=h, p0=p0: e.tensor_tensor(out=mixb[p0:p0 + 64, 4 + h // 2, :],
                                                                              in0=banks[hh][0:64, 0:512], in1=rec[0:64, r, :],
                                                                              op=ALU.mult),
                      reads=[bank_r[hh], rec_r[r]], writes=[mr[4 + h // 2]])

    def attn_sample(cx, l):
        N = cx.N
        kb.dma(SP, wukT_sb[0:64, :], wukT_b[l], reads=[wsm_r[l][2]], writes=[wukT_r])
        kb.dma(SP, wuv_sb[:, :], wuv_b[l], reads=[wsm_r[l][3]], writes=[wuv_r])
        for h in range(8):
            bank, br = next_bank()
            kb.mm([lambda e, h=h, bank=bank: e.matmul(bank[:, 0:N], wukT_sb[0:64, h * 128:(h + 1) * 128], Qb[0:64, h, 0:N],
                                                      start=True, stop=True)], reads=[wukT_r, q_r[h]], writes=[br])
            qv = qlat[:, :].rearrange("p (s h t) -> p s h t", s=NS, h=8)[:, :, h, :]
            kb.op(ACT if h % 2 == 0 else DVE,
                  (lambda e, qv=qv, bank=bank: e.activation(out=qv, in_=v3(bank[:, 0:N], cx, TS), func=AF.Copy)) if h % 2 == 0 else
                  (lambda e, qv=qv, bank=bank: e.tensor_copy(out=qv, in_=v3(bank[:, 0:N], cx, TS))),
                  reads=[br], writes=[qlat_r])
        bank, br = banks[3], bank_r[3]
        for s_ in range(NS):
            kb.mm([lambda e, s_=s_, bank=bank: e.transpose(bank[0:TS, s_ * 128:(s_ + 1) * 128], arF[:, 7, s_ * TS:(s_ + 1) * TS],
                                                            ident[:, :])], reads=[yr[7], ident_r], writes=[br])
        kb.op(DVE, lambda e, bank=bank: e.tensor_copy(out=cntok[0:TS, :, :], in_=bank[0:TS, 0:NS * 128].rearrange("p (s r) -> p s r", s=NS)),
              reads=[br], writes=[cntok_r])
        W = 8 * TS
        ybank, ybr = banks[0], bank_r[0]
        chunk_i = [0]
        for s_ in range(NS):
            obank, obr = banks[1 + s_ % 2], bank_r[1 + s_ % 2]
            nblk = NPB + 1
            blk = 0
            pend = []
            qlv = qlat[:, s_ * W:(s_ + 1) * W]
            qpv = qpe[64:96, s_ * W:(s_ + 1) * W]

            def do_pv(cap, cres, kk, pi, b, obank=obank, obr=obr, nblk=nblk):
                kb.mm([lambda e: e.matmul(obank[:, 0:W], cap, pts[0:kk, pi, :], start=(b == 0), stop=(b == nblk - 1)),
                       lambda e: e.matmul(obank[:, W:2 * W], ones_b[0:kk, :], pts[0:kk, pi, :], start=False, stop=(b == nblk - 1),
                                          skip_group_check=True)],
                      reads=[pts_r[pi], ones_r] + list(cres), writes=[obr])

            def do_sc(ctap, ktap, cres, kk, b, qlv=qlv, qpv=qpv):
                sbi = 4 + b % 3
                pi = b % 2
                kb.mm([lambda e: e.matmul(banks[sbi][0:kk, 0:W], ctap, qlv, start=True, stop=False),
                       lambda e: e.matmul(banks[sbi][0:kk, 0:W], ktap, qpv, start=False, stop=True)],
                      reads=list(cres) + [qlat_r, qpe_r], writes=[bank_r[sbi]])
                kb.op(ACT, lambda e: e.activation(out=pts[0:kk, pi, :], in_=banks[sbi][0:kk, 0:W], func=AF.Exp, scale=SCALE),
                      reads=[bank_r[sbi]], writes=[pts_r[pi]])
                return pi

            for ch in range(PAST // 1024):
                ci = chunk_i[0] % 2
                chunk_i[0] += 1
                sa, sb_ = 2 * ci, 2 * ci + 1
                resA = ar[4 * sa:4 * sa + 4]
                resB = ar[4 * sb_:4 * sb_ + 4]
                cTs = arB[:, 4 * sa:4 * sa + 2, :]
                ccs = arB[:, 4 * sa + 2:4 * sa + 4, :]
                kTs = arB[64:96, 4 * sb_:4 * sb_ + 2, :]
                kb.dma(GQ, cTs, cT_in[l, s_, :, ch * 1024:(ch + 1) * 1024].rearrange("p (a n) -> p a n", a=2), writes=resA)
                kb.dma(GQ, ccs.rearrange("p a (b r) -> p (a b) r", r=128),
                       cc_in[l, s_, ch * 1024:(ch + 1) * 1024, :].rearrange("(b p) r -> p b r", p=128), writes=resA)
                kb.dma(GQ, kTs, kT_in[l, s_, :, ch * 1024:(ch + 1) * 1024].rearrange("p (a n) -> p a n", a=2), writes=resB)
                for bi in range(8):
                    ctap = arB[:, 4 * sa + bi // 4, (bi % 4) * 128:(bi % 4) * 128 + 128]
                    ktap = arB[64:96, 4 * sb_ + bi // 4, (bi % 4) * 128:(bi % 4) * 128 + 128]
                    cap = arB[:, 4 * sa + 2 + bi // 4, (bi % 4) * 128:(bi % 4) * 128 + 128]
                    pi = do_sc(ctap, ktap, list(resA) + list(resB), 128, blk)
                    pend.append((cap, list(resA), 128, pi, blk))
                    blk += 1
                    if len(pend) > 1:
                        do_pv(*pend.pop(0))
            pi = do_sc(ckvb[:, s_ * TS:(s_ + 1) * TS], kpeb[64:96, s_ * TS:(s_ + 1) * TS], [ckvb_r, kpeb_r], TS, blk)
            pend.append((cntok[0:TS, s_, :], [cntok_r], TS, pi, blk))
            while pend:
                do_pv(*pend.pop(0))
            oi = s_ % 2
            kb.op(DVE, lambda e, obank=obank: e.reciprocal(out=recs[:, :], in_=obank[:, W:2 * W]), reads=[obr], writes=[recs_r])
            kb.op(DVE, lambda e, obank=obank, oi=oi: e.tensor_tensor(out=onb[:, oi, :], in0=obank[:, 0:W], in1=recs[:, :], op=ALU.mult),
                  reads=[obr, recs_r], writes=[onb_r[oi]])
            for h in range(8):
                p0 = (h % 2) * 64
                c0 = (h // 2) * N + s_ * TS
                kb.mm([lambda e, h=h, p0=p0, c0=c0, oi=oi: e.matmul(ybank[p0:p0 + 64, c0:c0 + TS], wuv_sb[:, h * 64:(h + 1) * 64],
                                                                    onb[:, oi, h * TS:(h + 1) * TS], start=True, stop=True)],
                      reads=[wuv_r, onb_r[oi]], writes=[ybr])
        kb.op(DVE, lambda e: e.tensor_copy(out=mixb[:, 4:8, 0:N], in_=ybank[:, 0:4 * N].rearrange("p (c n) -> p c n", c=4)),
              reads=[ybr], writes=mr[4:8])

    def ffn(cx, l):
        N, S, T = cx.N, cx.S, cx.T
        W2 = T + 2
        rms_pre(cx, l, V_GFPRE)
        ffn_tail = []
        for g in range(11):
            s = wa_load(wup_b[l, g], wup_r[l][g])
            for pi in range(2):
                pair = g * 2 + pi
                r = pair % 2
                bks = []
                for xx in range(2):
                    bank, br = next_bank()
                    mm_k8(bank, br, N, s, (pi * 2 + xx) * 128, 128, hb, hr, split=(pair == 0 and xx == 0))
                    bks.append((bank, br))
                ubv = ub[:, r, :, 0:S * W2].rearrange("p a (s t) -> p a s t", s=S)
                for xx in range(2):
                    bank, br = bks[xx]
                    kb.op(ACT, lambda e, xx=xx, bank=bank, ubv=ubv: e.activation(out=ubv[:, xx, :, 2:W2], in_=v3(bank[:, 0:N], cx, T),
                                                                                  func=AF.Copy), reads=[br], writes=[ub_r[r]])
                kb.op(POOL, lambda e, ubv=ubv, pair=pair: e.tensor_copy(out=ubv[:, :, :, 0:2], in_=cx.fst[:, l, pair, :, :, :]),
                      reads=[cx.fst_r[l][pair]], writes=[ub_r[r]])
                kb.op(POOL, lambda e, ubv=ubv, pair=pair: e.tensor_copy(out=cx.fst[:, l, pair, :, :, :], in_=ubv[:, :, :, T:W2]),
                      reads=[ub_r[r]], writes=[cx.fst_r[l][pair]])
                for xx in range(2):
                    q = DVE
                    chn = pair + xx * NPAIR
                    acc = v3(facc[:, r, xx, 0:N], cx, T)
                    fr = facc_r[r][xx]
                    kb.op(ACT, lambda e, xx=xx, acc=acc, ubv=ubv, chn=chn: e.activation(
                        out=acc, in_=ubv[:, xx, :, 0:T], func=AF.Identity, scale=vcol(l, V_WFFN + chn), bias=vcol(l, V_BFFN + chn)),
                        reads=[ub_r[r], vecs_r], writes=[fr])
                    kb.op(q, lambda e, xx=xx, acc=acc, ubv=ubv, chn=chn: e.scalar_tensor_tensor(
                        out=acc, in0=ubv[:, xx, :, 1:T + 1], scalar=vcol(l, V_WFFN + 44 + chn), in1=acc, op0=ALU.mult, op1=ALU.add),
                        reads=[ub_r[r], vecs_r, fr], writes=[fr])
                    kb.op(q, lambda e, xx=xx, acc=acc, ubv=ubv, chn=chn: e.scalar_tensor_tensor(
                        out=acc, in0=ubv[:, xx, :, 2:W2], scalar=vcol(l, V_WFFN + 88 + chn), in1=acc, op0=ALU.mult, op1=ALU.add),
                        reads=[ub_r[r], vecs_r, fr], writes=[fr])
                def tail(r=r, pair=pair):
                    kb.op(ACT, lambda e: e.activation(out=facc[:, r, 0, 0:N], in_=facc[:, r, 0, 0:N], func=AF.Silu),
                          reads=[facc_r[r][0]], writes=[facc_r[r][0]])
                    kb.op(POOL, lambda e: e.tensor_tensor(out=arB[:, pair, 0:N], in0=facc[:, r, 0, 0:N],
                                                          in1=facc[:, r, 1, 0:N], op=ALU.mult),
                          reads=[facc_r[r][0], facc_r[r][1]], writes=[ar[pair]])
                while ffn_tail:
                    ffn_tail.pop(0)()
                ffn_tail.append(tail)
        while ffn_tail:
            ffn_tail.pop(0)()
        for m in range(8):
            s = rot["wb"]
            rot["wb"] = (s + 1) % 3
            kb.dma(SP, wb[:, s], wdn_b[l, m], reads=[wdn_r[l][m]], writes=[wb_r[s]])
            bank, br = next_bank()
            dfns = [lambda e, kc=kc, s=s, bank=bank: e.matmul(bank[:, 0:N], wb[:, s, kc, :], arB[:, kc, 0:N],
                                                              start=(kc == 0), stop=(kc == NPAIR - 1)) for kc in range(NPAIR)]
            if m == 0:
                for a_, b_ in ((0, 12), (12, 16), (16, 18), (18, 20), (20, 21), (21, 22)):
                    kb.mm(dfns[a_:b_], reads=[wb_r[s]] + ar[a_:b_], writes=[br])
            else:
                kb.mm(dfns, reads=[wb_r[s]] + ar, writes=[br])
            post_consumer(cx, m, bank, br)
        post_finish(cx, l, V_GFPOST)

    def run_tile(cx):
        N = cx.N
        kb.dma(SP, x[:, :, 0:N], cx.x_in, writes=xr)
        kb.dma(SP, cosF[64:96, 0:N], cx.cos_in, writes=[cs_r])
        kb.dma(SP, sinF[64:96, 0:N], cx.sin_in, writes=[cs_r])
        for l in range(L):
            if not cx.prompt and l + 1 < L:
                emit_casts(l + 1)
            mixer(cx, l)
            ffn(cx, l)
        kb.dma(SP, cx.y_out, x[:, :, 0:N], reads=xr)

    cx = Ctx()
    cx.prompt = False
    cx.S, cx.T, cx.N, cx.j = NS, TS, NSX, 0
    cx.cst, cx.cst_r, cx.fst, cx.fst_r = cstS, cstS_r, fstS, fstS_r
    cx.x_in = xsT.rearrange("(c p) n -> p c n", p=128)
    cx.y_out = ysT.rearrange("(c p) n -> p c n", p=128)
    cx.cos_in, cx.sin_in = cosS[:, :], sinS[:, :]
    cx.ckv_out = lambda l: sckvT[l]
    cx.kpe_out = lambda l: skpeT[l]
    run_tile(cx)
    for j in range(NT):
        cx = Ctx()
        cx.prompt = True
        cx.S, cx.T, cx.N, cx.j = 1, 512, 512, j
        cx.cst, cx.cst_r, cx.fst, cx.fst_r = cstP, cstP_r, fstP, fstP_r
        cs = slice(j * 512, (j + 1) * 512)
        cx.x_in = xT.rearrange("(c p) n -> p c n", p=128)[:, :, cs]
        cx.y_out = yT.rearrange("(c p) n -> p c n", p=128)[:, :, cs]
        cx.cos_in, cx.sin_in = cosP[:, cs], sinP[:, cs]
        cx.ckv_out = lambda l, cs=cs: pckvT[l, :, cs]
        cx.kpe_out = lambda l, cs=cs: pkpeT[l, :, cs]
        run_tile(cx)
    allc = [r for rr in cstP_r for r in rr]
    kb.dma(SP, pconv.rearrange("p (l c s t) -> p l c s t", l=L, c=4, s=1), cstP[:, :, :, :, :], reads=allc)
    kb.dma(SP, pffn.rearrange("p (l c a s t) -> p l c a s t", l=L, c=NPAIR, a=2, s=1), fstP[:, :, :, :, :, :],
           reads=[r for rr in fstP_r for r in rr])
    kb.dma(SP, sconv.rearrange("p (l c s t) -> p l c s t", l=L, c=4, s=NS), cstS[:, :, :, :, :],
           reads=[r for rr in cstS_r for r in rr])
    kb.dma(SP, sffn.rearrange("p (l c a s t) -> p l c a s t", l=L, c=NPAIR, a=2, s=NS), fstS[:, :, :, :, :, :],
           reads=[r for rr in fstS_r for r in rr])
    kb.finish()

    with nc.Block() as block:
        @block.sync
        def _(e):
            for th in SP.prog:
                th(e)

        @block.tensor
        def _(e):
            for th in PE.prog:
                th(e)

        @block.scalar
        def _(e):
            for th in ACT.prog:
                th(e)

        @block.vector
        def _(e):
            for th in DVE.prog:
                th(e)

        @block.gpsimd
        def _(e):
            for th in POOL.prog:
                th(e)
    es.close()
    return nc


def _fm(v, nch):
    Lh = v.shape[0]
    return np.ascontiguousarray(v.reshape(Lh, nch, 128).transpose(2, 0, 1))


def _rope_tables(pos):
    inv = (1.0 / (np.float32(10000.0) ** (np.arange(0, 32, 2, dtype=np.float32) / np.float32(32)))).astype(np.float32)
    ang = pos.astype(np.float32)[:, None] * inv[None, :]
    c = np.cos(ang).astype(np.float32).T
    s = np.sin(ang).astype(np.float32).T
    return np.ascontiguousarray(np.concatenate([c, c], 0)), np.ascontiguousarray(np.concatenate([-s, s], 0))


def kernel(x_prompt, x_sample, cache_ckv, cache_kpe, state_conv, state_ffn,
           w_in, w_conv, g_qa, w_uq, g_kva, w_uk, w_uv, w_o, g_mix_pre, g_mix_post,
           w_up, w_ffn_conv, b_ffn_conv, w_down, g_ffn_pre, g_ffn_post):
    f = lambda a: np.asarray(a, dtype=np.float32)
    x_prompt, x_sample, cache_ckv, cache_kpe, state_conv, state_ffn = map(f, (x_prompt, x_sample, cache_ckv, cache_kpe, state_conv, state_ffn))
    w_in, w_conv, g_qa, w_uq, g_kva, w_uk, w_uv, w_o = map(f, (w_in, w_conv, g_qa, w_uq, g_kva, w_uk, w_uv, w_o))
    g_mix_pre, g_mix_post, w_up, w_ffn_conv, b_ffn_conv, w_down, g_ffn_pre, g_ffn_post = map(
        f, (g_mix_pre, g_mix_post, w_up, w_ffn_conv, b_ffn_conv, w_down, g_ffn_pre, g_ffn_post))
    BP, SEQ, _ = x_prompt.shape
    BS, TS, _ = x_sample.shape
    L = w_in.shape[0]
    PAST = cache_ckv.shape[2]
    assert BP == N_CORES and BS % N_CORES == 0
    NS = BS // N_CORES
    NSX = NS * TS

    xv, gb, gc, qa, kva, kpe = np.split(w_in, [512, 1024, 1536, 1792, 1920], axis=2)
    kpes = np.concatenate([kpe[:, :, 16:32], kpe[:, :, 0:16]], axis=2)
    pad = np.zeros((L, D, 64), np.float32)
    w_in_x = np.ascontiguousarray(np.concatenate([xv, gc, gb, qa, kva, kpe, kpes, pad], axis=2))
    wq = w_uq.reshape(L, 256, 8, 96)
    w_uq_x = np.ascontiguousarray(np.concatenate([wq, wq[..., 80:96], wq[..., 64:80]], axis=3).reshape(L, 256, 1024))
    w_uk2 = np.ascontiguousarray(w_uk.reshape(L, 128, 512))
    w_ukT = np.ascontiguousarray(w_uk.transpose(0, 3, 2, 1).reshape(L, 64, 1024))
    w_uv2 = np.ascontiguousarray(w_uv.reshape(L, 128, 512))
    upa = w_up[:, :, :DFF].reshape(L, D, NPAIR, 1, 128)
    upb = w_up[:, :, DFF:].reshape(L, D, NPAIR, 1, 128)
    w_up_x = np.ascontiguousarray(np.concatenate([upa, upb], axis=3).reshape(L, D, 2 * DFF))
    vecs = np.zeros((128, L, VL), np.float32)
    vecs[:, :, V_GMPRE:V_GMPRE + 8] = _fm(g_mix_pre, 8)
    vecs[:, :, V_GMPOST:V_GMPOST + 8] = _fm(g_mix_post, 8)
    vecs[:, :, V_GFPRE:V_GFPRE + 8] = _fm(g_ffn_pre, 8)
    vecs[:, :, V_GFPOST:V_GFPOST + 8] = _fm(g_ffn_post, 8)
    vecs[:, :, V_GQA:V_GQA + 2] = _fm(g_qa, 2)
    vecs[:, :, V_GKVA:V_GKVA + 1] = _fm(g_kva, 1)
    for k in range(3):
        vecs[:, :, V_WCONV + 4 * k:V_WCONV + 4 * k + 4] = _fm(w_conv[:, k, :], 4)
        vecs[:, :, V_WFFN + 44 * k:V_WFFN + 44 * k + 44] = _fm(w_ffn_conv[:, k, :], 44)
    vecs[:, :, V_BFFN:V_BFFN + 44] = _fm(b_ffn_conv, 44)
    vecs = np.ascontiguousarray(vecs.reshape(128, L * VL))
    cosP, sinP = _rope_tables(np.arange(SEQ))
    cS, sS = _rope_tables(PAST + np.arange(TS))
    cosS = np.ascontiguousarray(np.tile(cS, (1, NS)))
    sinS = np.ascontiguousarray(np.tile(sS, (1, NS)))
    ident = np.eye(128, dtype=np.float32)

    nc = build(L, SEQ, PAST, NS, TS)
    in_maps = []
    for c in range(N_CORES):
        sl = slice(c * NS, (c + 1) * NS)
        cc = cache_ckv[:, sl]
        stc = state_conv[:, sl]
        stf = state_ffn[:, sl]
        cst = stc.reshape(L, NS, 2, 4, 128).transpose(4, 0, 3, 1, 2)
        fst = stf.reshape(L, NS, 2, 2, NPAIR, 128).transpose(5, 0, 4, 3, 1, 2)
        in_maps.append({
            "xT": np.ascontiguousarray(x_prompt[c].T),
            "xsT": np.ascontiguousarray(x_sample[sl].reshape(NSX, D).T),
            "cT": np.ascontiguousarray(cc.transpose(0, 1, 3, 2)),
            "kT": np.ascontiguousarray(cache_kpe[:, sl].transpose(0, 1, 3, 2)),
            "cc": np.ascontiguousarray(cc),
            "cst": np.ascontiguousarray(cst).reshape(128, -1),
            "fst": np.ascontiguousarray(fst).reshape(128, -1),
            "w_in": w_in_x, "w_uq": w_uq_x, "w_uk": w_uk2, "w_ukT": w_ukT, "w_uv": w_uv2, "w_o": w_o,
            "w_up": w_up_x, "w_dn": w_down, "vecs": vecs,
            "cosP": cosP, "sinP": sinP, "cosS": cosS, "sinS": sinS, "ident": ident,
        })
    res = run_bass_kernel_spmd(nc, in_maps, core_ids=list(range(N_CORES)))
    R = res.results
    if DEBUG:
        kernel.dbg = [R[c]["dbg"] for c in range(N_CORES)]
    y_prompt = np.stack([R[c]["yT"].T for c in range(N_CORES)])
    y_sample = np.concatenate([R[c]["ysT"].T.reshape(NS, TS, D) for c in range(N_CORES)])
    p_ckv = np.stack([R[c]["pckvT"].transpose(0, 2, 1) for c in range(N_CORES)], axis=1)
    p_kpe = np.stack([R[c]["pkpeT"].transpose(0, 2, 1) for c in range(N_CORES)], axis=1)
    p_conv = np.stack([R[c]["pconv"].reshape(128, L, 4, 2).transpose(1, 3, 2, 0).reshape(L, 2, 512) for c in range(N_CORES)], axis=1)
    p_ffn = np.stack([R[c]["pffn"].reshape(128, L, NPAIR, 2, 2).transpose(1, 4, 3, 2, 0).reshape(L, 2, 2 * DFF)
                      for c in range(N_CORES)], axis=1)
    s_ckv = np.concatenate([R[c]["sckvT"].transpose(0, 2, 1).reshape(L, NS, TS, 128) for c in range(N_CORES)], axis=1)
    s_kpe = np.concatenate([R[c]["skpeT"].transpose(0, 2, 1).reshape(L, NS, TS, 32) for c in range(N_CORES)], axis=1)
    s_conv = np.concatenate([R[c]["sconv"].reshape(128, L, 4, NS, 2).transpose(1, 3, 4, 2, 0).reshape(L, NS, 2, 512)
                             for c in range(N_CORES)], axis=1)
    s_ffn = np.concatenate([R[c]["sffn"].reshape(128, L, NPAIR, 2, NS, 2).transpose(1, 4, 5, 3, 2, 0).reshape(L, NS, 2, 2 * DFF)
                            for c in range(N_CORES)], axis=1)
    outs = (y_prompt, y_sample, p_ckv, p_kpe, p_conv, p_ffn, s_ckv, s_kpe, s_conv, s_ffn)
    return tuple(np.ascontiguousarray(o, dtype=np.float32) for o in outs)
```

```python
import math
from contextlib import ExitStack
import numpy as np
import concourse.bass as bass
import concourse.mybir as mybir
from concourse.bass_utils import run_bass_kernel_spmd

F32 = mybir.dt.float32
BF16 = mybir.dt.bfloat16
AF = mybir.ActivationFunctionType
ALU = mybir.AluOpType

D = 1024
DC = 512
NH = 8
DFF = 2816
NPAIR = 22
EPS = 1e-6
SCALE = 1.0 / math.sqrt(96.0)
VL = 223
V_GMPRE, V_GMPOST, V_GFPRE, V_GFPOST, V_GQA, V_GKVA, V_WCONV, V_WFFN, V_BFFN = 0, 8, 16, 24, 32, 34, 35, 47, 179
SEM_LIMIT = 30000
NSLOT = 10
N_CORES = 8
DEBUG = False


class Res:
    __slots__ = ("lw", "rd")

    def __init__(self):
        self.lw = None
        self.rd = {}


class Q:
    def __init__(self, kb, kind):
        self.kb = kb
        self.kind = kind
        self.prog = []
        self.seen = {}
        self.own = set()
        self.sem = None
        self.val = 0
        self.slots = [[None, 0] for _ in range(NSLOT)]
        self.nd = 0

    def wait(self, ev):
        if ev is None:
            return
        s, v = ev
        if self.kind == "pe" and s in self.own:
            return
        if self.seen.get(s, 0) >= v:
            return
        self.seen[s] = v
        self.prog.append(lambda e, s=s, v=v: e.wait_ge(s, v))

    def bump(self, fn):
        if self.sem is None or self.val >= SEM_LIMIT:
            self.sem = self.kb.new_sem()
            self.own.add(self.sem)
            self.val = 0
        self.val += 1
        s = self.sem
        self.prog.append(lambda e, fn=fn, s=s: fn(e).then_inc(s, 1))
        return (s, self.val)


class KB:
    def __init__(self, nc, es):
        self.nc = nc
        self.es = es
        self.nsem = 0
        self.pe = Q(self, "pe")
        self.act = Q(self, "cmp")
        self.dve = Q(self, "cmp")
        self.pool = Q(self, "cmp")
        self.sp = Q(self, "dma")
        self.gq = self.pool

    def new_sem(self):
        self.nsem += 1
        return self.es.enter_context(self.nc.semaphore(f"s{self.nsem}"))

    def deps(self, q, reads, writes):
        for r in reads:
            q.wait(r.lw)
        for w in writes:
            q.wait(w.lw)
            for s, v in w.rd.items():
                q.wait((s, v))

    def done(self, ev, reads, writes):
        s, v = ev
        for r in reads:
            if r.rd.get(s, 0) < v:
                r.rd[s] = v
        for w in writes:
            w.lw = ev
            w.rd = {}

    def op(self, q, fn, reads=(), writes=()):
        self.deps(q, reads, writes)
        ev = q.bump(fn)
        self.done(ev, reads, writes)

    def mm(self, fns, reads=(), writes=()):
        q = self.pe
        self.deps(q, reads, writes)
        for f in fns[:-1]:
            q.prog.append(lambda e, f=f: f(e))
        ev = q.bump(fns[-1])
        self.done(ev, reads, writes)

    def dma(self, q, out, in_, reads=(), writes=()):
        self.deps(q, reads, writes)
        k = q.nd % NSLOT
        q.nd += 1
        slot = q.slots[k]
        if slot[0] is None or slot[1] >= SEM_LIMIT:
            slot[0] = self.new_sem()
            slot[1] = 0
        else:
            q.wait((slot[0], slot[1]))
        slot[1] += 16
        s = slot[0]
        q.prog.append(lambda e, out=out, in_=in_, s=s: e.dma_start(out=out, in_=in_).then_inc(s, 16))
        self.done((s, slot[1]), reads, writes)

    def finish(self):
        for q in (self.sp, self.pool):
            for s, v in q.slots:
                if s is not None:
                    q.wait((s, v))


def build(L, SEQ, PAST, NS, TS):
    NT = SEQ // 512
    NSX = NS * TS
    NPB = PAST // 128
    assert PAST % 1024 == 0 and SEQ % 512 == 0
    nc = bass.Bass("TRN2", target_bir_lowering=False)
    es = ExitStack()
    kb = KB(nc, es)
    PE, ACT, DVE, POOL, SP = kb.pe, kb.act, kb.dve, kb.pool, kb.sp

    def din(name, shape):
        return nc.dram_tensor(name, list(shape), F32, kind="ExternalInput").ap()

    def dout(name, shape):
        return nc.dram_tensor(name, list(shape), F32, kind="ExternalOutput").ap()

    def dscr(name, shape):
        return nc.dram_tensor(name, list(shape), BF16, kind="Internal").ap()

    def sb(name, shape, dt=F32):
        return es.enter_context(nc.sbuf_tensor("sb_" + name, list(shape), dt))

    xT = din("xT", [D, SEQ])
    xsT = din("xsT", [D, NSX])
    cT_in = din("cT", [L, NS, 128, PAST])
    kT_in = din("kT", [L, NS, 32, PAST])
    cc_in = din("cc", [L, NS, PAST, 128])
    cst_in = din("cst", [128, L * 4 * NS * 2])
    fst_in = din("fst", [128, L * NPAIR * 2 * NS * 2])
    w_in = din("w_in", [L, D, 2048])
    w_uq = din("w_uq", [L, 256, 1024])
    w_uk = din("w_uk", [L, 128, 512])
    w_ukT = din("w_ukT", [L, 64, 1024])
    w_uv = din("w_uv", [L, 128, 512])
    w_o = din("w_o", [L, D, D])
    w_up = din("w_up", [L, D, 2 * DFF])
    w_dn = din("w_dn", [L, DFF, D])
    vecs_in = din("vecs", [128, L * VL])
    cosP = din("cosP", [32, SEQ])
    sinP = din("sinP", [32, SEQ])
    cosS = din("cosS", [32, NSX])
    sinS = din("sinS", [32, NSX])
    ident_in = din("ident", [128, 128])

    yT = dout("yT", [D, SEQ])
    ysT = dout("ysT", [D, NSX])
    pckvT = dout("pckvT", [L, 128, SEQ])
    pkpeT = dout("pkpeT", [L, 32, SEQ])
    pconv = dout("pconv", [128, L * 4 * 2])
    pffn = dout("pffn", [128, L * NPAIR * 2 * 2])
    sckvT = dout("sckvT", [L, 128, NSX])
    skpeT = dout("skpeT", [L, 32, NSX])
    sconv = dout("sconv", [128, L * 4 * NS * 2])
    sffn = dout("sffn", [128, L * NPAIR * 2 * NS * 2])
    dbg = dout("dbg", [128, 8, NSX]) if DEBUG else None

    win_b = dscr("win_b", [L, D, 2048])
    wo_b = dscr("wo_b", [L, D, D])
    wup_b = dscr("wup_b", [L, D, 2 * DFF])
    wdn_b = dscr("wdn_b", [L, DFF, D])
    wuq_b = dscr("wuq_b", [L, 256, 1024])
    wuk_b = dscr("wuk_b", [L, 128, 512])
    wukT_b = dscr("wukT_b", [L, 64, 1024])
    wuv_b = dscr("wuv_b", [L, 128, 512])
    Ksc = dscr("Ksc", [L, 2, NT, 96, 4, 512])
    Vsc = dscr("Vsc", [L, 2, NT, 128, 4, 4, 128])
    win_r = [[Res() for _ in range(4)] for _ in range(L)]
    wo_r = [[Res() for _ in range(2)] for _ in range(L)]
    wup_r = [[Res() for _ in range(11)] for _ in range(L)]
    wdn_r = [[Res() for _ in range(8)] for _ in range(L)]
    wsm_r = [[Res() for _ in range(4)] for _ in range(L)]
    ksc_r = [[[Res() for _ in range(NT)] for _ in range(2)] for _ in range(L)]
    vsc_r = [[[Res() for _ in range(NT)] for _ in range(2)] for _ in range(L)]

    x = sb("x", [128, 8, 512]); xr = [Res() for _ in range(8)]
    hb = sb("hb", [128, 8, 512], BF16); hr = [Res() for _ in range(8)]
    mixb = sb("mixb", [128, 8, 512], BF16); mr = [Res() for _ in range(8)]
    arF = sb("arF", [128, 8, 512]); yr = [Res() for _ in range(8)]
    arB = sb("arB", [128, NPAIR, 512], BF16); ar = [Res() for _ in range(NPAIR)]
    sq = sb("sq", [128, 2, 512], BF16); sqr = [Res(), Res()]
    srt = sb("srt", [128, 512]); srt_r = Res()
    rstd = sb("rstd", [128, 512]); rstd_r = Res()
    ubc = sb("ubc", [128, 2, 516]); ubc_r = [Res(), Res()]
    qn = sb("qn", [128, 2, 512], BF16); qn_r = [Res(), Res()]
    ckvb = sb("ckvb", [128, 512], BF16); ckvb_r = Res()
    rt1 = sb("rt1", [128, 2, 512]); rt1_r = [Res(), Res()]
    rt2 = sb("rt2", [128, 2, 512]); rt2_r = [Res(), Res()]
    kpef = sb("kpef", [128, 512]); kpef_r = Res()
    Qb = sb("Qb", [128, 8, 512], BF16); q_r = [Res() for _ in range(8)]
    Kcur = sb("Kcur", [128, 8, 512], BF16); k_r = [Res() for _ in range(8)]
    Vcur = sb("Vcur", [128, 2, 4, 4, 128], BF16); v_r = [Res() for _ in range(4)]
    rec = sb("rec", [128, 2, 512]); rec_r = [Res(), Res()]
    wa = sb("wa", [128, 3, 8, 512], BF16); wa_r = [Res(), Res(), Res()]
    wb = sb("wb", [128, 3, NPAIR, 128], BF16); wb_r = [Res(), Res(), Res()]
    wuq_sb = sb("wuq_sb", [128, 2, 1024], BF16); wuq_r = Res()
    wuk_sb = sb("wuk_sb", [128, 512], BF16); wuk_r = Res()
    wukT_sb = sb("wukT_sb", [128, 1024], BF16); wukT_r = Res()
    wuv_sb = sb("wuv_sb", [128, 512], BF16); wuv_r = Res()
    ub = sb("ub", [128, 2, 2, 516]); ub_r = [Res(), Res()]
    facc = sb("facc", [128, 2, 2, 512]); facc_r = [[Res(), Res()], [Res(), Res()]]
    cstP = sb("cstP", [128, L, 4, 1, 2]); cstP_r = [[Res() for _ in range(4)] for _ in range(L)]
    fstP = sb("fstP", [128, L, NPAIR, 2, 1, 2]); fstP_r = [[Res() for _ in range(NPAIR)] for _ in range(L)]
    cstS = sb("cstS", [128, L, 4, NS, 2]); cstS_r = [[Res() for _ in range(4)] for _ in range(L)]
    fstS = sb("fstS", [128, L, NPAIR, 2, NS, 2]); fstS_r = [[Res() for _ in range(NPAIR)] for _ in range(L)]
    ones_b = sb("ones_b", [128, 128], BF16); ones_r = Res()
    ident = sb("ident", [128, 128]); ident_r = Res()
    vecs = sb("vecs", [128, L * VL]); vecs_r = Res()
    cosF = sb("cosF", [128, 512]); sinF = sb("sinF", [128, 512]); cs_r = Res()
    qlat = sb("qlat", [128, NS * 8 * TS], BF16); qlat_r = Res()
    qpe = sb("qpe", [128, NS * 8 * TS], BF16); qpe_r = Res()
    kpeb = sb("kpeb", [128, NSX], BF16); kpeb_r = Res()
    onb = sb("onb", [128, 2, 8 * TS], BF16); onb_r = [Res(), Res()]
    cntok = sb("cntok", [128, NS, 128], BF16); cntok_r = Res()
    recs = sb("recs", [128, 8 * TS]); recs_r = Res()
    pts = sb("pts", [128, 2, 8 * TS], BF16); pts_r = [Res(), Res()]

    banks = [es.enter_context(nc.psum_tensor(f"bank{i}", [128, 512], F32)) for i in range(8)]
    bank_r = [Res() for _ in range(8)]
    STATB = 7
    rot = {"bank": 0, "wa": 0, "wb": 0, "sq": 0}

    def next_bank():
        i = rot["bank"]
        rot["bank"] = (i + 1) % 7
        return banks[i], bank_r[i]

    def vcol(l, off):
        c = l * VL + off
        return vecs[:, c:c + 1]

    kb.op(POOL, lambda e: e.memset(ones_b[:, :], 1.0), writes=[ones_r])
    kb.op(POOL, lambda e: e.memset(Vcur[:, :, :, :, :].rearrange("p a b c d -> p (a b c) d")[:, :, 64:128], 1.0), writes=v_r)
    kb.op(POOL, lambda e: e.memset(cstP[:, :, :, :, :].rearrange("p l c s t -> p (l c s t)"), 0.0), writes=[r for rr in cstP_r for r in rr])
    kb.op(POOL, lambda e: e.memset(fstP[:, :, :, :, :, :].rearrange("p l c a s t -> p (l c a s t)"), 0.0), writes=[r for rr in fstP_r for r in rr])
    kb.dma(SP, vecs[:, :], vecs_in[:, :], writes=[vecs_r])
    kb.dma(SP, ident[:, :], ident_in[:, :], writes=[ident_r])
    kb.dma(SP, cstS[:, :, :, :, :], cst_in.rearrange("p (l c s t) -> p l c s t", l=L, c=4, s=NS),
           writes=[r for rr in cstS_r for r in rr])
    kb.dma(SP, fstS[:, :, :, :, :, :], fst_in.rearrange("p (l c a s t) -> p l c a s t", l=L, c=NPAIR, a=2, s=NS),
           writes=[r for rr in fstS_r for r in rr])
    GQ = kb.gq

    def emit_casts(l):
        kb.dma(GQ, win_b[l], w_in[l], writes=win_r[l])
        kb.dma(GQ, wuq_b[l], w_uq[l], writes=[wsm_r[l][0]])
        kb.dma(GQ, wuk_b[l], w_uk[l], writes=[wsm_r[l][1]])
        kb.dma(GQ, wukT_b[l], w_ukT[l], writes=[wsm_r[l][2]])
        kb.dma(GQ, wuv_b[l], w_uv[l], writes=[wsm_r[l][3]])
        kb.dma(GQ, wo_b[l], w_o[l], writes=wo_r[l])
        kb.dma(GQ, wup_b[l], w_up[l], writes=wup_r[l])
        kb.dma(GQ, wdn_b[l], w_dn[l], writes=wdn_r[l])

    emit_casts(0)

    class Ctx:
        pass

    eps_t = sb("eps_t", [128, 1]); eps_r = Res()
    eps_ap = eps_t[:, 0:1]
    kb.op(POOL, lambda e: e.memset(eps_t[:, :], EPS), writes=[eps_r])

    def sq_next():
        i = rot["sq"]
        rot["sq"] = 1 - i
        return i

    def rms_pre(cx, l, goff):
        N = cx.N
        bank, br = banks[STATB], bank_r[STATB]
        for c in range(8):
            i = sq_next()
            if c % 4 in (0, 1):
                kb.op(ACT, lambda e, c=c, i=i: e.activation(out=sq[:, i, 0:N], in_=x[:, c, 0:N], func=AF.Square),
                      reads=[xr[c]], writes=[sqr[i]])
            else:
                kb.op(DVE if c % 4 == 2 else POOL, lambda e, c=c, i=i: e.tensor_tensor(out=sq[:, i, 0:N], in0=x[:, c, 0:N], in1=x[:, c, 0:N],
                                                                                      op=ALU.mult), reads=[xr[c]], writes=[sqr[i]])
            kb.mm([lambda e, c=c, i=i: e.matmul(bank[:, 0:N], ones_b[:, :], sq[:, i, 0:N], start=(c == 0), stop=(c == 7))],
                  reads=[sqr[i], ones_r], writes=[br])
        for c in (5, 6, 7):
            kb.op(ACT, lambda e, c=c: e.activation(out=arF[:, c, 0:N], in_=x[:, c, 0:N], func=AF.Copy, scale=vcol(l, goff + c)),
                  reads=[xr[c], vecs_r], writes=[yr[c]])
        kb.op(ACT, lambda e: e.activation(out=srt[:, 0:N], in_=bank[:, 0:N], func=AF.Ln, scale=1.0 / D, bias=eps_ap),
              reads=[br, eps_r], writes=[srt_r])
        kb.op(ACT, lambda e: e.activation(out=rstd[:, 0:N], in_=srt[:, 0:N], func=AF.Exp, scale=-0.5), reads=[srt_r], writes=[rstd_r])
        for c in range(8):
            if c < 5:
                kb.op(DVE, lambda e, c=c: e.scalar_tensor_tensor(out=hb[:, c, 0:N], in0=x[:, c, 0:N], scalar=vcol(l, goff + c),
                                                                 in1=rstd[:, 0:N], op0=ALU.mult, op1=ALU.mult),
                      reads=[xr[c], rstd_r, vecs_r], writes=[hr[c]])
            else:
                kb.op(POOL, lambda e, c=c: e.tensor_tensor(out=hb[:, c, 0:N], in0=arF[:, c, 0:N], in1=rstd[:, 0:N], op=ALU.mult),
                      reads=[yr[c], rstd_r], writes=[hr[c]])

    pend_stats = []

    def post_consumer(cx, m, bank, br):
        N = cx.N
        while pend_stats:
            pend_stats.pop(0)()
        i = sq_next()
        kb.op(ACT, lambda e: e.activation(out=arF[:, m, 0:N], in_=bank[:, 0:N], func=AF.Copy), reads=[br], writes=[yr[m]])
        kb.op(ACT, lambda e: e.activation(out=sq[:, i, 0:N], in_=bank[:, 0:N], func=AF.Square), reads=[br], writes=[sqr[i]])
        sbk, sbr = banks[STATB], bank_r[STATB]
        pend_stats.append(lambda: kb.mm([lambda e: e.matmul(sbk[:, 0:N], ones_b[:, :], sq[:, i, 0:N], start=(m == 0), stop=(m == 7))],
                                        reads=[sqr[i], ones_r], writes=[sbr]))

    def post_finish(cx, l, goff):
        N = cx.N
        sbk, sbr = banks[STATB], bank_r[STATB]
        while pend_stats:
            pend_stats.pop(0)()
        kb.op(ACT, lambda e: e.activation(out=srt[:, 0:N], in_=sbk[:, 0:N], func=AF.Ln, scale=1.0 / D, bias=eps_ap),
              reads=[sbr, eps_r], writes=[srt_r])
        kb.op(ACT, lambda e: e.activation(out=rstd[:, 0:N], in_=srt[:, 0:N], func=AF.Exp, scale=-0.5), reads=[srt_r], writes=[rstd_r])
        for c in range(8):
            q = DVE if c % 2 == 0 else POOL
            kb.op(q, lambda e, c=c: e.tensor_tensor(out=arF[:, c, 0:N], in0=arF[:, c, 0:N], in1=rstd[:, 0:N], op=ALU.mult),
                  reads=[yr[c], rstd_r], writes=[yr[c]])
            kb.op(DVE, lambda e, c=c: e.scalar_tensor_tensor(out=x[:, c, 0:N], in0=arF[:, c, 0:N], scalar=vcol(l, goff + c),
                                                           in1=x[:, c, 0:N], op0=ALU.mult, op1=ALU.add),
                  reads=[yr[c], xr[c], vecs_r], writes=[xr[c]])

    def wa_load(src_ap, src_res):
        s = rot["wa"]
        rot["wa"] = (s + 1) % 3
        kb.dma(SP, wa[:, s], src_ap, reads=[src_res], writes=[wa_r[s]])
        return s

    def mm_k8(bank, br, N, s, col0, M, rhs, rhs_res, split=False):
        fns = [lambda e, kc=kc: e.matmul(bank[0:M, 0:N], wa[:, s, kc, col0:col0 + M], rhs[:, kc, 0:N],
                                         start=(kc == 0), stop=(kc == 7)) for kc in range(8)]
        if split:
            for kc in range(8):
                kb.mm([fns[kc]], reads=[wa_r[s], rhs_res[kc]], writes=[br])
        else:
            kb.mm(fns, reads=[wa_r[s]] + list(rhs_res), writes=[br])

    def v3(ap2, cx, width):
        return ap2.rearrange("p (s t) -> p s t", s=cx.S)

    def mixer(cx, l):
        N, S, T = cx.N, cx.S, cx.T
        W2 = T + 2
        rms_pre(cx, l, V_GMPRE)
        s = wa_load(win_b[l].rearrange("(kc p) n -> p kc n", p=128)[:, :, 0 * 512:1 * 512], win_r[l][0])
        for c in range(4):
            bank, br = next_bank()
            mm_k8(bank, br, N, s, c * 128, 128, hb, hr, split=(c == 0))
            kb.op(ACT, lambda e, c=c, bank=bank: e.activation(out=arF[:, c, 0:N], in_=bank[:, 0:N], func=AF.Copy),
                  reads=[br], writes=[yr[c]])
        s = wa_load(win_b[l].rearrange("(kc p) n -> p kc n", p=128)[:, :, 1 * 512:2 * 512], win_r[l][1])
        for c in range(4):
            bank, br = next_bank()
            mm_k8(bank, br, N, s, c * 128, 128, hb, hr)
            r = c % 2
            uv = v3(ubc[:, r, 0:S * W2], cx, W2)
            kb.op(DVE, lambda e, c=c, bank=bank, uv=uv: e.tensor_tensor(out=uv[:, :, 2:W2], in0=v3(bank[:, 0:N], cx, T),
                                                                        in1=v3(arF[:, c, 0:N], cx, T), op=ALU.mult),
                  reads=[br, yr[c]], writes=[ubc_r[r]])
            kb.op(POOL, lambda e, c=c, uv=uv: e.tensor_copy(out=uv[:, :, 0:2], in_=cx.cst[:, l, c, :, :]),
                  reads=[cx.cst_r[l][c]], writes=[ubc_r[r]])
            kb.op(POOL, lambda e, c=c, uv=uv: e.tensor_copy(out=cx.cst[:, l, c, :, :], in_=uv[:, :, T:W2]),
                  reads=[ubc_r[r]], writes=[cx.cst_r[l][c]])
            acc = v3(arF[:, c, 0:N], cx, T)
            kb.op(DVE, lambda e, c=c, uv=uv, acc=acc: e.tensor_scalar(out=acc, in0=uv[:, :, 0:T], scalar1=vcol(l, V_WCONV + c),
                                                                      scalar2=0.0, op0=ALU.mult, op1=ALU.add),
                  reads=[ubc_r[r], vecs_r], writes=[yr[c]])
            kb.op(DVE, lambda e, c=c, uv=uv, acc=acc: e.scalar_tensor_tensor(out=acc, in0=uv[:, :, 1:T + 1],
                                                                              scalar=vcol(l, V_WCONV + 4 + c), in1=acc,
                                                                              op0=ALU.mult, op1=ALU.add),
                  reads=[ubc_r[r], vecs_r, yr[c]], writes=[yr[c]])
            kb.op(DVE, lambda e, c=c, uv=uv, acc=acc: e.scalar_tensor_tensor(out=acc, in0=uv[:, :, 2:W2],
                                                                             scalar=vcol(l, V_WCONV + 8 + c), in1=acc,
                                                                             op0=ALU.mult, op1=ALU.add),
                  reads=[ubc_r[r], vecs_r, yr[c]], writes=[yr[c]])
        s = wa_load(win_b[l].rearrange("(kc p) n -> p kc n", p=128)[:, :, 2 * 512:3 * 512], win_r[l][2])
        for c in range(4):
            bank, br = next_bank()
            mm_k8(bank, br, N, s, c * 128, 128, hb, hr)
            kb.op(DVE, lambda e, c=c, bank=bank: e.tensor_tensor(out=mixb[:, c, 0:N], in0=bank[:, 0:N], in1=arF[:, c, 0:N],
                                                                 op=ALU.mult), reads=[br, yr[c]], writes=[mr[c]])
        s = wa_load(win_b[l].rearrange("(kc p) n -> p kc n", p=128)[:, :, 3 * 512:4 * 512], win_r[l][3])
        sbk, sbr = banks[STATB], bank_r[STATB]
        qi = []
        for c in range(2):
            bank, br = next_bank()
            mm_k8(bank, br, N, s, c * 128, 128, hb, hr)
            i = sq_next()
            qi.append(i)
            kb.op(ACT, lambda e, c=c, bank=bank: e.activation(out=arF[:, 4 + c, 0:N], in_=bank[:, 0:N], func=AF.Copy),
                  reads=[br], writes=[yr[4 + c]])
            kb.op(ACT, lambda e, i=i, bank=bank: e.activation(out=sq[:, i, 0:N], in_=bank[:, 0:N], func=AF.Square),
                  reads=[br], writes=[sqr[i]])
        bkv, bkvr = next_bank()
        mm_k8(bkv, bkvr, N, s, 256, 128, hb, hr)
        bA, bAr = next_bank()
        mm_k8(bA, bAr, N, s, 320, 96, hb, hr)
        bB, bBr = next_bank()
        mm_k8(bB, bBr, N, s, 352, 96, hb, hr)
        for c in range(2):
            i = qi[c]
            kb.mm([lambda e, c=c, i=i: e.matmul(sbk[:, 0:N], ones_b[:, :], sq[:, i, 0:N], start=(c == 0), stop=(c == 1))],
                  reads=[sqr[i], ones_r], writes=[sbr])
        kb.op(ACT, lambda e: e.activation(out=srt[:, 0:N], in_=sbk[:, 0:N], func=AF.Ln, scale=1.0 / 256, bias=eps_ap),
              reads=[sbr, eps_r], writes=[srt_r])
        kb.op(ACT, lambda e: e.activation(out=rstd[:, 0:N], in_=srt[:, 0:N], func=AF.Exp, scale=-0.5), reads=[srt_r], writes=[rstd_r])
        for c in range(2):
            kb.op(DVE, lambda e, c=c: e.scalar_tensor_tensor(out=qn[:, c, 0:N], in0=arF[:, 4 + c, 0:N], scalar=vcol(l, V_GQA + c),
                                                             in1=rstd[:, 0:N], op0=ALU.mult, op1=ALU.mult),
                  reads=[yr[4 + c], rstd_r, vecs_r], writes=[qn_r[c]])
        i = sq_next()
        kb.op(ACT, lambda e: e.activation(out=arF[:, 6, 0:N], in_=bkv[:, 0:N], func=AF.Copy), reads=[bkvr], writes=[yr[6]])
        kb.op(ACT, lambda e, i=i: e.activation(out=sq[:, i, 0:N], in_=bkv[:, 0:N], func=AF.Square),
              reads=[bkvr], writes=[sqr[i]])
        kb.mm([lambda e, i=i: e.matmul(sbk[:, 0:N], ones_b[:, :], sq[:, i, 0:N], start=True, stop=True)],
              reads=[sqr[i], ones_r], writes=[sbr])
        kb.op(ACT, lambda e: e.activation(out=srt[:, 0:N], in_=sbk[:, 0:N], func=AF.Ln, scale=1.0 / 128, bias=eps_ap),
              reads=[sbr, eps_r], writes=[srt_r])
        kb.op(ACT, lambda e: e.activation(out=rstd[:, 0:N], in_=srt[:, 0:N], func=AF.Exp, scale=-0.5), reads=[srt_r], writes=[rstd_r])
        kb.op(DVE, lambda e: e.scalar_tensor_tensor(out=arF[:, 7, 0:N], in0=arF[:, 6, 0:N], scalar=vcol(l, V_GKVA),
                                                    in1=rstd[:, 0:N], op0=ALU.mult, op1=ALU.mult),
              reads=[yr[6], rstd_r, vecs_r], writes=[yr[7]])
        kb.op(POOL, lambda e: e.tensor_copy(out=ckvb[:, 0:N], in_=arF[:, 7, 0:N]), reads=[yr[7]], writes=[ckvb_r])
        kb.dma(SP, cx.ckv_out(l), arF[:, 7, 0:N], reads=[yr[7]])
        kb.op(DVE, lambda e: e.tensor_tensor(out=rt1[64:96, 0, 0:N], in0=bA[64:96, 0:N], in1=cosF[64:96, 0:N], op=ALU.mult),
              reads=[bAr, cs_r], writes=[rt1_r[0]])
        kb.op(DVE, lambda e: e.tensor_tensor(out=rt2[64:96, 0, 0:N], in0=bB[64:96, 0:N], in1=sinF[64:96, 0:N], op=ALU.mult),
              reads=[bBr, cs_r], writes=[rt2_r[0]])
        kb.op(POOL, lambda e: e.tensor_tensor(out=kpef[64:96, 0:N], in0=rt1[64:96, 0, 0:N], in1=rt2[64:96, 0, 0:N], op=ALU.add),
              reads=[rt1_r[0], rt2_r[0]], writes=[kpef_r])
        kb.dma(SP, cx.kpe_out(l), kpef[64:96, 0:N], reads=[kpef_r])
        if cx.prompt:
            for h in range(8):
                q = ACT if h % 2 == 0 else POOL
                if q is ACT:
                    kb.op(q, lambda e, h=h: e.activation(out=Kcur[64:96, h, 0:N], in_=kpef[64:96, 0:N], func=AF.Copy),
                          reads=[kpef_r], writes=[k_r[h]])
                else:
                    kb.op(q, lambda e, h=h: e.tensor_copy(out=Kcur[64:96, h, 0:N], in_=kpef[64:96, 0:N]),
                          reads=[kpef_r], writes=[k_r[h]])
        else:
            kb.op(POOL, lambda e: e.tensor_copy(out=kpeb[64:96, 0:N], in_=kpef[64:96, 0:N]), reads=[kpef_r], writes=[kpeb_r])
        kb.dma(SP, wuq_sb[:, :, :], wuq_b[l].rearrange("(kc p) n -> p kc n", p=128), reads=[wsm_r[l][0]], writes=[wuq_r])
        for h in range(8):
            b1, b1r = next_bank()
            kb.mm([lambda e, kc=kc, h=h, b1=b1: e.matmul(b1[0:96, 0:N], wuq_sb[:, kc, h * 128:h * 128 + 96], qn[:, kc, 0:N],
                                                          start=(kc == 0), stop=(kc == 1)) for kc in range(2)],
                  reads=[wuq_r, qn_r[0], qn_r[1]], writes=[b1r])
            b2, b2r = next_bank()
            kb.mm([lambda e, kc=kc, h=h, b2=b2: e.matmul(b2[0:96, 0:N], wuq_sb[:, kc, h * 128 + 32:h * 128 + 128], qn[:, kc, 0:N],
                                                          start=(kc == 0), stop=(kc == 1)) for kc in range(2)],
                  reads=[wuq_r, qn_r[0], qn_r[1]], writes=[b2r])
            r = h % 2
            kb.op(ACT, lambda e, h=h, b1=b1: e.activation(out=Qb[0:64, h, 0:N], in_=b1[0:64, 0:N], func=AF.Copy),
                  reads=[b1r], writes=[q_r[h]])
            kb.op(DVE, lambda e, b1=b1, r=r: e.tensor_tensor(out=rt1[64:96, r, 0:N], in0=b1[64:96, 0:N], in1=cosF[64:96, 0:N],
                                                             op=ALU.mult), reads=[b1r, cs_r], writes=[rt1_r[r]])
            kb.op(DVE, lambda e, b2=b2, r=r: e.tensor_tensor(out=rt2[64:96, r, 0:N], in0=b2[64:96, 0:N], in1=sinF[64:96, 0:N],
                                                             op=ALU.mult), reads=[b2r, cs_r], writes=[rt2_r[r]])
            if cx.prompt:
                kb.op(POOL, lambda e, h=h, r=r: e.tensor_tensor(out=Qb[64:96, h, 0:N], in0=rt1[64:96, r, 0:N],
                                                                in1=rt2[64:96, r, 0:N], op=ALU.add),
                      reads=[rt1_r[r], rt2_r[r]], writes=[q_r[h]])
            else:
                qv = qpe[64:96, :].rearrange("p (s h t) -> p s h t", s=NS, h=8)[:, :, h, :]
                kb.op(POOL, lambda e, qv=qv, r=r: e.tensor_tensor(out=qv, in0=v3(rt1[64:96, r, 0:N], cx, T),
                                                                  in1=v3(rt2[64:96, r, 0:N], cx, T), op=ALU.add),
                      reads=[rt1_r[r], rt2_r[r]], writes=[qpe_r])
        if cx.prompt:
            attn_prompt(cx, l)
        else:
            attn_sample(cx, l)
        if DEBUG and (not cx.prompt) and l == 0:
            kb.dma(GQ, dbg[:, :, :], mixb[:, :, 0:N], reads=mr)
        for g in range(2):
            s = wa_load(wo_b[l].rearrange("(kc p) n -> p kc n", p=128)[:, :, g * 512:(g + 1) * 512], wo_r[l][g])
            for mi in range(4):
                bank, br = next_bank()
                mm_k8(bank, br, N, s, mi * 128, 128, mixb, mr, split=(g == 0 and mi == 0))
                post_consumer(cx, g * 4 + mi, bank, br)
        post_finish(cx, l, V_GMPOST)

    def attn_prompt(cx, l):
        j = cx.j
        kb.dma(SP, wuk_sb[:, :], wuk_b[l], reads=[wsm_r[l][1]], writes=[wuk_r])
        kb.dma(SP, wuv_sb[:, :], wuv_b[l], reads=[wsm_r[l][3]], writes=[wuv_r])
        for h in range(8):
            bank, br = next_bank()
            kb.mm([lambda e, h=h, bank=bank: e.matmul(bank[0:64, 0:512], wuk_sb[:, h * 64:(h + 1) * 64], ckvb[:, 0:512],
                                                      start=True, stop=True)], reads=[wuk_r, ckvb_r], writes=[br])
            if h % 2 == 0:
                kb.op(ACT, lambda e, h=h, bank=bank: e.activation(out=Kcur[0:64, h, :], in_=bank[0:64, 0:512], func=AF.Copy),
                      reads=[br], writes=[k_r[h]])
            else:
                kb.op(DVE, lambda e, h=h, bank=bank: e.tensor_copy(out=Kcur[0:64, h, :], in_=bank[0:64, 0:512]),
                      reads=[br], writes=[k_r[h]])
        for kbi in range(4):
            bank, br = next_bank()
            kb.mm([lambda e, kbi=kbi, bank=bank: e.matmul(bank[:, 0:512], ckvb[:, kbi * 128:(kbi + 1) * 128], wuv_sb[:, :],
                                                          start=True, stop=True)], reads=[wuv_r, ckvb_r], writes=[br])
            src = bank[:, 0:512].rearrange("p (a h v) -> p a h v", a=2, h=4)
            kb.op(DVE, lambda e, kbi=kbi, src=src: e.tensor_copy(out=Vcur[:, 0, kbi, :, 0:64], in_=src[:, 0]),
                  reads=[br], writes=[v_r[kbi]])
            kb.op(ACT, lambda e, kbi=kbi, src=src: e.activation(out=Vcur[:, 1, kbi, :, 0:64], in_=src[:, 1], func=AF.Copy),
                  reads=[br], writes=[v_r[kbi]])
        if j < NT - 1:
            for hp in range(2):
                kb.dma(SP, Ksc[l, hp, j], Kcur[0:96, hp * 4:(hp + 1) * 4, :], reads=k_r[hp * 4:(hp + 1) * 4],
                       writes=[ksc_r[l][hp][j]])
                kb.dma(SP, Vsc[l, hp, j], Vcur[:, hp], reads=v_r, writes=[vsc_r[l][hp][j]])
        slot_i = [0]
        for hp in range(2):
            steps = []
            chunk_dma = []
            for jj in range(j):
                si = slot_i[0] % 2
                slot_i[0] += 1
                ks, vs = 2 * si, 2 * si + 1
                ksl = arB[0:96, 4 * ks:4 * ks + 4, :]
                vsl = arB[:, 4 * vs:4 * vs + 4, :].rearrange("p a (h v) -> p a h v", h=4)
                kres = ar[4 * ks:4 * ks + 4]
                vres = ar[4 * vs:4 * vs + 4]
                chunk_dma.append((ksl, vsl, kres, vres))
                for kbi in range(4):
                    for hh in range(4):
                        steps.append((arB[0:96, 4 * ks + hh, kbi * 128:(kbi + 1) * 128], kres,
                                      vsl[:, kbi, hh, :], vres, 0, False, hh))
            for kbi in range(4):
                for hh in range(4):
                    h = hp * 4 + hh
                    steps.append((Kcur[0:96, h, kbi * 128:(kbi + 1) * 128], [k_r[h]],
                                  Vcur[:, hp, kbi, hh, :], [v_r[kbi]], kbi * 128, True, hh))
            nsteps = len(steps)
            first = [True] * 4
            lastidx = [max(i for i in range(nsteps) if steps[i][6] == hh) for hh in range(4)]
            pend = []

            def do_pv(idx, pi, c0):
                kap, kres, vap, vres, _, _, hh = steps[idx]
                st = first[hh]
                first[hh] = False
                kb.mm([lambda e: e.matmul(banks[hh][:, c0:512], vap, arB[:, 16 + pi, c0:512], start=st, stop=(idx == lastidx[hh]))],
                      reads=[ar[16 + pi]] + list(vres), writes=[bank_r[hh]])

            def emit_dma(jj, hp=hp, chunk_dma=chunk_dma):
                ksl, vsl, kres, vres = chunk_dma[jj]
                kb.dma(SP, ksl, Ksc[l, hp, jj], reads=[ksc_r[l][hp][jj]], writes=kres)
                kb.dma(SP, vsl, Vsc[l, hp, jj], reads=[vsc_r[l][hp][jj]], writes=vres)

            for idx in range(nsteps):
                if idx == 0:
                    for jj in range(min(2, j)):
                        emit_dma(jj)
                elif idx % 16 == 3 and 2 <= idx // 16 + 1 < j:
                    emit_dma(idx // 16 + 1)
                kap, kres, vap, vres, c0, diag, hh = steps[idx]
                h = hp * 4 + hh
                sbi = 4 + idx % 3
                pi = idx % 4
                kb.mm([lambda e, kap=kap, h=h, sbi=sbi, c0=c0: e.matmul(banks[sbi][:, c0:512], kap, Qb[0:96, h, c0:512],
                                                                         start=True, stop=True)],
                      reads=list(kres) + [q_r[h]], writes=[bank_r[sbi]])
                kb.op(ACT, lambda e, sbi=sbi, pi=pi, c0=c0: e.activation(out=arB[:, 16 + pi, c0:512], in_=banks[sbi][:, c0:512],
                                                                       func=AF.Exp, scale=SCALE),
                      reads=[bank_r[sbi]], writes=[ar[16 + pi]])
                if diag:
                    kb.op(POOL, lambda e, pi=pi, c0=c0: e.memset(arB[64:128, 16 + pi, c0:c0 + 64], 0.0), writes=[ar[16 + pi]])
                pend.append((idx, pi, c0))
                if len(pend) > 2:
                    do_pv(*pend.pop(0))
            while pend:
                do_pv(*pend.pop(0))
            for hh in range(4):
                h = hp * 4 + hh
                r = hh % 2
                kb.op(ACT, lambda e, hh=hh, r=r: e.activation(out=rec[0:64, r, :], in_=banks[hh][64:128, 0:512], func=AF.Ln),
                      reads=[bank_r[hh]], writes=[rec_r[r]])
                kb.op(ACT, lambda e, r=r: e.activation(out=rec[0:64, r, :], in_=rec[0:64, r, :], func=AF.Exp, scale=-1.0),
                      reads=[rec_r[r]], writes=[rec_r[r]])
                p0 = (h % 2) * 64
                kb.op(DVE, lambda e, hh=hh, r=r, h=h, p0=p0: e.tensor_tensor(out=mixb[p0:p0 + 64, 4 + h // 2, :],
                                                                              in0=banks[hh][0:64, 0:512], in1=rec[0:64, r, :],
                                                                              op=ALU.mult),
                      reads=[bank_r[hh], rec_r[r]], writes=[mr[4 + h // 2]])

    def attn_sample(cx, l):
        N = cx.N
        kb.dma(SP, wukT_sb[0:64, :], wukT_b[l], reads=[wsm_r[l][2]], writes=[wukT_r])
        kb.dma(SP, wuv_sb[:, :], wuv_b[l], reads=[wsm_r[l][3]], writes=[wuv_r])
        for h in range(8):
            bank, br = next_bank()
            kb.mm([lambda e, h=h, bank=bank: e.matmul(bank[:, 0:N], wukT_sb[0:64, h * 128:(h + 1) * 128], Qb[0:64, h, 0:N],
                                                      start=True, stop=True)], reads=[wukT_r, q_r[h]], writes=[br])
            qv = qlat[:, :].rearrange("p (s h t) -> p s h t", s=NS, h=8)[:, :, h, :]
            kb.op(ACT if h % 2 == 0 else DVE,
                  (lambda e, qv=qv, bank=bank: e.activation(out=qv, in_=v3(bank[:, 0:N], cx, TS), func=AF.Copy)) if h % 2 == 0 else
                  (lambda e, qv=qv, bank=bank: e.tensor_copy(out=qv, in_=v3(bank[:, 0:N], cx, TS))),
                  reads=[br], writes=[qlat_r])
        bank, br = banks[3], bank_r[3]
        for s_ in range(NS):
            kb.mm([lambda e, s_=s_, bank=bank: e.transpose(bank[0:TS, s_ * 128:(s_ + 1) * 128], arF[:, 7, s_ * TS:(s_ + 1) * TS],
                                                            ident[:, :])], reads=[yr[7], ident_r], writes=[br])
        kb.op(DVE, lambda e, bank=bank: e.tensor_copy(out=cntok[0:TS, :, :], in_=bank[0:TS, 0:NS * 128].rearrange("p (s r) -> p s r", s=NS)),
              reads=[br], writes=[cntok_r])
        W = 8 * TS
        ybank, ybr = banks[0], bank_r[0]
        chunk_i = [0]
        for s_ in range(NS):
            obank, obr = banks[1 + s_ % 2], bank_r[1 + s_ % 2]
            nblk = NPB + 1
            blk = 0
            pend = []
            qlv = qlat[:, s_ * W:(s_ + 1) * W]
            qpv = qpe[64:96, s_ * W:(s_ + 1) * W]

            def do_pv(cap, cres, kk, pi, b, obank=obank, obr=obr, nblk=nblk):
                kb.mm([lambda e: e.matmul(obank[:, 0:W], cap, pts[0:kk, pi, :], start=(b == 0), stop=(b == nblk - 1)),
                       lambda e: e.matmul(obank[:, W:2 * W], ones_b[0:kk, :], pts[0:kk, pi, :], start=False, stop=(b == nblk - 1),
                                          skip_group_check=True)],
                      reads=[pts_r[pi], ones_r] + list(cres), writes=[obr])

            def do_sc(ctap, ktap, cres, kk, b, qlv=qlv, qpv=qpv):
                sbi = 4 + b % 3
                pi = b % 2
                kb.mm([lambda e: e.matmul(banks[sbi][0:kk, 0:W], ctap, qlv, start=True, stop=False),
                       lambda e: e.matmul(banks[sbi][0:kk, 0:W], ktap, qpv, start=False, stop=True)],
                      reads=list(cres) + [qlat_r, qpe_r], writes=[bank_r[sbi]])
                kb.op(ACT, lambda e: e.activation(out=pts[0:kk, pi, :], in_=banks[sbi][0:kk, 0:W], func=AF.Exp, scale=SCALE),
                      reads=[bank_r[sbi]], writes=[pts_r[pi]])
                return pi

            for ch in range(PAST // 1024):
                ci = chunk_i[0] % 2
                chunk_i[0] += 1
                sa, sb_ = 2 * ci, 2 * ci + 1
                resA = ar[4 * sa:4 * sa + 4]
                resB = ar[4 * sb_:4 * sb_ + 4]
                cTs = arB[:, 4 * sa:4 * sa + 2, :]
                ccs = arB[:, 4 * sa + 2:4 * sa + 4, :]
                kTs = arB[64:96, 4 * sb_:4 * sb_ + 2, :]
                kb.dma(GQ, cTs, cT_in[l, s_, :, ch * 1024:(ch + 1) * 1024].rearrange("p (a n) -> p a n", a=2), writes=resA)
                kb.dma(GQ, ccs.rearrange("p a (b r) -> p (a b) r", r=128),
                       cc_in[l, s_, ch * 1024:(ch + 1) * 1024, :].rearrange("(b p) r -> p b r", p=128), writes=resA)
                kb.dma(GQ, kTs, kT_in[l, s_, :, ch * 1024:(ch + 1) * 1024].rearrange("p (a n) -> p a n", a=2), writes=resB)
                for bi in range(8):
                    ctap = arB[:, 4 * sa + bi // 4, (bi % 4) * 128:(bi % 4) * 128 + 128]
                    ktap = arB[64:96, 4 * sb_ + bi // 4, (bi % 4) * 128:(bi % 4) * 128 + 128]
                    cap = arB[:, 4 * sa + 2 + bi // 4, (bi % 4) * 128:(bi % 4) * 128 + 128]
                    pi = do_sc(ctap, ktap, list(resA) + list(resB), 128, blk)
                    pend.append((cap, list(resA), 128, pi, blk))
                    blk += 1
                    if len(pend) > 1:
                        do_pv(*pend.pop(0))
            pi = do_sc(ckvb[:, s_ * TS:(s_ + 1) * TS], kpeb[64:96, s_ * TS:(s_ + 1) * TS], [ckvb_r, kpeb_r], TS, blk)
            pend.append((cntok[0:TS, s_, :], [cntok_r], TS, pi, blk))
            while pend:
                do_pv(*pend.pop(0))
            oi = s_ % 2
            kb.op(DVE, lambda e, obank=obank: e.reciprocal(out=recs[:, :], in_=obank[:, W:2 * W]), reads=[obr], writes=[recs_r])
            kb.op(DVE, lambda e, obank=obank, oi=oi: e.tensor_tensor(out=onb[:, oi, :], in0=obank[:, 0:W], in1=recs[:, :], op=ALU.mult),
                  reads=[obr, recs_r], writes=[onb_r[oi]])
            for h in range(8):
                p0 = (h % 2) * 64
                c0 = (h // 2) * N + s_ * TS
                kb.mm([lambda e, h=h, p0=p0, c0=c0, oi=oi: e.matmul(ybank[p0:p0 + 64, c0:c0 + TS], wuv_sb[:, h * 64:(h + 1) * 64],
                                                                    onb[:, oi, h * TS:(h + 1) * TS], start=True, stop=True)],
                      reads=[wuv_r, onb_r[oi]], writes=[ybr])
        kb.op(DVE, lambda e: e.tensor_copy(out=mixb[:, 4:8, 0:N], in_=ybank[:, 0:4 * N].rearrange("p (c n) -> p c n", c=4)),
              reads=[ybr], writes=mr[4:8])

    def ffn(cx, l):
        N, S, T = cx.N, cx.S, cx.T
        W2 = T + 2
        rms_pre(cx, l, V_GFPRE)
        ffn_tail = []
        for g in range(11):
            s = wa_load(wup_b[l].rearrange("(kc p) n -> p kc n", p=128)[:, :, g * 512:(g + 1) * 512], wup_r[l][g])
            for pi in range(2):
                pair = g * 2 + pi
                r = pair % 2
                bks = []
                for xx in range(2):
                    bank, br = next_bank()
                    mm_k8(bank, br, N, s, (pi * 2 + xx) * 128, 128, hb, hr, split=(pair == 0 and xx == 0))
                    bks.append((bank, br))
                ubv = ub[:, r, :, 0:S * W2].rearrange("p a (s t) -> p a s t", s=S)
                for xx in range(2):
                    bank, br = bks[xx]
                    kb.op(ACT, lambda e, xx=xx, bank=bank, ubv=ubv: e.activation(out=ubv[:, xx, :, 2:W2], in_=v3(bank[:, 0:N], cx, T),
                                                                                  func=AF.Copy), reads=[br], writes=[ub_r[r]])
                kb.op(POOL, lambda e, ubv=ubv, pair=pair: e.tensor_copy(out=ubv[:, :, :, 0:2], in_=cx.fst[:, l, pair, :, :, :]),
                      reads=[cx.fst_r[l][pair]], writes=[ub_r[r]])
                kb.op(POOL, lambda e, ubv=ubv, pair=pair: e.tensor_copy(out=cx.fst[:, l, pair, :, :, :], in_=ubv[:, :, :, T:W2]),
                      reads=[ub_r[r]], writes=[cx.fst_r[l][pair]])
                for xx in range(2):
                    q = DVE
                    chn = pair + xx * NPAIR
                    acc = v3(facc[:, r, xx, 0:N], cx, T)
                    fr = facc_r[r][xx]
                    kb.op(ACT, lambda e, xx=xx, acc=acc, ubv=ubv, chn=chn: e.activation(
                        out=acc, in_=ubv[:, xx, :, 0:T], func=AF.Identity, scale=vcol(l, V_WFFN + chn), bias=vcol(l, V_BFFN + chn)),
                        reads=[ub_r[r], vecs_r], writes=[fr])
                    kb.op(q, lambda e, xx=xx, acc=acc, ubv=ubv, chn=chn: e.scalar_tensor_tensor(
                        out=acc, in0=ubv[:, xx, :, 1:T + 1], scalar=vcol(l, V_WFFN + 44 + chn), in1=acc, op0=ALU.mult, op1=ALU.add),
                        reads=[ub_r[r], vecs_r, fr], writes=[fr])
                    kb.op(q, lambda e, xx=xx, acc=acc, ubv=ubv, chn=chn: e.scalar_tensor_tensor(
                        out=acc, in0=ubv[:, xx, :, 2:W2], scalar=vcol(l, V_WFFN + 88 + chn), in1=acc, op0=ALU.mult, op1=ALU.add),
                        reads=[ub_r[r], vecs_r, fr], writes=[fr])
                def tail(r=r, pair=pair):
                    kb.op(ACT, lambda e: e.activation(out=facc[:, r, 0, 0:N], in_=facc[:, r, 0, 0:N], func=AF.Silu),
                          reads=[facc_r[r][0]], writes=[facc_r[r][0]])
                    kb.op(POOL, lambda e: e.tensor_tensor(out=arB[:, pair, 0:N], in0=facc[:, r, 0, 0:N],
                                                          in1=facc[:, r, 1, 0:N], op=ALU.mult),
                          reads=[facc_r[r][0], facc_r[r][1]], writes=[ar[pair]])
                while ffn_tail:
                    ffn_tail.pop(0)()
                ffn_tail.append(tail)
        while ffn_tail:
            ffn_tail.pop(0)()
        for m in range(8):
            s = rot["wb"]
            rot["wb"] = (s + 1) % 3
            kb.dma(SP, wb[:, s], wdn_b[l].rearrange("(kc p) n -> p kc n", p=128)[:, :, m * 128:(m + 1) * 128], reads=[wdn_r[l][m]], writes=[wb_r[s]])
            bank, br = next_bank()
            dfns = [lambda e, kc=kc, s=s, bank=bank: e.matmul(bank[:, 0:N], wb[:, s, kc, :], arB[:, kc, 0:N],
                                                              start=(kc == 0), stop=(kc == NPAIR - 1)) for kc in range(NPAIR)]
            if m == 0:
                for a_, b_ in ((0, 12), (12, 16), (16, 18), (18, 20), (20, 21), (21, 22)):
                    kb.mm(dfns[a_:b_], reads=[wb_r[s]] + ar[a_:b_], writes=[br])
            else:
                kb.mm(dfns, reads=[wb_r[s]] + ar, writes=[br])
            post_consumer(cx, m, bank, br)
        post_finish(cx, l, V_GFPOST)

    def run_tile(cx):
        N = cx.N
        kb.dma(SP, x[:, :, 0:N], cx.x_in, writes=xr)
        kb.dma(SP, cosF[64:96, 0:N], cx.cos_in, writes=[cs_r])
        kb.dma(SP, sinF[64:96, 0:N], cx.sin_in, writes=[cs_r])
        for l in range(L):
            if not cx.prompt and l + 1 < L:
                emit_casts(l + 1)
            mixer(cx, l)
            ffn(cx, l)
        kb.dma(SP, cx.y_out, x[:, :, 0:N], reads=xr)

    cx = Ctx()
    cx.prompt = False
    cx.S, cx.T, cx.N, cx.j = NS, TS, NSX, 0
    cx.cst, cx.cst_r, cx.fst, cx.fst_r = cstS, cstS_r, fstS, fstS_r
    cx.x_in = xsT.rearrange("(c p) n -> p c n", p=128)
    cx.y_out = ysT.rearrange("(c p) n -> p c n", p=128)
    cx.cos_in, cx.sin_in = cosS[:, :], sinS[:, :]
    cx.ckv_out = lambda l: sckvT[l]
    cx.kpe_out = lambda l: skpeT[l]
    run_tile(cx)
    for j in range(NT):
        cx = Ctx()
        cx.prompt = True
        cx.S, cx.T, cx.N, cx.j = 1, 512, 512, j
        cx.cst, cx.cst_r, cx.fst, cx.fst_r = cstP, cstP_r, fstP, fstP_r
        cs = slice(j * 512, (j + 1) * 512)
        cx.x_in = xT.rearrange("(c p) n -> p c n", p=128)[:, :, cs]
        cx.y_out = yT.rearrange("(c p) n -> p c n", p=128)[:, :, cs]
        cx.cos_in, cx.sin_in = cosP[:, cs], sinP[:, cs]
        cx.ckv_out = lambda l, cs=cs: pckvT[l, :, cs]
        cx.kpe_out = lambda l, cs=cs: pkpeT[l, :, cs]
        run_tile(cx)
    allc = [r for rr in cstP_r for r in rr]
    kb.dma(SP, pconv.rearrange("p (l c s t) -> p l c s t", l=L, c=4, s=1), cstP[:, :, :, :, :], reads=allc)
    kb.dma(SP, pffn.rearrange("p (l c a s t) -> p l c a s t", l=L, c=NPAIR, a=2, s=1), fstP[:, :, :, :, :, :],
           reads=[r for rr in fstP_r for r in rr])
    kb.dma(SP, sconv.rearrange("p (l c s t) -> p l c s t", l=L, c=4, s=NS), cstS[:, :, :, :, :],
           reads=[r for rr in cstS_r for r in rr])
    kb.dma(SP, sffn.rearrange("p (l c a s t) -> p l c a s t", l=L, c=NPAIR, a=2, s=NS), fstS[:, :, :, :, :, :],
           reads=[r for rr in fstS_r for r in rr])
    kb.finish()

    with nc.Block() as block:
        @block.sync
        def _(e):
            for th in SP.prog:
                th(e)

        @block.tensor
        def _(e):
            for th in PE.prog:
                th(e)

        @block.scalar
        def _(e):
            for th in ACT.prog:
                th(e)

        @block.vector
        def _(e):
            for th in DVE.prog:
                th(e)

        @block.gpsimd
        def _(e):
            for th in POOL.prog:
                th(e)
    es.close()
    return nc


def _fm(v, nch):
    Lh = v.shape[0]
    return np.ascontiguousarray(v.reshape(Lh, nch, 128).transpose(2, 0, 1))


def _rope_tables(pos):
    inv = (1.0 / (np.float32(10000.0) ** (np.arange(0, 32, 2, dtype=np.float32) / np.float32(32)))).astype(np.float32)
    ang = pos.astype(np.float32)[:, None] * inv[None, :]
    c = np.cos(ang).astype(np.float32).T
    s = np.sin(ang).astype(np.float32).T
    return np.ascontiguousarray(np.concatenate([c, c], 0)), np.ascontiguousarray(np.concatenate([-s, s], 0))


def kernel(x_prompt, x_sample, cache_ckv, cache_kpe, state_conv, state_ffn,
           w_in, w_conv, g_qa, w_uq, g_kva, w_uk, w_uv, w_o, g_mix_pre, g_mix_post,
           w_up, w_ffn_conv, b_ffn_conv, w_down, g_ffn_pre, g_ffn_post):
    f = lambda a: np.asarray(a, dtype=np.float32)
    x_prompt, x_sample, cache_ckv, cache_kpe, state_conv, state_ffn = map(f, (x_prompt, x_sample, cache_ckv, cache_kpe, state_conv, state_ffn))
    w_in, w_conv, g_qa, w_uq, g_kva, w_uk, w_uv, w_o = map(f, (w_in, w_conv, g_qa, w_uq, g_kva, w_uk, w_uv, w_o))
    g_mix_pre, g_mix_post, w_up, w_ffn_conv, b_ffn_conv, w_down, g_ffn_pre, g_ffn_post = map(
        f, (g_mix_pre, g_mix_post, w_up, w_ffn_conv, b_ffn_conv, w_down, g_ffn_pre, g_ffn_post))
    BP, SEQ, _ = x_prompt.shape
    BS, TS, _ = x_sample.shape
    L = w_in.shape[0]
    PAST = cache_ckv.shape[2]
    assert BP == N_CORES and BS % N_CORES == 0
    NS = BS // N_CORES
    NSX = NS * TS

    xv, gb, gc, qa, kva, kpe = np.split(w_in, [512, 1024, 1536, 1792, 1920], axis=2)
    kpes = np.concatenate([kpe[:, :, 16:32], kpe[:, :, 0:16]], axis=2)
    pad = np.zeros((L, D, 64), np.float32)
    w_in_x = np.ascontiguousarray(np.concatenate([xv, gc, gb, qa, kva, kpe, kpes, pad], axis=2))
    wq = w_uq.reshape(L, 256, 8, 96)
    w_uq_x = np.ascontiguousarray(np.concatenate([wq, wq[..., 80:96], wq[..., 64:80]], axis=3).reshape(L, 256, 1024))
    w_uk2 = np.ascontiguousarray(w_uk.reshape(L, 128, 512))
    w_ukT = np.ascontiguousarray(w_uk.transpose(0, 3, 2, 1).reshape(L, 64, 1024))
    w_uv2 = np.ascontiguousarray(w_uv.reshape(L, 128, 512))
    upa = w_up[:, :, :DFF].reshape(L, D, NPAIR, 1, 128)
    upb = w_up[:, :, DFF:].reshape(L, D, NPAIR, 1, 128)
    w_up_x = np.ascontiguousarray(np.concatenate([upa, upb], axis=3).reshape(L, D, 2 * DFF))
    vecs = np.zeros((128, L, VL), np.float32)
    vecs[:, :, V_GMPRE:V_GMPRE + 8] = _fm(g_mix_pre, 8)
    vecs[:, :, V_GMPOST:V_GMPOST + 8] = _fm(g_mix_post, 8)
    vecs[:, :, V_GFPRE:V_GFPRE + 8] = _fm(g_ffn_pre, 8)
    vecs[:, :, V_GFPOST:V_GFPOST + 8] = _fm(g_ffn_post, 8)
    vecs[:, :, V_GQA:V_GQA + 2] = _fm(g_qa, 2)
    vecs[:, :, V_GKVA:V_GKVA + 1] = _fm(g_kva, 1)
    for k in range(3):
        vecs[:, :, V_WCONV + 4 * k:V_WCONV + 4 * k + 4] = _fm(w_conv[:, k, :], 4)
        vecs[:, :, V_WFFN + 44 * k:V_WFFN + 44 * k + 44] = _fm(w_ffn_conv[:, k, :], 44)
    vecs[:, :, V_BFFN:V_BFFN + 44] = _fm(b_ffn_conv, 44)
    vecs = np.ascontiguousarray(vecs.reshape(128, L * VL))
    cosP, sinP = _rope_tables(np.arange(SEQ))
    cS, sS = _rope_tables(PAST + np.arange(TS))
    cosS = np.ascontiguousarray(np.tile(cS, (1, NS)))
    sinS = np.ascontiguousarray(np.tile(sS, (1, NS)))
    ident = np.eye(128, dtype=np.float32)

    nc = build(L, SEQ, PAST, NS, TS)
    in_maps = []
    for c in range(N_CORES):
        sl = slice(c * NS, (c + 1) * NS)
        cc = cache_ckv[:, sl]
        stc = state_conv[:, sl]
        stf = state_ffn[:, sl]
        cst = stc.reshape(L, NS, 2, 4, 128).transpose(4, 0, 3, 1, 2)
        fst = stf.reshape(L, NS, 2, 2, NPAIR, 128).transpose(5, 0, 4, 3, 1, 2)
        in_maps.append({
            "xT": np.ascontiguousarray(x_prompt[c].T),
            "xsT": np.ascontiguousarray(x_sample[sl].reshape(NSX, D).T),
            "cT": np.ascontiguousarray(cc.transpose(0, 1, 3, 2)),
            "kT": np.ascontiguousarray(cache_kpe[:, sl].transpose(0, 1, 3, 2)),
            "cc": np.ascontiguousarray(cc),
            "cst": np.ascontiguousarray(cst).reshape(128, -1),
            "fst": np.ascontiguousarray(fst).reshape(128, -1),
            "w_in": w_in_x, "w_uq": w_uq_x, "w_uk": w_uk2, "w_ukT": w_ukT, "w_uv": w_uv2, "w_o": w_o,
            "w_up": w_up_x, "w_dn": w_down, "vecs": vecs,
            "cosP": cosP, "sinP": sinP, "cosS": cosS, "sinS": sinS, "ident": ident,
        })
    res = run_bass_kernel_spmd(nc, in_maps, core_ids=list(range(N_CORES)))
    R = res.results
    if DEBUG:
        kernel.dbg = [R[c]["dbg"] for c in range(N_CORES)]
    y_prompt = np.stack([R[c]["yT"].T for c in range(N_CORES)])
    y_sample = np.concatenate([R[c]["ysT"].T.reshape(NS, TS, D) for c in range(N_CORES)])
    p_ckv = np.stack([R[c]["pckvT"].transpose(0, 2, 1) for c in range(N_CORES)], axis=1)
    p_kpe = np.stack([R[c]["pkpeT"].transpose(0, 2, 1) for c in range(N_CORES)], axis=1)
    p_conv = np.stack([R[c]["pconv"].reshape(128, L, 4, 2).transpose(1, 3, 2, 0).reshape(L, 2, 512) for c in range(N_CORES)], axis=1)
    p_ffn = np.stack([R[c]["pffn"].reshape(128, L, NPAIR, 2, 2).transpose(1, 4, 3, 2, 0).reshape(L, 2, 2 * DFF)
                      for c in range(N_CORES)], axis=1)
    s_ckv = np.concatenate([R[c]["sckvT"].transpose(0, 2, 1).reshape(L, NS, TS, 128) for c in range(N_CORES)], axis=1)
    s_kpe = np.concatenate([R[c]["skpeT"].transpose(0, 2, 1).reshape(L, NS, TS, 32) for c in range(N_CORES)], axis=1)
    s_conv = np.concatenate([R[c]["sconv"].reshape(128, L, 4, NS, 2).transpose(1, 3, 4, 2, 0).reshape(L, NS, 2, 512)
                             for c in range(N_CORES)], axis=1)
    s_ffn = np.concatenate([R[c]["sffn"].reshape(128, L, NPAIR, 2, NS, 2).transpose(1, 4, 5, 3, 2, 0).reshape(L, NS, 2, 2 * DFF)
                            for c in range(N_CORES)], axis=1)
    outs = (y_prompt, y_sample, p_ckv, p_kpe, p_conv, p_ffn, s_ckv, s_kpe, s_conv, s_ffn)
    return tuple(np.ascontiguousarray(o, dtype=np.float32) for o in outs)
```

```python
import math
from contextlib import ExitStack
import numpy as np
import concourse.bass as bass
import concourse.mybir as mybir
from concourse.bass_utils import run_bass_kernel_spmd

F32 = mybir.dt.float32
BF16 = mybir.dt.bfloat16
AF = mybir.ActivationFunctionType
ALU = mybir.AluOpType

D = 1024
DC = 512
NH = 8
DFF = 2816
NPAIR = 22
EPS = 1e-6
SCALE = 1.0 / math.sqrt(96.0)
VL = 223
V_GMPRE, V_GMPOST, V_GFPRE, V_GFPOST, V_GQA, V_GKVA, V_WCONV, V_WFFN, V_BFFN = 0, 8, 16, 24, 32, 34, 35, 47, 179
SEM_LIMIT = 30000
NSLOT = 10
N_CORES = 8
DEBUG = False


class Res:
    __slots__ = ("lw", "rd")

    def __init__(self):
        self.lw = None
        self.rd = {}


class Q:
    def __init__(self, kb, kind):
        self.kb = kb
        self.kind = kind
        self.prog = []
        self.seen = {}
        self.own = set()
        self.sem = None
        self.val = 0
        self.slots = [[None, 0] for _ in range(NSLOT)]
        self.nd = 0

    def wait(self, ev):
        if ev is None:
            return
        s, v = ev
        if self.kind == "pe" and s in self.own:
            return
        if self.seen.get(s, 0) >= v:
            return
        self.seen[s] = v
        self.prog.append(lambda e, s=s, v=v: e.wait_ge(s, v))

    def bump(self, fn):
        if self.sem is None or self.val >= SEM_LIMIT:
            self.sem = self.kb.new_sem()
            self.own.add(self.sem)
            self.val = 0
        self.val += 1
        s = self.sem
        self.prog.append(lambda e, fn=fn, s=s: fn(e).then_inc(s, 1))
        return (s, self.val)


class KB:
    def __init__(self, nc, es):
        self.nc = nc
        self.es = es
        self.nsem = 0
        self.pe = Q(self, "pe")
        self.act = Q(self, "cmp")
        self.dve = Q(self, "cmp")
        self.pool = Q(self, "cmp")
        self.sp = Q(self, "dma")
        self.gq = self.pool

    def new_sem(self):
        self.nsem += 1
        return self.es.enter_context(self.nc.semaphore(f"s{self.nsem}"))

    def deps(self, q, reads, writes):
        for r in reads:
            q.wait(r.lw)
        for w in writes:
            q.wait(w.lw)
            for s, v in w.rd.items():
                q.wait((s, v))

    def done(self, ev, reads, writes):
        s, v = ev
        for r in reads:
            if r.rd.get(s, 0) < v:
                r.rd[s] = v
        for w in writes:
            w.lw = ev
            w.rd = {}

    def op(self, q, fn, reads=(), writes=()):
        self.deps(q, reads, writes)
        ev = q.bump(fn)
        self.done(ev, reads, writes)

    def mm(self, fns, reads=(), writes=()):
        q = self.pe
        self.deps(q, reads, writes)
        for f in fns[:-1]:
            q.prog.append(lambda e, f=f: f(e))
        ev = q.bump(fns[-1])
        self.done(ev, reads, writes)

    def dma(self, q, out, in_, reads=(), writes=()):
        self.deps(q, reads, writes)
        k = q.nd % NSLOT
        q.nd += 1
        slot = q.slots[k]
        if slot[0] is None or slot[1] >= SEM_LIMIT:
            slot[0] = self.new_sem()
            slot[1] = 0
        else:
            q.wait((slot[0], slot[1]))
        slot[1] += 16
        s = slot[0]
        q.prog.append(lambda e, out=out, in_=in_, s=s: e.dma_start(out=out, in_=in_).then_inc(s, 16))
        self.done((s, slot[1]), reads, writes)

    def finish(self):
        for q in (self.sp, self.pool):
            for s, v in q.slots:
                if s is not None:
                    q.wait((s, v))


def build(L, SEQ, PAST, NS, TS):
    NT = SEQ // 512
    NSX = NS * TS
    NPB = PAST // 128
    assert PAST % 1024 == 0 and SEQ % 512 == 0
    nc = bass.Bass("TRN2", target_bir_lowering=False)
    es = ExitStack()
    kb = KB(nc, es)
    PE, ACT, DVE, POOL, SP = kb.pe, kb.act, kb.dve, kb.pool, kb.sp

    def din(name, shape):
        return nc.dram_tensor(name, list(shape), F32, kind="ExternalInput").ap()

    def dout(name, shape):
        return nc.dram_tensor(name, list(shape), F32, kind="ExternalOutput").ap()

    def dscr(name, shape):
        return nc.dram_tensor(name, list(shape), BF16, kind="Internal").ap()

    def sb(name, shape, dt=F32):
        return es.enter_context(nc.sbuf_tensor("sb_" + name, list(shape), dt))

    xT = din("xT", [D, SEQ])
    xsT = din("xsT", [D, NSX])
    cT_in = din("cT", [L, NS, 128, PAST])
    kT_in = din("kT", [L, NS, 32, PAST])
    cc_in = din("cc", [L, NS, PAST, 128])
    cst_in = din("cst", [128, L * 4 * NS * 2])
    fst_in = din("fst", [128, L * NPAIR * 2 * NS * 2])
    w_in = din("w_in", [L, D, 2048])
    w_uq = din("w_uq", [L, 256, 1024])
    w_uk = din("w_uk", [L, 128, 512])
    w_ukT = din("w_ukT", [L, 64, 1024])
    w_uv = din("w_uv", [L, 128, 512])
    w_o = din("w_o", [L, D, D])
    w_up = din("w_up", [L, D, 2 * DFF])
    w_dn = din("w_dn", [L, DFF, D])
    vecs_in = din("vecs", [128, L * VL])
    cosP = din("cosP", [32, SEQ])
    sinP = din("sinP", [32, SEQ])
    cosS = din("cosS", [32, NSX])
    sinS = din("sinS", [32, NSX])
    ident_in = din("ident", [128, 128])

    yT = dout("yT", [D, SEQ])
    ysT = dout("ysT", [D, NSX])
    pckvT = dout("pckvT", [L, 128, SEQ])
    pkpeT = dout("pkpeT", [L, 32, SEQ])
    pconv = dout("pconv", [128, L * 4 * 2])
    pffn = dout("pffn", [128, L * NPAIR * 2 * 2])
    sckvT = dout("sckvT", [L, 128, NSX])
    skpeT = dout("skpeT", [L, 32, NSX])
    sconv = dout("sconv", [128, L * 4 * NS * 2])
    sffn = dout("sffn", [128, L * NPAIR * 2 * NS * 2])
    dbg = dout("dbg", [128, 8, NSX]) if DEBUG else None

    win_b = dscr("win_b", [L, D, 2048])
    wo_b = dscr("wo_b", [L, D, D])
    wup_b = dscr("wup_b", [L, D, 2 * DFF])
    wdn_b = dscr("wdn_b", [L, DFF, D])
    wuq_b = dscr("wuq_b", [L, 256, 1024])
    wuk_b = dscr("wuk_b", [L, 128, 512])
    wukT_b = dscr("wukT_b", [L, 64, 1024])
    wuv_b = dscr("wuv_b", [L, 128, 512])
    Ksc = dscr("Ksc", [L, 2, NT, 96, 4, 512])
    Vsc = dscr("Vsc", [L, 2, NT, 128, 4, 4, 128])
    win_r = [[Res() for _ in range(4)] for _ in range(L)]
    wo_r = [[Res() for _ in range(2)] for _ in range(L)]
    wup_r = [[Res() for _ in range(11)] for _ in range(L)]
    wdn_r = [[Res() for _ in range(8)] for _ in range(L)]
    wsm_r = [[Res() for _ in range(4)] for _ in range(L)]
    ksc_r = [[[Res() for _ in range(NT)] for _ in range(2)] for _ in range(L)]
    vsc_r = [[[Res() for _ in range(NT)] for _ in range(2)] for _ in range(L)]

    x = sb("x", [128, 8, 512]); xr = [Res() for _ in range(8)]
    hb = sb("hb", [128, 8, 512], BF16); hr = [Res() for _ in range(8)]
    mixb = sb("mixb", [128, 8, 512], BF16); mr = [Res() for _ in range(8)]
    arF = sb("arF", [128, 8, 512]); yr = [Res() for _ in range(8)]
    arB = sb("arB", [128, NPAIR, 512], BF16); ar = [Res() for _ in range(NPAIR)]
    sq = sb("sq", [128, 2, 512], BF16); sqr = [Res(), Res()]
    srt = sb("srt", [128, 512]); srt_r = Res()
    rstd = sb("rstd", [128, 512]); rstd_r = Res()
    ubc = sb("ubc", [128, 2, 516]); ubc_r = [Res(), Res()]
    qn = sb("qn", [128, 2, 512], BF16); qn_r = [Res(), Res()]
    ckvb = sb("ckvb", [128, 512], BF16); ckvb_r = Res()
    rt1 = sb("rt1", [128, 2, 512]); rt1_r = [Res(), Res()]
    rt2 = sb("rt2", [128, 2, 512]); rt2_r = [Res(), Res()]
    kpef = sb("kpef", [128, 512]); kpef_r = Res()
    Qb = sb("Qb", [128, 8, 512], BF16); q_r = [Res() for _ in range(8)]
    Kcur = sb("Kcur", [128, 8, 512], BF16); k_r = [Res() for _ in range(8)]
    Vcur = sb("Vcur", [128, 2, 4, 4, 128], BF16); v_r = [Res() for _ in range(4)]
    rec = sb("rec", [128, 2, 512]); rec_r = [Res(), Res()]
    wa = sb("wa", [128, 3, 8, 512], BF16); wa_r = [Res(), Res(), Res()]
    wb = sb("wb", [128, 3, NPAIR, 128], BF16); wb_r = [Res(), Res(), Res()]
    wuq_sb = sb("wuq_sb", [128, 2, 1024], BF16); wuq_r = Res()
    wuk_sb = sb("wuk_sb", [128, 512], BF16); wuk_r = Res()
    wukT_sb = sb("wukT_sb", [128, 1024], BF16); wukT_r = Res()
    wuv_sb = sb("wuv_sb", [128, 512], BF16); wuv_r = Res()
    ub = sb("ub", [128, 2, 2, 516]); ub_r = [Res(), Res()]
    facc = sb("facc", [128, 2, 2, 512]); facc_r = [[Res(), Res()], [Res(), Res()]]
    cstP = sb("cstP", [128, L, 4, 1, 2]); cstP_r = [[Res() for _ in range(4)] for _ in range(L)]
    fstP = sb("fstP", [128, L, NPAIR, 2, 1, 2]); fstP_r = [[Res() for _ in range(NPAIR)] for _ in range(L)]
    cstS = sb("cstS", [128, L, 4, NS, 2]); cstS_r = [[Res() for _ in range(4)] for _ in range(L)]
    fstS = sb("fstS", [128, L, NPAIR, 2, NS, 2]); fstS_r = [[Res() for _ in range(NPAIR)] for _ in range(L)]
    ones_b = sb("ones_b", [128, 128], BF16); ones_r = Res()
    ident = sb("ident", [128, 128]); ident_r = Res()
    vecs = sb("vecs", [128, L * VL]); vecs_r = Res()
    cosF = sb("cosF", [128, 512]); sinF = sb("sinF", [128, 512]); cs_r = Res()
    qlat = sb("qlat", [128, NS * 8 * TS], BF16); qlat_r = Res()
    qpe = sb("qpe", [128, NS * 8 * TS], BF16); qpe_r = Res()
    kpeb = sb("kpeb", [128, NSX], BF16); kpeb_r = Res()
    onb = sb("onb", [128, 2, 8 * TS], BF16); onb_r = [Res(), Res()]
    cntok = sb("cntok", [128, NS, 128], BF16); cntok_r = Res()
    recs = sb("recs", [128, 8 * TS]); recs_r = Res()
    pts = sb("pts", [128, 2, 8 * TS], BF16); pts_r = [Res(), Res()]

    banks = [es.enter_context(nc.psum_tensor(f"bank{i}", [128, 512], F32)) for i in range(8)]
    bank_r = [Res() for _ in range(8)]
    STATB = 7
    rot = {"bank": 0, "wa": 0, "wb": 0, "sq": 0}

    def next_bank():
        i = rot["bank"]
        rot["bank"] = (i + 1) % 7
        return banks[i], bank_r[i]

    def vcol(l, off):
        c = l * VL + off
        return vecs[:, c:c + 1]

    kb.op(POOL, lambda e: e.memset(ones_b[:, :], 1.0), writes=[ones_r])
    kb.op(POOL, lambda e: e.memset(Vcur[:, :, :, :, :].rearrange("p a b c d -> p (a b c) d")[:, :, 64:128], 1.0), writes=v_r)
    kb.op(POOL, lambda e: e.memset(cstP[:, :, :, :, :].rearrange("p l c s t -> p (l c s t)"), 0.0), writes=[r for rr in cstP_r for r in rr])
    kb.op(POOL, lambda e: e.memset(fstP[:, :, :, :, :, :].rearrange("p l c a s t -> p (l c a s t)"), 0.0), writes=[r for rr in fstP_r for r in rr])
    kb.dma(SP, vecs[:, :], vecs_in[:, :], writes=[vecs_r])
    kb.dma(SP, ident[:, :], ident_in[:, :], writes=[ident_r])
    kb.dma(SP, cstS[:, :, :, :, :], cst_in.rearrange("p (l c s t) -> p l c s t", l=L, c=4, s=NS),
           writes=[r for rr in cstS_r for r in rr])
    kb.dma(SP, fstS[:, :, :, :, :, :], fst_in.rearrange("p (l c a s t) -> p l c a s t", l=L, c=NPAIR, a=2, s=NS),
           writes=[r for rr in fstS_r for r in rr])
    GQ = kb.gq

    def emit_casts(l, part=2):
        if part in (0, 2):
            kb.dma(GQ, win_b[l], w_in[l], writes=win_r[l])
        if part == 0:
            return
        kb.dma(GQ, wuq_b[l], w_uq[l], writes=[wsm_r[l][0]])
        kb.dma(GQ, wuk_b[l], w_uk[l], writes=[wsm_r[l][1]])
        kb.dma(GQ, wukT_b[l], w_ukT[l], writes=[wsm_r[l][2]])
        kb.dma(GQ, wuv_b[l], w_uv[l], writes=[wsm_r[l][3]])
        kb.dma(GQ, wo_b[l], w_o[l], writes=wo_r[l])
        kb.dma(GQ, wup_b[l], w_up[l], writes=wup_r[l])
        kb.dma(GQ, wdn_b[l], w_dn[l], writes=wdn_r[l])

    emit_casts(0, 0)

    class Ctx:
        pass

    eps_t = sb("eps_t", [128, 1]); eps_r = Res()
    eps_ap = eps_t[:, 0:1]
    kb.op(POOL, lambda e: e.memset(eps_t[:, :], EPS), writes=[eps_r])

    def sq_next():
        i = rot["sq"]
        rot["sq"] = 1 - i
        return i

    def rms_pre(cx, l, goff):
        N = cx.N
        bank, br = banks[STATB], bank_r[STATB]
        for c in range(8):
            i = sq_next()
            if c % 4 in (0, 1):
                kb.op(ACT, lambda e, c=c, i=i: e.activation(out=sq[:, i, 0:N], in_=x[:, c, 0:N], func=AF.Square),
                      reads=[xr[c]], writes=[sqr[i]])
            else:
                kb.op(DVE if c % 4 == 2 else POOL, lambda e, c=c, i=i: e.tensor_tensor(out=sq[:, i, 0:N], in0=x[:, c, 0:N], in1=x[:, c, 0:N],
                                                                                      op=ALU.mult), reads=[xr[c]], writes=[sqr[i]])
            kb.mm([lambda e, c=c, i=i: e.matmul(bank[:, 0:N], ones_b[:, :], sq[:, i, 0:N], start=(c == 0), stop=(c == 7))],
                  reads=[sqr[i], ones_r], writes=[br])
        for c in (5, 6, 7):
            kb.op(ACT, lambda e, c=c: e.activation(out=arF[:, c, 0:N], in_=x[:, c, 0:N], func=AF.Copy, scale=vcol(l, goff + c)),
                  reads=[xr[c], vecs_r], writes=[yr[c]])
        kb.op(ACT, lambda e: e.activation(out=srt[:, 0:N], in_=bank[:, 0:N], func=AF.Ln, scale=1.0 / D, bias=eps_ap),
              reads=[br, eps_r], writes=[srt_r])
        kb.op(ACT, lambda e: e.activation(out=rstd[:, 0:N], in_=srt[:, 0:N], func=AF.Exp, scale=-0.5), reads=[srt_r], writes=[rstd_r])
        for c in range(8):
            if c < 5:
                kb.op(DVE, lambda e, c=c: e.scalar_tensor_tensor(out=hb[:, c, 0:N], in0=x[:, c, 0:N], scalar=vcol(l, goff + c),
                                                                 in1=rstd[:, 0:N], op0=ALU.mult, op1=ALU.mult),
                      reads=[xr[c], rstd_r, vecs_r], writes=[hr[c]])
            else:
                kb.op(POOL, lambda e, c=c: e.tensor_tensor(out=hb[:, c, 0:N], in0=arF[:, c, 0:N], in1=rstd[:, 0:N], op=ALU.mult),
                      reads=[yr[c], rstd_r], writes=[hr[c]])

    pend_stats = []

    def post_consumer(cx, m, bank, br):
        N = cx.N
        while pend_stats:
            pend_stats.pop(0)()
        i = sq_next()
        kb.op(ACT, lambda e: e.activation(out=arF[:, m, 0:N], in_=bank[:, 0:N], func=AF.Copy), reads=[br], writes=[yr[m]])
        kb.op(ACT, lambda e: e.activation(out=sq[:, i, 0:N], in_=bank[:, 0:N], func=AF.Square), reads=[br], writes=[sqr[i]])
        sbk, sbr = banks[STATB], bank_r[STATB]
        pend_stats.append(lambda: kb.mm([lambda e: e.matmul(sbk[:, 0:N], ones_b[:, :], sq[:, i, 0:N], start=(m == 0), stop=(m == 7))],
                                        reads=[sqr[i], ones_r], writes=[sbr]))

    def post_finish(cx, l, goff):
        N = cx.N
        sbk, sbr = banks[STATB], bank_r[STATB]
        while pend_stats:
            pend_stats.pop(0)()
        kb.op(ACT, lambda e: e.activation(out=srt[:, 0:N], in_=sbk[:, 0:N], func=AF.Ln, scale=1.0 / D, bias=eps_ap),
              reads=[sbr, eps_r], writes=[srt_r])
        kb.op(ACT, lambda e: e.activation(out=rstd[:, 0:N], in_=srt[:, 0:N], func=AF.Exp, scale=-0.5), reads=[srt_r], writes=[rstd_r])
        for c in range(8):
            q = DVE if c % 2 == 0 else POOL
            kb.op(q, lambda e, c=c: e.tensor_tensor(out=arF[:, c, 0:N], in0=arF[:, c, 0:N], in1=rstd[:, 0:N], op=ALU.mult),
                  reads=[yr[c], rstd_r], writes=[yr[c]])
            kb.op(DVE, lambda e, c=c: e.scalar_tensor_tensor(out=x[:, c, 0:N], in0=arF[:, c, 0:N], scalar=vcol(l, goff + c),
                                                           in1=x[:, c, 0:N], op0=ALU.mult, op1=ALU.add),
                  reads=[yr[c], xr[c], vecs_r], writes=[xr[c]])

    def wa_load(src_ap, src_res):
        s = rot["wa"]
        rot["wa"] = (s + 1) % 3
        kb.dma(SP, wa[:, s], src_ap, reads=[src_res], writes=[wa_r[s]])
        return s

    def mm_k8(bank, br, N, s, col0, M, rhs, rhs_res, split=False):
        fns = [lambda e, kc=kc: e.matmul(bank[0:M, 0:N], wa[:, s, kc, col0:col0 + M], rhs[:, kc, 0:N],
                                         start=(kc == 0), stop=(kc == 7)) for kc in range(8)]
        if split:
            for kc in range(8):
                kb.mm([fns[kc]], reads=[wa_r[s], rhs_res[kc]], writes=[br])
        else:
            kb.mm(fns, reads=[wa_r[s]] + list(rhs_res), writes=[br])

    def v3(ap2, cx, width):
        return ap2.rearrange("p (s t) -> p s t", s=cx.S)

    def mixer(cx, l):
        N, S, T = cx.N, cx.S, cx.T
        W2 = T + 2
        rms_pre(cx, l, V_GMPRE)
        if (not cx.prompt) and l == 0:
            emit_casts(0, 1)
        s = wa_load(win_b[l].rearrange("(kc p) n -> p kc n", p=128)[:, :, 0 * 512:1 * 512], win_r[l][0])
        for c in range(4):
            bank, br = next_bank()
            mm_k8(bank, br, N, s, c * 128, 128, hb, hr, split=(c == 0))
            kb.op(ACT, lambda e, c=c, bank=bank: e.activation(out=arF[:, c, 0:N], in_=bank[:, 0:N], func=AF.Copy),
                  reads=[br], writes=[yr[c]])
        s = wa_load(win_b[l].rearrange("(kc p) n -> p kc n", p=128)[:, :, 1 * 512:2 * 512], win_r[l][1])
        for c in range(4):
            bank, br = next_bank()
            mm_k8(bank, br, N, s, c * 128, 128, hb, hr)
            r = c % 2
            uv = v3(ubc[:, r, 0:S * W2], cx, W2)
            kb.op(DVE, lambda e, c=c, bank=bank, uv=uv: e.tensor_tensor(out=uv[:, :, 2:W2], in0=v3(bank[:, 0:N], cx, T),
                                                                        in1=v3(arF[:, c, 0:N], cx, T), op=ALU.mult),
                  reads=[br, yr[c]], writes=[ubc_r[r]])
            kb.op(POOL, lambda e, c=c, uv=uv: e.tensor_copy(out=uv[:, :, 0:2], in_=cx.cst[:, l, c, :, :]),
                  reads=[cx.cst_r[l][c]], writes=[ubc_r[r]])
            kb.op(POOL, lambda e, c=c, uv=uv: e.tensor_copy(out=cx.cst[:, l, c, :, :], in_=uv[:, :, T:W2]),
                  reads=[ubc_r[r]], writes=[cx.cst_r[l][c]])
            acc = v3(arF[:, c, 0:N], cx, T)
            kb.op(DVE, lambda e, c=c, uv=uv, acc=acc: e.tensor_scalar(out=acc, in0=uv[:, :, 0:T], scalar1=vcol(l, V_WCONV + c),
                                                                      scalar2=0.0, op0=ALU.mult, op1=ALU.add),
                  reads=[ubc_r[r], vecs_r], writes=[yr[c]])
            kb.op(DVE, lambda e, c=c, uv=uv, acc=acc: e.scalar_tensor_tensor(out=acc, in0=uv[:, :, 1:T + 1],
                                                                              scalar=vcol(l, V_WCONV + 4 + c), in1=acc,
                                                                              op0=ALU.mult, op1=ALU.add),
                  reads=[ubc_r[r], vecs_r, yr[c]], writes=[yr[c]])
            kb.op(DVE, lambda e, c=c, uv=uv, acc=acc: e.scalar_tensor_tensor(out=acc, in0=uv[:, :, 2:W2],
                                                                             scalar=vcol(l, V_WCONV + 8 + c), in1=acc,
                                                                             op0=ALU.mult, op1=ALU.add),
                  reads=[ubc_r[r], vecs_r, yr[c]], writes=[yr[c]])
        s = wa_load(win_b[l].rearrange("(kc p) n -> p kc n", p=128)[:, :, 2 * 512:3 * 512], win_r[l][2])
        for c in range(4):
            bank, br = next_bank()
            mm_k8(bank, br, N, s, c * 128, 128, hb, hr)
            kb.op(DVE, lambda e, c=c, bank=bank: e.tensor_tensor(out=mixb[:, c, 0:N], in0=bank[:, 0:N], in1=arF[:, c, 0:N],
                                                                 op=ALU.mult), reads=[br, yr[c]], writes=[mr[c]])
        s = wa_load(win_b[l].rearrange("(kc p) n -> p kc n", p=128)[:, :, 3 * 512:4 * 512], win_r[l][3])
        sbk, sbr = banks[STATB], bank_r[STATB]
        for c in range(2):
            bank, br = next_bank()
            mm_k8(bank, br, N, s, c * 128, 128, hb, hr)
            i = sq_next()
            kb.op(ACT, lambda e, c=c, bank=bank: e.activation(out=arF[:, 4 + c, 0:N], in_=bank[:, 0:N], func=AF.Copy),
                  reads=[br], writes=[yr[4 + c]])
            kb.op(ACT, lambda e, i=i, bank=bank: e.activation(out=sq[:, i, 0:N], in_=bank[:, 0:N], func=AF.Square),
                  reads=[br], writes=[sqr[i]])
            kb.mm([lambda e, c=c, i=i: e.matmul(sbk[:, 0:N], ones_b[:, :], sq[:, i, 0:N], start=(c == 0), stop=(c == 1))],
                  reads=[sqr[i], ones_r], writes=[sbr])
        kb.op(ACT, lambda e: e.activation(out=srt[:, 0:N], in_=sbk[:, 0:N], func=AF.Ln, scale=1.0 / 256, bias=eps_ap),
              reads=[sbr, eps_r], writes=[srt_r])
        kb.op(ACT, lambda e: e.activation(out=rstd[:, 0:N], in_=srt[:, 0:N], func=AF.Exp, scale=-0.5), reads=[srt_r], writes=[rstd_r])
        for c in range(2):
            kb.op(DVE, lambda e, c=c: e.scalar_tensor_tensor(out=qn[:, c, 0:N], in0=arF[:, 4 + c, 0:N], scalar=vcol(l, V_GQA + c),
                                                             in1=rstd[:, 0:N], op0=ALU.mult, op1=ALU.mult),
                  reads=[yr[4 + c], rstd_r, vecs_r], writes=[qn_r[c]])
        bank, br = next_bank()
        mm_k8(bank, br, N, s, 256, 128, hb, hr)
        i = sq_next()
        kb.op(ACT, lambda e, bank=bank: e.activation(out=arF[:, 6, 0:N], in_=bank[:, 0:N], func=AF.Copy), reads=[br], writes=[yr[6]])
        kb.op(ACT, lambda e, bank=bank, i=i: e.activation(out=sq[:, i, 0:N], in_=bank[:, 0:N], func=AF.Square),
              reads=[br], writes=[sqr[i]])
        kb.mm([lambda e, i=i: e.matmul(sbk[:, 0:N], ones_b[:, :], sq[:, i, 0:N], start=True, stop=True)],
              reads=[sqr[i], ones_r], writes=[sbr])
        kb.op(ACT, lambda e: e.activation(out=srt[:, 0:N], in_=sbk[:, 0:N], func=AF.Ln, scale=1.0 / 128, bias=eps_ap),
              reads=[sbr, eps_r], writes=[srt_r])
        kb.op(ACT, lambda e: e.activation(out=rstd[:, 0:N], in_=srt[:, 0:N], func=AF.Exp, scale=-0.5), reads=[srt_r], writes=[rstd_r])
        kb.op(DVE, lambda e: e.scalar_tensor_tensor(out=arF[:, 7, 0:N], in0=arF[:, 6, 0:N], scalar=vcol(l, V_GKVA),
                                                    in1=rstd[:, 0:N], op0=ALU.mult, op1=ALU.mult),
              reads=[yr[6], rstd_r, vecs_r], writes=[yr[7]])
        kb.op(ACT, lambda e: e.activation(out=ckvb[:, 0:N], in_=arF[:, 7, 0:N], func=AF.Copy), reads=[yr[7]], writes=[ckvb_r])
        kb.dma(SP, cx.ckv_out(l), arF[:, 7, 0:N], reads=[yr[7]])
        bA, bAr = next_bank()
        mm_k8(bA, bAr, N, s, 320, 96, hb, hr)
        bB, bBr = next_bank()
        mm_k8(bB, bBr, N, s, 352, 96, hb, hr)
        kb.op(DVE, lambda e: e.tensor_tensor(out=rt1[64:96, 0, 0:N], in0=bA[64:96, 0:N], in1=cosF[64:96, 0:N], op=ALU.mult),
              reads=[bAr, cs_r], writes=[rt1_r[0]])
        kb.op(DVE, lambda e: e.tensor_tensor(out=rt2[64:96, 0, 0:N], in0=bB[64:96, 0:N], in1=sinF[64:96, 0:N], op=ALU.mult),
              reads=[bBr, cs_r], writes=[rt2_r[0]])
        kb.op(POOL, lambda e: e.tensor_tensor(out=kpef[64:96, 0:N], in0=rt1[64:96, 0, 0:N], in1=rt2[64:96, 0, 0:N], op=ALU.add),
              reads=[rt1_r[0], rt2_r[0]], writes=[kpef_r])
        kb.dma(SP, cx.kpe_out(l), kpef[64:96, 0:N], reads=[kpef_r])
        if cx.prompt:
            for h in range(8):
                q = ACT
                if q is ACT:
                    kb.op(q, lambda e, h=h: e.activation(out=Kcur[64:96, h, 0:N], in_=kpef[64:96, 0:N], func=AF.Copy),
                          reads=[kpef_r], writes=[k_r[h]])
                else:
                    kb.op(q, lambda e, h=h: e.tensor_copy(out=Kcur[64:96, h, 0:N], in_=kpef[64:96, 0:N]),
                          reads=[kpef_r], writes=[k_r[h]])
        else:
            kb.op(POOL, lambda e: e.tensor_copy(out=kpeb[64:96, 0:N], in_=kpef[64:96, 0:N]), reads=[kpef_r], writes=[kpeb_r])
        kb.dma(SP, wuq_sb[:, :, :], wuq_b[l].rearrange("(kc p) n -> p kc n", p=128), reads=[wsm_r[l][0]], writes=[wuq_r])
        for h in range(8):
            b1, b1r = next_bank()
            kb.mm([lambda e, kc=kc, h=h, b1=b1: e.matmul(b1[0:96, 0:N], wuq_sb[:, kc, h * 128:h * 128 + 96], qn[:, kc, 0:N],
                                                          start=(kc == 0), stop=(kc == 1)) for kc in range(2)],
                  reads=[wuq_r, qn_r[0], qn_r[1]], writes=[b1r])
            b2, b2r = next_bank()
            kb.mm([lambda e, kc=kc, h=h, b2=b2: e.matmul(b2[0:96, 0:N], wuq_sb[:, kc, h * 128 + 32:h * 128 + 128], qn[:, kc, 0:N],
                                                          start=(kc == 0), stop=(kc == 1)) for kc in range(2)],
                  reads=[wuq_r, qn_r[0], qn_r[1]], writes=[b2r])
            r = h % 2
            kb.op(ACT, lambda e, h=h, b1=b1: e.activation(out=Qb[0:64, h, 0:N], in_=b1[0:64, 0:N], func=AF.Copy),
                  reads=[b1r], writes=[q_r[h]])
            kb.op(DVE, lambda e, b1=b1, r=r: e.tensor_tensor(out=rt1[64:96, r, 0:N], in0=b1[64:96, 0:N], in1=cosF[64:96, 0:N],
                                                             op=ALU.mult), reads=[b1r, cs_r], writes=[rt1_r[r]])
            kb.op(DVE, lambda e, b2=b2, r=r: e.tensor_tensor(out=rt2[64:96, r, 0:N], in0=b2[64:96, 0:N], in1=sinF[64:96, 0:N],
                                                             op=ALU.mult), reads=[b2r, cs_r], writes=[rt2_r[r]])
            if cx.prompt:
                kb.op(POOL, lambda e, h=h, r=r: e.tensor_tensor(out=Qb[64:96, h, 0:N], in0=rt1[64:96, r, 0:N],
                                                                in1=rt2[64:96, r, 0:N], op=ALU.add),
                      reads=[rt1_r[r], rt2_r[r]], writes=[q_r[h]])
            else:
                qv = qpe[64:96, :].rearrange("p (s h t) -> p s h t", s=NS, h=8)[:, :, h, :]
                kb.op(POOL, lambda e, qv=qv, r=r: e.tensor_tensor(out=qv, in0=v3(rt1[64:96, r, 0:N], cx, T),
                                                                  in1=v3(rt2[64:96, r, 0:N], cx, T), op=ALU.add),
                      reads=[rt1_r[r], rt2_r[r]], writes=[qpe_r])
        if cx.prompt:
            attn_prompt(cx, l)
        else:
            attn_sample(cx, l)
        if DEBUG and (not cx.prompt) and l == 0:
            kb.dma(GQ, dbg[:, :, :], mixb[:, :, 0:N], reads=mr)
        for g in range(2):
            s = wa_load(wo_b[l].rearrange("(kc p) n -> p kc n", p=128)[:, :, g * 512:(g + 1) * 512], wo_r[l][g])
            for mi in range(4):
                bank, br = next_bank()
                mm_k8(bank, br, N, s, mi * 128, 128, mixb, mr, split=(g == 0 and mi == 0))
                post_consumer(cx, g * 4 + mi, bank, br)
        post_finish(cx, l, V_GMPOST)

    def attn_prompt(cx, l):
        j = cx.j
        kb.dma(SP, wuk_sb[:, :], wuk_b[l], reads=[wsm_r[l][1]], writes=[wuk_r])
        kb.dma(SP, wuv_sb[:, :], wuv_b[l], reads=[wsm_r[l][3]], writes=[wuv_r])
        for h in range(8):
            bank, br = next_bank()
            kb.mm([lambda e, h=h, bank=bank: e.matmul(bank[0:64, 0:512], wuk_sb[:, h * 64:(h + 1) * 64], ckvb[:, 0:512],
                                                      start=True, stop=True)], reads=[wuk_r, ckvb_r], writes=[br])
            if h % 2 == 0:
                kb.op(ACT, lambda e, h=h, bank=bank: e.activation(out=Kcur[0:64, h, :], in_=bank[0:64, 0:512], func=AF.Copy),
                      reads=[br], writes=[k_r[h]])
            else:
                kb.op(DVE, lambda e, h=h, bank=bank: e.tensor_copy(out=Kcur[0:64, h, :], in_=bank[0:64, 0:512]),
                      reads=[br], writes=[k_r[h]])
        for kbi in range(4):
            bank, br = next_bank()
            kb.mm([lambda e, kbi=kbi, bank=bank: e.matmul(bank[:, 0:512], ckvb[:, kbi * 128:(kbi + 1) * 128], wuv_sb[:, :],
                                                          start=True, stop=True)], reads=[wuv_r, ckvb_r], writes=[br])
            src = bank[:, 0:512].rearrange("p (a h v) -> p a h v", a=2, h=4)
            kb.op(DVE, lambda e, kbi=kbi, src=src: e.tensor_copy(out=Vcur[:, 0, kbi, :, 0:64], in_=src[:, 0]),
                  reads=[br], writes=[v_r[kbi]])
            kb.op(ACT, lambda e, kbi=kbi, src=src: e.activation(out=Vcur[:, 1, kbi, :, 0:64], in_=src[:, 1], func=AF.Copy),
                  reads=[br], writes=[v_r[kbi]])
        if j < NT - 1:
            for hp in range(2):
                kb.dma(SP, Ksc[l, hp, j], Kcur[0:96, hp * 4:(hp + 1) * 4, :], reads=k_r[hp * 4:(hp + 1) * 4],
                       writes=[ksc_r[l][hp][j]])
                kb.dma(SP, Vsc[l, hp, j], Vcur[:, hp], reads=v_r, writes=[vsc_r[l][hp][j]])
        slot_i = [0]
        for hp in range(2):
            steps = []
            chunk_dma = []
            for jj in range(j):
                si = slot_i[0] % 2
                slot_i[0] += 1
                ks, vs = 2 * si, 2 * si + 1
                ksl = arB[0:96, 4 * ks:4 * ks + 4, :]
                vsl = arB[:, 4 * vs:4 * vs + 4, :].rearrange("p a (h v) -> p a h v", h=4)
                kres = ar[4 * ks:4 * ks + 4]
                vres = ar[4 * vs:4 * vs + 4]
                chunk_dma.append((ksl, vsl, kres, vres))
                for kbi in range(4):
                    for hh in range(4):
                        steps.append((arB[0:96, 4 * ks + hh, kbi * 128:(kbi + 1) * 128], kres,
                                      vsl[:, kbi, hh, :], vres, 0, False, hh))
            for kbi in range(4):
                for hh in range(4):
                    h = hp * 4 + hh
                    steps.append((Kcur[0:96, h, kbi * 128:(kbi + 1) * 128], [k_r[h]],
                                  Vcur[:, hp, kbi, hh, :], [v_r[kbi]], kbi * 128, True, hh))
            nsteps = len(steps)
            first = [True] * 4
            lastidx = [max(i for i in range(nsteps) if steps[i][6] == hh) for hh in range(4)]
            pend = []

            def do_pv(idx, pi, c0):
                kap, kres, vap, vres, _, _, hh = steps[idx]
                st = first[hh]
                first[hh] = False
                kb.mm([lambda e: e.matmul(banks[hh][:, c0:512], vap, arB[:, 16 + pi, c0:512], start=st, stop=(idx == lastidx[hh]))],
                      reads=[ar[16 + pi]] + list(vres), writes=[bank_r[hh]])

            def emit_dma(jj, hp=hp, chunk_dma=chunk_dma):
                ksl, vsl, kres, vres = chunk_dma[jj]
                kb.dma(SP, ksl, Ksc[l, hp, jj], reads=[ksc_r[l][hp][jj]], writes=kres)
                kb.dma(SP, vsl, Vsc[l, hp, jj], reads=[vsc_r[l][hp][jj]], writes=vres)

            for idx in range(nsteps):
                if idx == 0:
                    for jj in range(min(2, j)):
                        emit_dma(jj)
                elif idx % 16 == 3 and 2 <= idx // 16 + 1 < j:
                    emit_dma(idx // 16 + 1)
                kap, kres, vap, vres, c0, diag, hh = steps[idx]
                h = hp * 4 + hh
                sbi = 4 + idx % 3
                pi = idx % 4
                kb.mm([lambda e, kap=kap, h=h, sbi=sbi, c0=c0: e.matmul(banks[sbi][:, c0:512], kap, Qb[0:96, h, c0:512],
                                                                         start=True, stop=True)],
                      reads=list(kres) + [q_r[h]], writes=[bank_r[sbi]])
                kb.op(ACT, lambda e, sbi=sbi, pi=pi, c0=c0: e.activation(out=arB[:, 16 + pi, c0:512], in_=banks[sbi][:, c0:512],
                                                                       func=AF.Exp, scale=SCALE),
                      reads=[bank_r[sbi]], writes=[ar[16 + pi]])
                if diag:
                    kb.op(POOL, lambda e, pi=pi, c0=c0: e.memset(arB[64:128, 16 + pi, c0:c0 + 64], 0.0), writes=[ar[16 + pi]])
                pend.append((idx, pi, c0))
                if len(pend) > 2:
                    do_pv(*pend.pop(0))
            while pend:
                do_pv(*pend.pop(0))
            for hh in range(4):
                h = hp * 4 + hh
                r = hh % 2
                kb.op(ACT, lambda e, hh=hh, r=r: e.activation(out=rec[0:64, r, :], in_=banks[hh][64:128, 0:512], func=AF.Ln),
                      reads=[bank_r[hh]], writes=[rec_r[r]])
                kb.op(ACT, lambda e, r=r: e.activation(out=rec[0:64, r, :], in_=rec[0:64, r, :], func=AF.Exp, scale=-1.0),
                      reads=[rec_r[r]], writes=[rec_r[r]])
                p0 = (h % 2) * 64
                kb.op(DVE, lambda e, hh=hh, r=r, h=h, p0=p0: e.tensor_tensor(out=mixb[p0:p0 + 64, 4 + h // 2, :],
                                                                              in0=banks[hh][0:64, 0:512], in1=rec[0:64, r, :],
                                                                              op=ALU.mult),
                      reads=[bank_r[hh], rec_r[r]], writes=[mr[4 + h // 2]])

    def attn_sample(cx, l):
        N = cx.N
        kb.dma(SP, wukT_sb[0:64, :], wukT_b[l], reads=[wsm_r[l][2]], writes=[wukT_r])
        kb.dma(SP, wuv_sb[:, :], wuv_b[l], reads=[wsm_r[l][3]], writes=[wuv_r])
        for h in range(8):
            bank, br = next_bank()
            kb.mm([lambda e, h=h, bank=bank: e.matmul(bank[:, 0:N], wukT_sb[0:64, h * 128:(h + 1) * 128], Qb[0:64, h, 0:N],
                                                      start=True, stop=True)], reads=[wukT_r, q_r[h]], writes=[br])
            qv = qlat[:, :].rearrange("p (s h t) -> p s h t", s=NS, h=8)[:, :, h, :]
            kb.op(ACT if h % 2 == 0 else DVE,
                  (lambda e, qv=qv, bank=bank: e.activation(out=qv, in_=v3(bank[:, 0:N], cx, TS), func=AF.Copy)) if h % 2 == 0 else
                  (lambda e, qv=qv, bank=bank: e.tensor_copy(out=qv, in_=v3(bank[:, 0:N], cx, TS))),
                  reads=[br], writes=[qlat_r])
        bank, br = banks[3], bank_r[3]
        for s_ in range(NS):
            kb.mm([lambda e, s_=s_, bank=bank: e.transpose(bank[0:TS, s_ * 128:(s_ + 1) * 128], arF[:, 7, s_ * TS:(s_ + 1) * TS],
                                                            ident[:, :])], reads=[yr[7], ident_r], writes=[br])
        kb.op(DVE, lambda e, bank=bank: e.tensor_copy(out=cntok[0:TS, :, :], in_=bank[0:TS, 0:NS * 128].rearrange("p (s r) -> p s r", s=NS)),
              reads=[br], writes=[cntok_r])
        W = 8 * TS
        ybank, ybr = banks[0], bank_r[0]
        chunk_i = [0]
        for s_ in range(NS):
            obank, obr = banks[1 + s_ % 2], bank_r[1 + s_ % 2]
            nblk = NPB + 1
            blk = 0
            pend = []
            qlv = qlat[:, s_ * W:(s_ + 1) * W]
            qpv = qpe[64:96, s_ * W:(s_ + 1) * W]

            def do_pv(cap, cres, kk, pi, b, obank=obank, obr=obr, nblk=nblk):
                kb.mm([lambda e: e.matmul(obank[:, 0:W], cap, pts[0:kk, pi, :], start=(b == 0), stop=(b == nblk - 1)),
                       lambda e: e.matmul(obank[:, W:2 * W], ones_b[0:kk, :], pts[0:kk, pi, :], start=False, stop=(b == nblk - 1),
                                          skip_group_check=True)],
                      reads=[pts_r[pi], ones_r] + list(cres), writes=[obr])

            def do_sc(ctap, ktap, cres, kk, b, qlv=qlv, qpv=qpv):
                sbi = 4 + b % 3
                pi = b % 2
                kb.mm([lambda e: e.matmul(banks[sbi][0:kk, 0:W], ctap, qlv, start=True, stop=False),
                       lambda e: e.matmul(banks[sbi][0:kk, 0:W], ktap, qpv, start=False, stop=True)],
                      reads=list(cres) + [qlat_r, qpe_r], writes=[bank_r[sbi]])
                kb.op(ACT, lambda e: e.activation(out=pts[0:kk, pi, :], in_=banks[sbi][0:kk, 0:W], func=AF.Exp, scale=SCALE),
                      reads=[bank_r[sbi]], writes=[pts_r[pi]])
                return pi

            for ch in range(PAST // 1024):
                ci = chunk_i[0] % 2
                chunk_i[0] += 1
                sa, sb_ = 2 * ci, 2 * ci + 1
                resA = ar[4 * sa:4 * sa + 4]
                resB = ar[4 * sb_:4 * sb_ + 4]
                cTs = arB[:, 4 * sa:4 * sa + 2, :]
                ccs = arB[:, 4 * sa + 2:4 * sa + 4, :]
                kTs = arB[64:96, 4 * sb_:4 * sb_ + 2, :]
                kb.dma(GQ, cTs, cT_in[l, s_, :, ch * 1024:(ch + 1) * 1024].rearrange("p (a n) -> p a n", a=2), writes=resA)
                kb.dma(GQ, ccs.rearrange("p a (b r) -> p (a b) r", r=128),
                       cc_in[l, s_, ch * 1024:(ch + 1) * 1024, :].rearrange("(b p) r -> p b r", p=128), writes=resA)
                kb.dma(GQ, kTs, kT_in[l, s_, :, ch * 1024:(ch + 1) * 1024].rearrange("p (a n) -> p a n", a=2), writes=resB)
                for bi in range(8):
                    ctap = arB[:, 4 * sa + bi // 4, (bi % 4) * 128:(bi % 4) * 128 + 128]
                    ktap = arB[64:96, 4 * sb_ + bi // 4, (bi % 4) * 128:(bi % 4) * 128 + 128]
                    cap = arB[:, 4 * sa + 2 + bi // 4, (bi % 4) * 128:(bi % 4) * 128 + 128]
                    pi = do_sc(ctap, ktap, list(resA) + list(resB), 128, blk)
                    pend.append((cap, list(resA), 128, pi, blk))
                    blk += 1
                    if len(pend) > 1:
                        do_pv(*pend.pop(0))
            pi = do_sc(ckvb[:, s_ * TS:(s_ + 1) * TS], kpeb[64:96, s_ * TS:(s_ + 1) * TS], [ckvb_r, kpeb_r], TS, blk)
            pend.append((cntok[0:TS, s_, :], [cntok_r], TS, pi, blk))
            while pend:
                do_pv(*pend.pop(0))
            oi = s_ % 2
            kb.op(DVE, lambda e, obank=obank: e.reciprocal(out=recs[:, :], in_=obank[:, W:2 * W]), reads=[obr], writes=[recs_r])
            kb.op(DVE, lambda e, obank=obank, oi=oi: e.tensor_tensor(out=onb[:, oi, :], in0=obank[:, 0:W], in1=recs[:, :], op=ALU.mult),
                  reads=[obr, recs_r], writes=[onb_r[oi]])
            for h in range(8):
                p0 = (h % 2) * 64
                c0 = (h // 2) * N + s_ * TS
                kb.mm([lambda e, h=h, p0=p0, c0=c0, oi=oi: e.matmul(ybank[p0:p0 + 64, c0:c0 + TS], wuv_sb[:, h * 64:(h + 1) * 64],
                                                                    onb[:, oi, h * TS:(h + 1) * TS], start=True, stop=True)],
                      reads=[wuv_r, onb_r[oi]], writes=[ybr])
        kb.op(DVE, lambda e: e.tensor_copy(out=mixb[:, 4:8, 0:N], in_=ybank[:, 0:4 * N].rearrange("p (c n) -> p c n", c=4)),
              reads=[ybr], writes=mr[4:8])

    def ffn(cx, l):
        N, S, T = cx.N, cx.S, cx.T
        W2 = T + 2
        rms_pre(cx, l, V_GFPRE)
        ffn_tail = []
        for g in range(11):
            s = wa_load(wup_b[l].rearrange("(kc p) n -> p kc n", p=128)[:, :, g * 512:(g + 1) * 512], wup_r[l][g])
            for pi in range(2):
                pair = g * 2 + pi
                r = pair % 2
                bks = []
                for xx in range(2):
                    bank, br = next_bank()
                    mm_k8(bank, br, N, s, (pi * 2 + xx) * 128, 128, hb, hr, split=(pair == 0 and xx == 0))
                    bks.append((bank, br))
                ubv = ub[:, r, :, 0:S * W2].rearrange("p a (s t) -> p a s t", s=S)
                for xx in range(2):
                    bank, br = bks[xx]
                    kb.op(ACT, lambda e, xx=xx, bank=bank, ubv=ubv: e.activation(out=ubv[:, xx, :, 2:W2], in_=v3(bank[:, 0:N], cx, T),
                                                                                  func=AF.Copy), reads=[br], writes=[ub_r[r]])
                kb.op(POOL, lambda e, ubv=ubv, pair=pair: e.tensor_copy(out=ubv[:, :, :, 0:2], in_=cx.fst[:, l, pair, :, :, :]),
                      reads=[cx.fst_r[l][pair]], writes=[ub_r[r]])
                kb.op(POOL, lambda e, ubv=ubv, pair=pair: e.tensor_copy(out=cx.fst[:, l, pair, :, :, :], in_=ubv[:, :, :, T:W2]),
                      reads=[ub_r[r]], writes=[cx.fst_r[l][pair]])
                for xx in range(2):
                    q = DVE
                    chn = pair + xx * NPAIR
                    acc = v3(facc[:, r, xx, 0:N], cx, T)
                    fr = facc_r[r][xx]
                    kb.op(ACT, lambda e, xx=xx, acc=acc, ubv=ubv, chn=chn: e.activation(
                        out=acc, in_=ubv[:, xx, :, 0:T], func=AF.Identity, scale=vcol(l, V_WFFN + chn), bias=vcol(l, V_BFFN + chn)),
                        reads=[ub_r[r], vecs_r], writes=[fr])
                    kb.op(q, lambda e, xx=xx, acc=acc, ubv=ubv, chn=chn: e.scalar_tensor_tensor(
                        out=acc, in0=ubv[:, xx, :, 1:T + 1], scalar=vcol(l, V_WFFN + 44 + chn), in1=acc, op0=ALU.mult, op1=ALU.add),
                        reads=[ub_r[r], vecs_r, fr], writes=[fr])
                    kb.op(q, lambda e, xx=xx, acc=acc, ubv=ubv, chn=chn: e.scalar_tensor_tensor(
                        out=acc, in0=ubv[:, xx, :, 2:W2], scalar=vcol(l, V_WFFN + 88 + chn), in1=acc, op0=ALU.mult, op1=ALU.add),
                        reads=[ub_r[r], vecs_r, fr], writes=[fr])
                def tail(r=r, pair=pair):
                    kb.op(ACT, lambda e: e.activation(out=facc[:, r, 0, 0:N], in_=facc[:, r, 0, 0:N], func=AF.Silu),
                          reads=[facc_r[r][0]], writes=[facc_r[r][0]])
                    kb.op(POOL, lambda e: e.tensor_tensor(out=arB[:, pair, 0:N], in0=facc[:, r, 0, 0:N],
                                                          in1=facc[:, r, 1, 0:N], op=ALU.mult),
                          reads=[facc_r[r][0], facc_r[r][1]], writes=[ar[pair]])
                while ffn_tail:
                    ffn_tail.pop(0)()
                ffn_tail.append(tail)
        while ffn_tail:
            ffn_tail.pop(0)()
        for m in range(8):
            s = rot["wb"]
            rot["wb"] = (s + 1) % 3
            kb.dma(SP, wb[:, s], wdn_b[l].rearrange("(kc p) n -> p kc n", p=128)[:, :, m * 128:(m + 1) * 128], reads=[wdn_r[l][m]], writes=[wb_r[s]])
            bank, br = next_bank()
            dfns = [lambda e, kc=kc, s=s, bank=bank: e.matmul(bank[:, 0:N], wb[:, s, kc, :], arB[:, kc, 0:N],
                                                              start=(kc == 0), stop=(kc == NPAIR - 1)) for kc in range(NPAIR)]
            if m == 0:
                for a_, b_ in ((0, 12), (12, 16), (16, 18), (18, 20), (20, 21), (21, 22)):
                    kb.mm(dfns[a_:b_], reads=[wb_r[s]] + ar[a_:b_], writes=[br])
            else:
                kb.mm(dfns, reads=[wb_r[s]] + ar, writes=[br])
            post_consumer(cx, m, bank, br)
        post_finish(cx, l, V_GFPOST)

    def run_tile(cx):
        N = cx.N
        kb.dma(SP, x[:, :, 0:N], cx.x_in, writes=xr)
        kb.dma(SP, cosF[64:96, 0:N], cx.cos_in, writes=[cs_r])
        kb.dma(SP, sinF[64:96, 0:N], cx.sin_in, writes=[cs_r])
        for l in range(L):
            if not cx.prompt and l + 1 < L:
                emit_casts(l + 1)
            mixer(cx, l)
            ffn(cx, l)
        kb.dma(SP, cx.y_out, x[:, :, 0:N], reads=xr)

    cx = Ctx()
    cx.prompt = False
    cx.S, cx.T, cx.N, cx.j = NS, TS, NSX, 0
    cx.cst, cx.cst_r, cx.fst, cx.fst_r = cstS, cstS_r, fstS, fstS_r
    cx.x_in = xsT.rearrange("(c p) n -> p c n", p=128)
    cx.y_out = ysT.rearrange("(c p) n -> p c n", p=128)
    cx.cos_in, cx.sin_in = cosS[:, :], sinS[:, :]
    cx.ckv_out = lambda l: sckvT[l]
    cx.kpe_out = lambda l: skpeT[l]
    run_tile(cx)
    for j in range(NT):
        cx = Ctx()
        cx.prompt = True
        cx.S, cx.T, cx.N, cx.j = 1, 512, 512, j
        cx.cst, cx.cst_r, cx.fst, cx.fst_r = cstP, cstP_r, fstP, fstP_r
        cs = slice(j * 512, (j + 1) * 512)
        cx.x_in = xT.rearrange("(c p) n -> p c n", p=128)[:, :, cs]
        cx.y_out = yT.rearrange("(c p) n -> p c n", p=128)[:, :, cs]
        cx.cos_in, cx.sin_in = cosP[:, cs], sinP[:, cs]
        cx.ckv_out = lambda l, cs=cs: pckvT[l, :, cs]
        cx.kpe_out = lambda l, cs=cs: pkpeT[l, :, cs]
        run_tile(cx)
    allc = [r for rr in cstP_r for r in rr]
    kb.dma(SP, pconv.rearrange("p (l c s t) -> p l c s t", l=L, c=4, s=1), cstP[:, :, :, :, :], reads=allc)
    kb.dma(SP, pffn.rearrange("p (l c a s t) -> p l c a s t", l=L, c=NPAIR, a=2, s=1), fstP[:, :, :, :, :, :],
           reads=[r for rr in fstP_r for r in rr])
    kb.dma(SP, sconv.rearrange("p (l c s t) -> p l c s t", l=L, c=4, s=NS), cstS[:, :, :, :, :],
           reads=[r for rr in cstS_r for r in rr])
    kb.dma(SP, sffn.rearrange("p (l c a s t) -> p l c a s t", l=L, c=NPAIR, a=2, s=NS), fstS[:, :, :, :, :, :],
           reads=[r for rr in fstS_r for r in rr])
    kb.finish()

    with nc.Block() as block:
        @block.sync
        def _(e):
            for th in SP.prog:
                th(e)

        @block.tensor
        def _(e):
            for th in PE.prog:
                th(e)

        @block.scalar
        def _(e):
            for th in ACT.prog:
                th(e)

        @block.vector
        def _(e):
            for th in DVE.prog:
                th(e)

        @block.gpsimd
        def _(e):
            for th in POOL.prog:
                th(e)
    es.close()
    return nc


def _fm(v, nch):
    Lh = v.shape[0]
    return np.ascontiguousarray(v.reshape(Lh, nch, 128).transpose(2, 0, 1))


def _rope_tables(pos):
    inv = (1.0 / (np.float32(10000.0) ** (np.arange(0, 32, 2, dtype=np.float32) / np.float32(32)))).astype(np.float32)
    ang = pos.astype(np.float32)[:, None] * inv[None, :]
    c = np.cos(ang).astype(np.float32).T
    s = np.sin(ang).astype(np.float32).T
    return np.ascontiguousarray(np.concatenate([c, c], 0)), np.ascontiguousarray(np.concatenate([-s, s], 0))


def kernel(x_prompt, x_sample, cache_ckv, cache_kpe, state_conv, state_ffn,
           w_in, w_conv, g_qa, w_uq, g_kva, w_uk, w_uv, w_o, g_mix_pre, g_mix_post,
           w_up, w_ffn_conv, b_ffn_conv, w_down, g_ffn_pre, g_ffn_post):
    f = lambda a: np.asarray(a, dtype=np.float32)
    x_prompt, x_sample, cache_ckv, cache_kpe, state_conv, state_ffn = map(f, (x_prompt, x_sample, cache_ckv, cache_kpe, state_conv, state_ffn))
    w_in, w_conv, g_qa, w_uq, g_kva, w_uk, w_uv, w_o = map(f, (w_in, w_conv, g_qa, w_uq, g_kva, w_uk, w_uv, w_o))
    g_mix_pre, g_mix_post, w_up, w_ffn_conv, b_ffn_conv, w_down, g_ffn_pre, g_ffn_post = map(
        f, (g_mix_pre, g_mix_post, w_up, w_ffn_conv, b_ffn_conv, w_down, g_ffn_pre, g_ffn_post))
    BP, SEQ, _ = x_prompt.shape
    BS, TS, _ = x_sample.shape
    L = w_in.shape[0]
    PAST = cache_ckv.shape[2]
    assert BP == N_CORES and BS % N_CORES == 0
    NS = BS // N_CORES
    NSX = NS * TS

    xv, gb, gc, qa, kva, kpe = np.split(w_in, [512, 1024, 1536, 1792, 1920], axis=2)
    kpes = np.concatenate([kpe[:, :, 16:32], kpe[:, :, 0:16]], axis=2)
    pad = np.zeros((L, D, 64), np.float32)
    w_in_x = np.ascontiguousarray(np.concatenate([xv, gc, gb, qa, kva, kpe, kpes, pad], axis=2))
    wq = w_uq.reshape(L, 256, 8, 96)
    w_uq_x = np.ascontiguousarray(np.concatenate([wq, wq[..., 80:96], wq[..., 64:80]], axis=3).reshape(L, 256, 1024))
    w_uk2 = np.ascontiguousarray(w_uk.reshape(L, 128, 512))
    w_ukT = np.ascontiguousarray(w_uk.transpose(0, 3, 2, 1).reshape(L, 64, 1024))
    w_uv2 = np.ascontiguousarray(w_uv.reshape(L, 128, 512))
    upa = w_up[:, :, :DFF].reshape(L, D, NPAIR, 1, 128)
    upb = w_up[:, :, DFF:].reshape(L, D, NPAIR, 1, 128)
    w_up_x = np.ascontiguousarray(np.concatenate([upa, upb], axis=3).reshape(L, D, 2 * DFF))
    vecs = np.zeros((128, L, VL), np.float32)
    vecs[:, :, V_GMPRE:V_GMPRE + 8] = _fm(g_mix_pre, 8)
    vecs[:, :, V_GMPOST:V_GMPOST + 8] = _fm(g_mix_post, 8)
    vecs[:, :, V_GFPRE:V_GFPRE + 8] = _fm(g_ffn_pre, 8)
    vecs[:, :, V_GFPOST:V_GFPOST + 8] = _fm(g_ffn_post, 8)
    vecs[:, :, V_GQA:V_GQA + 2] = _fm(g_qa, 2)
    vecs[:, :, V_GKVA:V_GKVA + 1] = _fm(g_kva, 1)
    for k in range(3):
        vecs[:, :, V_WCONV + 4 * k:V_WCONV + 4 * k + 4] = _fm(w_conv[:, k, :], 4)
        vecs[:, :, V_WFFN + 44 * k:V_WFFN + 44 * k + 44] = _fm(w_ffn_conv[:, k, :], 44)
    vecs[:, :, V_BFFN:V_BFFN + 44] = _fm(b_ffn_conv, 44)
    vecs = np.ascontiguousarray(vecs.reshape(128, L * VL))
    cosP, sinP = _rope_tables(np.arange(SEQ))
    cS, sS = _rope_tables(PAST + np.arange(TS))
    cosS = np.ascontiguousarray(np.tile(cS, (1, NS)))
    sinS = np.ascontiguousarray(np.tile(sS, (1, NS)))
    ident = np.eye(128, dtype=np.float32)

    nc = build(L, SEQ, PAST, NS, TS)
    in_maps = []
    for c in range(N_CORES):
        sl = slice(c * NS, (c + 1) * NS)
        cc = cache_ckv[:, sl]
        stc = state_conv[:, sl]
        stf = state_ffn[:, sl]
        cst = stc.reshape(L, NS, 2, 4, 128).transpose(4, 0, 3, 1, 2)
        fst = stf.reshape(L, NS, 2, 2, NPAIR, 128).transpose(5, 0, 4, 3, 1, 2)
        in_maps.append({
            "xT": np.ascontiguousarray(x_prompt[c].T),
            "xsT": np.ascontiguousarray(x_sample[sl].reshape(NSX, D).T),
            "cT": np.ascontiguousarray(cc.transpose(0, 1, 3, 2)),
            "kT": np.ascontiguousarray(cache_kpe[:, sl].transpose(0, 1, 3, 2)),
            "cc": np.ascontiguousarray(cc),
            "cst": np.ascontiguousarray(cst).reshape(128, -1),
            "fst": np.ascontiguousarray(fst).reshape(128, -1),
            "w_in": w_in_x, "w_uq": w_uq_x, "w_uk": w_uk2, "w_ukT": w_ukT, "w_uv": w_uv2, "w_o": w_o,
            "w_up": w_up_x, "w_dn": w_down, "vecs": vecs,
            "cosP": cosP, "sinP": sinP, "cosS": cosS, "sinS": sinS, "ident": ident,
        })
    res = run_bass_kernel_spmd(nc, in_maps, core_ids=list(range(N_CORES)))
    R = res.results
    if DEBUG:
        kernel.dbg = [R[c]["dbg"] for c in range(N_CORES)]
    y_prompt = np.stack([R[c]["yT"].T for c in range(N_CORES)])
    y_sample = np.concatenate([R[c]["ysT"].T.reshape(NS, TS, D) for c in range(N_CORES)])
    p_ckv = np.stack([R[c]["pckvT"].transpose(0, 2, 1) for c in range(N_CORES)], axis=1)
    p_kpe = np.stack([R[c]["pkpeT"].transpose(0, 2, 1) for c in range(N_CORES)], axis=1)
    p_conv = np.stack([R[c]["pconv"].reshape(128, L, 4, 2).transpose(1, 3, 2, 0).reshape(L, 2, 512) for c in range(N_CORES)], axis=1)
    p_ffn = np.stack([R[c]["pffn"].reshape(128, L, NPAIR, 2, 2).transpose(1, 4, 3, 2, 0).reshape(L, 2, 2 * DFF)
                      for c in range(N_CORES)], axis=1)
    s_ckv = np.concatenate([R[c]["sckvT"].transpose(0, 2, 1).reshape(L, NS, TS, 128) for c in range(N_CORES)], axis=1)
    s_kpe = np.concatenate([R[c]["skpeT"].transpose(0, 2, 1).reshape(L, NS, TS, 32) for c in range(N_CORES)], axis=1)
    s_conv = np.concatenate([R[c]["sconv"].reshape(128, L, 4, NS, 2).transpose(1, 3, 4, 2, 0).reshape(L, NS, 2, 512)
                             for c in range(N_CORES)], axis=1)
    s_ffn = np.concatenate([R[c]["sffn"].reshape(128, L, NPAIR, 2, NS, 2).transpose(1, 4, 5, 3, 2, 0).reshape(L, NS, 2, 2 * DFF)
                            for c in range(N_CORES)], axis=1)
    outs = (y_prompt, y_sample, p_ckv, p_kpe, p_conv, p_ffn, s_ckv, s_kpe, s_conv, s_ffn)
    return tuple(np.ascontiguousarray(o, dtype=np.float32) for o in outs)
```
